# Optimizing a Trainium2 kernel written in Bass

```python
import jax, jax.numpy as jnp
from jax import lax
import numpy as np

D_MODEL = 2048
BATCH = 16
SEQ = 256
DEPTH = 4
DEC_BATCH = 8
DEC_SEQ = 4096
PAST_LEN = 256

GRID_W = 64
N_EVEN = (DEPTH + 1) // 2
N_ODD = DEPTH // 2
CONV_CH = D_MODEL // 2
CONV_K = 31
MLA_HEADS = 8
QK_NOPE = 128
QK_ROPE = 64
V_HEAD = 128
QK_HEAD = QK_NOPE + QK_ROPE
Q_LORA = 768
KV_LORA = 512
IN_COLS = 2 * CONV_CH + Q_LORA + KV_LORA + QK_ROPE
MIX_OUT = CONV_CH + MLA_HEADS * V_HEAD
ROPE_THETA = 10000.0
Q_BLOCK = 128
POOL_WINDOWS = (2, 4, 8, 16)
POOL_GROUPS = len(POOL_WINDOWS)
POOL_G = D_MODEL // POOL_GROUPS
D_FF = ((8 * D_MODEL + 3 * 256 - 1) // (3 * 256)) * 256
N_MOD = 6
EPS = 1e-6

kernel_name = 'hybrid_diffusion_prefix_trunk_step'


def rms_norm(x, w):
    xf = x.astype(jnp.float32)
    y = xf * lax.rsqrt(jnp.mean(xf * xf, axis=-1, keepdims=True) + EPS)
    return (y * w.astype(jnp.float32)).astype(x.dtype)


def layer_norm(x, w, b):
    xf = x.astype(jnp.float32)
    mu = jnp.mean(xf, axis=-1, keepdims=True)
    var = jnp.mean(jnp.square(xf - mu), axis=-1, keepdims=True)
    y = (xf - mu) * lax.rsqrt(var + EPS)
    return (y * w.astype(jnp.float32) + b.astype(jnp.float32)).astype(x.dtype)


def axial_rope_tables(rows):
    row = jnp.broadcast_to(jnp.arange(rows, dtype=jnp.float32)[:, None], (rows, GRID_W)).reshape(-1)
    col = jnp.broadcast_to(jnp.arange(GRID_W, dtype=jnp.float32)[None, :], (rows, GRID_W)).reshape(-1)
    n_freq = QK_ROPE // 4
    inv_freq = 1.0 / (ROPE_THETA ** (jnp.arange(n_freq, dtype=jnp.float32) / n_freq))
    ang = jnp.concatenate([row[:, None] * inv_freq, col[:, None] * inv_freq], axis=-1)
    return jnp.cos(ang), jnp.sin(ang)


def apply_rope(x, cos, sin):
    xf = x.astype(jnp.float32).reshape(x.shape[:-1] + (QK_ROPE // 2, 2))
    c, s = cos[:, None, :], sin[:, None, :]
    x0, x1 = xf[..., 0], xf[..., 1]
    y = jnp.stack([x0 * c - x1 * s, x0 * s + x1 * c], axis=-1)
    return y.reshape(x.shape).astype(x.dtype)


def rope_heads(x, cos, sin):
    return jnp.concatenate([x[..., :QK_NOPE], apply_rope(x[..., QK_NOPE:], cos, sin)], axis=-1)


def blocked_attention(q, k, v):
    b, sq, h, dk = q.shape
    nblk = sq // Q_BLOCK
    qb = q.reshape(b, nblk, Q_BLOCK, h, dk).transpose(1, 0, 2, 3, 4)
    scale = dk ** -0.5

    def one_block(q_blk):
        s = jnp.einsum('bqhd,bkhd->bhqk', q_blk, k, preferred_element_type=jnp.float32) * scale
        p = jax.nn.softmax(s, axis=-1).astype(v.dtype)
        return jnp.einsum('bhqk,bkhd->bqhd', p, v)

    out = lax.map(one_block, qb)
    return out.transpose(1, 0, 2, 3, 4).reshape(b, sq, h, v.shape[-1])


def mla_expand(ckv, kpe, w_kv_b, k_norm_w):
    b, s, _ = ckv.shape
    kv = jnp.einsum('bsr,rn->bsn', ckv, w_kv_b).reshape(b, s, MLA_HEADS, QK_NOPE + V_HEAD)
    k_nope, v = kv[..., :QK_NOPE], kv[..., QK_NOPE:]
    k_rope = jnp.broadcast_to(kpe[:, :, None, :], (b, s, MLA_HEADS, QK_ROPE))
    k = rms_norm(jnp.concatenate([k_nope, k_rope], axis=-1), k_norm_w)
    return k, v


def conformer_conv(u, conv_dw_w, conv_dw_b, conv_ln_w, conv_ln_b):
    a, g = u[..., :CONV_CH], u[..., CONV_CH:]
    h = a * jax.nn.sigmoid(g)
    h = lax.conv_general_dilated(h, conv_dw_w[:, None, :].astype(h.dtype), window_strides=(1,),
                                 padding=[(CONV_K // 2, CONV_K // 2)],
                                 dimension_numbers=('NWC', 'WIO', 'NWC'),
                                 feature_group_count=CONV_CH) + conv_dw_b
    return jax.nn.silu(layer_norm(h, conv_ln_w, conv_ln_b))


def even_mixer(h, w_in, conv_dw_w, conv_dw_b, conv_ln_w, conv_ln_b, q_a_norm_w, w_q_b,
               kv_a_norm_w, w_kv_b, q_norm_w, k_norm_w, w_out, ctx_ckv=None, ctx_kpe=None, rope=None):
    b, s, _ = h.shape
    proj = jnp.einsum('bsd,dn->bsn', h, w_in)
    o1 = 2 * CONV_CH
    o2 = o1 + Q_LORA
    o3 = o2 + KV_LORA
    u_conv = proj[..., :o1]
    q_a = proj[..., o1:o2]
    ckv = rms_norm(proj[..., o2:o3], kv_a_norm_w)
    kpe = proj[..., o3:]

    conv_out = conformer_conv(u_conv, conv_dw_w, conv_dw_b, conv_ln_w, conv_ln_b)

    q = jnp.einsum('bsr,rn->bsn', rms_norm(q_a, q_a_norm_w), w_q_b).reshape(b, s, MLA_HEADS, QK_HEAD)
    q = rms_norm(q, q_norm_w)
    k, v = mla_expand(ckv, kpe, w_kv_b, k_norm_w)
    if rope is not None:
        cos, sin = rope
        q = rope_heads(q, cos, sin)
        k = rope_heads(k, cos, sin)
        k_c, v_c = mla_expand(ctx_ckv, ctx_kpe, w_kv_b, k_norm_w)
        k = jnp.concatenate([k_c, k], axis=1)
        v = jnp.concatenate([v_c, v], axis=1)
    attn = blocked_attention(q, k, v).reshape(b, s, MLA_HEADS * V_HEAD)
    out = jnp.einsum('bsn,nd->bsd', jnp.concatenate([conv_out, attn], axis=-1), w_out)
    return out, ckv, kpe


def pool_mixer(h, pool_w, pool_scale):
    b, s, d = h.shape
    hf = h.astype(jnp.float32)
    cs = jnp.concatenate([jnp.zeros((b, 1, d), jnp.float32), jnp.cumsum(hf, axis=1)], axis=1)
    t = jnp.arange(s)
    groups = []
    for gi, w in enumerate(POOL_WINDOWS):
        lo_c, hi_c = gi * POOL_G, (gi + 1) * POOL_G
        lo = jnp.clip(t - w // 2, 0, s)
        hi = jnp.clip(t + w // 2, 0, s)
        csg = cs[..., lo_c:hi_c]
        mean = (jnp.take(csg, hi, axis=1) - jnp.take(csg, lo, axis=1)) / (hi - lo).astype(jnp.float32)[:, None]
        groups.append(mean - hf[..., lo_c:hi_c])
    mixed = jnp.stack(groups, axis=2).astype(h.dtype)
    y = jnp.einsum('bsgc,gcn->bsgn', mixed, pool_w).reshape(b, s, d)
    return y * pool_scale


def swiglu(h, w_gate, w_up, w_down):
    return jnp.einsum('bsf,fd->bsd', jax.nn.silu(jnp.einsum('bsd,df->bsf', h, w_gate)) * jnp.einsum('bsd,df->bsf', h, w_up), w_down)


def modulation(cond, w_mod, b_mod):
    m = jnp.einsum('...d,dn->...n', jax.nn.silu(cond), w_mod) + b_mod
    return jnp.split(m[..., None, :], N_MOD, axis=-1)


def setup_inputs(seed: int = 0) -> dict:
    key = jax.random.key(seed)
    ks = jax.random.split(key, 32)
    f32 = jnp.float32
    nrm = lambda k, shape, scale: jax.random.normal(k, shape, f32) * scale
    gain = lambda k, shape: 1.0 + 0.1 * jax.random.normal(k, shape, f32)
    return {
        'x_prompt': nrm(ks[0], (BATCH, SEQ, D_MODEL), 1.0),
        'x_sample': nrm(ks[1], (DEC_BATCH, DEC_SEQ, D_MODEL), 1.0),
        'cache_ckv': nrm(ks[2], (DEC_BATCH, N_EVEN, PAST_LEN, KV_LORA), 1.0),
        'cache_kpe': nrm(ks[3], (DEC_BATCH, N_EVEN, PAST_LEN, QK_ROPE), 1.0),
        'c': nrm(ks[4], (DEC_BATCH, D_MODEL), 1.0),
        'c_ctx': nrm(ks[5], (D_MODEL,), 1.0),
        'norm1_w': gain(ks[6], (DEPTH, D_MODEL)),
        'norm2_w': gain(ks[7], (DEPTH, D_MODEL)),
        'w_mod': nrm(ks[8], (DEPTH, D_MODEL, N_MOD * D_MODEL), D_MODEL ** -0.5),
        'b_mod': nrm(ks[9], (DEPTH, N_MOD * D_MODEL), 0.01),
        'w_in': nrm(ks[10], (N_EVEN, D_MODEL, IN_COLS), D_MODEL ** -0.5),
        'conv_dw_w': nrm(ks[11], (N_EVEN, CONV_K, CONV_CH), CONV_K ** -0.5),
        'conv_dw_b': nrm(ks[12], (N_EVEN, CONV_CH), 0.01),
        'conv_ln_w': gain(ks[13], (N_EVEN, CONV_CH)),
        'conv_ln_b': nrm(ks[14], (N_EVEN, CONV_CH), 0.01),
        'q_a_norm_w': gain(ks[15], (N_EVEN, Q_LORA)),
        'w_q_b': nrm(ks[16], (N_EVEN, Q_LORA, MLA_HEADS * QK_HEAD), Q_LORA ** -0.5),
        'kv_a_norm_w': gain(ks[17], (N_EVEN, KV_LORA)),
        'w_kv_b': nrm(ks[18], (N_EVEN, KV_LORA, MLA_HEADS * (QK_NOPE + V_HEAD)), KV_LORA ** -0.5),
        'q_norm_w': gain(ks[19], (N_EVEN, QK_HEAD)),
        'k_norm_w': gain(ks[20], (N_EVEN, QK_HEAD)),
        'w_out': nrm(ks[21], (N_EVEN, MIX_OUT, D_MODEL), MIX_OUT ** -0.5),
        'pool_w': nrm(ks[22], (N_ODD, POOL_GROUPS, POOL_G, POOL_G), POOL_G ** -0.5),
        'pool_scale': gain(ks[23], (N_ODD, D_MODEL)),
        'ffn_w_gate': nrm(ks[24], (DEPTH, D_MODEL, D_FF), D_MODEL ** -0.5),
        'ffn_w_up': nrm(ks[25], (DEPTH, D_MODEL, D_FF), D_MODEL ** -0.5),
        'ffn_w_down': nrm(ks[26], (DEPTH, D_FF, D_MODEL), D_FF ** -0.5),
    }


def reference(x_prompt, x_sample, cache_ckv, cache_kpe, c, c_ctx, norm1_w, norm2_w, w_mod, b_mod,
              w_in, conv_dw_w, conv_dw_b, conv_ln_w, conv_ln_b, q_a_norm_w, w_q_b, kv_a_norm_w, w_kv_b,
              q_norm_w, k_norm_w, w_out, pool_w, pool_scale, ffn_w_gate, ffn_w_up, ffn_w_down):
    rows = x_sample.shape[1] // GRID_W
    rope = axial_rope_tables(rows)
    xp, xs = x_prompt, x_sample
    new_ckv, new_kpe = [], []
    for layer in range(DEPTH):
        sh1_p, sc1_p, g1_p, sh2_p, sc2_p, g2_p = modulation(c_ctx, w_mod[layer], b_mod[layer])
        sh1_s, sc1_s, g1_s, sh2_s, sc2_s, g2_s = modulation(c, w_mod[layer], b_mod[layer])
        hp = rms_norm(xp, norm1_w[layer]) * (1.0 + sc1_p) + sh1_p
        hs = rms_norm(xs, norm1_w[layer]) * (1.0 + sc1_s) + sh1_s
        if layer % 2 == 0:
            e = layer // 2
            params = (w_in[e], conv_dw_w[e], conv_dw_b[e], conv_ln_w[e], conv_ln_b[e], q_a_norm_w[e], w_q_b[e],
                      kv_a_norm_w[e], w_kv_b[e], q_norm_w[e], k_norm_w[e], w_out[e])
            mp, ckv_p, kpe_p = even_mixer(hp, *params)
            new_ckv.append(ckv_p)
            new_kpe.append(kpe_p)
            ms, _, _ = even_mixer(hs, *params, ctx_ckv=cache_ckv[:, e], ctx_kpe=cache_kpe[:, e], rope=rope)
        else:
            o = layer // 2
            mp = pool_mixer(hp, pool_w[o], pool_scale[o])
            ms = pool_mixer(hs, pool_w[o], pool_scale[o])
        xp = xp + g1_p * mp
        xs = xs + g1_s * ms
        hp = rms_norm(xp, norm2_w[layer]) * (1.0 + sc2_p) + sh2_p
        hs = rms_norm(xs, norm2_w[layer]) * (1.0 + sc2_s) + sh2_s
        xp = xp + g2_p * swiglu(hp, ffn_w_gate[layer], ffn_w_up[layer], ffn_w_down[layer])
        xs = xs + g2_s * swiglu(hs, ffn_w_gate[layer], ffn_w_up[layer], ffn_w_down[layer])
    new_ckv_arr = jnp.stack(new_ckv, axis=1)
    new_kpe_arr = jnp.stack(new_kpe, axis=1)
    return (xp, xs, new_ckv_arr, new_kpe_arr)
```

```python
import numpy as np
from contextlib import ExitStack
import concourse.bass as bass
import concourse.mybir as mybir
from concourse.bass_utils import run_bass_kernel_spmd

F32 = mybir.dt.float32
BF16 = mybir.dt.bfloat16
AF = mybir.ActivationFunctionType
ALU = mybir.AluOpType

D = 2048
KC = 16
DFF = 5632
FC = 44
CONV_K = 31
NH = 8
EPS = 1e-6
POOL_W = (2, 4, 8, 16)
GRID_W = 64
PAST = 256
PAD = 16


class Tok:
    __slots__ = ("sem", "val", "eng")

    def __init__(self, sem, val, eng):
        self.sem, self.val, self.eng = sem, val, eng


class Slot:
    __slots__ = ("w", "r")

    def __init__(self):
        self.w = None
        self.r = {}


class Sched:
    COMPUTE = ("pe", "act", "dve", "pool")
    NDMA = 12

    def __init__(self, nc, stack):
        self.nc = nc
        self.streams = {e: [] for e in ("pe", "act", "dve", "pool", "sp")}
        self.sem = {}
        self.cnt = {}
        for e in self.COMPUTE:
            self.sem[e] = stack.enter_context(nc.semaphore("s_" + e))
            self.cnt[e] = 0
        self.dsem = {}
        self.dcnt = {}
        self.dptr = {}
        for q in ("sp", "pool"):
            self.dsem[q] = [stack.enter_context(nc.semaphore("d_%s%d" % (q, i))) for i in range(self.NDMA)]
            self.dcnt[q] = [0] * self.NDMA
            self.dptr[q] = 0
        self.known = {e: {} for e in self.streams}
        self.nops = 0

    def _wait(self, eng, tok):
        k = self.known[eng]
        key = id(tok.sem)
        if k.get(key, 0) >= tok.val:
            return
        k[key] = tok.val
        sem, val = tok.sem, tok.val
        self.streams[eng].append(lambda e: e.wait_ge(sem, val))

    def op(self, eng, fn, reads=(), writes=(), dma=False):
        deps = []
        for s in reads:
            if s.w is not None:
                deps.append((s.w, True))
        for s in writes:
            if s.w is not None:
                deps.append((s.w, False))
            for t in s.r.values():
                deps.append((t, False))
        for t, raw in deps:
            if (not dma) and t.eng == eng:
                if eng == "pe" or not raw:
                    continue
            self._wait(eng, t)
        if dma:
            q = eng
            i = self.dptr[q]
            self.dptr[q] = (i + 1) % self.NDMA
            sem = self.dsem[q][i]
            if self.dcnt[q][i] > 0:
                self._wait(eng, Tok(sem, self.dcnt[q][i], "dma_" + q))
            self.dcnt[q][i] += 16
            tok = Tok(sem, self.dcnt[q][i], "dma_" + q)
            self.streams[eng].append(lambda e: fn(e).then_inc(sem, 16))
        else:
            self.cnt[eng] += 1
            sem = self.sem[eng]
            tok = Tok(sem, self.cnt[eng], eng)
            self.streams[eng].append(lambda e: fn(e).then_inc(sem, 1))
        for s in writes:
            s.w = tok
            s.r = {}
        for s in reads:
            s.r[id(tok.sem)] = tok
        self.nops += 1
        return tok

    def barrier(self):
        toks = []
        for e in self.COMPUTE:
            if self.cnt[e] > 0:
                toks.append(Tok(self.sem[e], self.cnt[e], e))
        for q in ("sp", "pool"):
            for i in range(self.NDMA):
                if self.dcnt[q][i] > 0:
                    toks.append(Tok(self.dsem[q][i], self.dcnt[q][i], "dma_" + q))
        for e in self.streams:
            for t in toks:
                self._wait(e, t)

    def emit(self):
        nc = self.nc
        self.barrier()
        with nc.Block() as block:
            @block.sync
            def _(eng):
                for f in self.streams["sp"]:
                    f(eng)

            @block.tensor
            def _(eng):
                for f in self.streams["pe"]:
                    f(eng)

            @block.scalar
            def _(eng):
                for f in self.streams["act"]:
                    f(eng)

            @block.vector
            def _(eng):
                for f in self.streams["dve"]:
                    f(eng)

            @block.gpsimd
            def _(eng):
                for f in self.streams["pool"]:
                    f(eng)


_UID = [0]


def SBT(nc, name, shape, dtype):
    _UID[0] += 1
    return nc.sbuf_tensor("%s_u%d" % (name, _UID[0]), shape, dtype)


class Ring:
    def __init__(self, nc, st, name, shape, dtype, n):
        self.t = st.enter_context(SBT(nc, name, [128, n] + list(shape), dtype))
        self.slots = [Slot() for _ in range(n)]
        self.i = 0
        self.n = n

    def get(self):
        i = self.i
        self.i = (i + 1) % self.n
        return self.t[:, i], self.slots[i]


def build(NSB=8, DEPTH=4, dbg=False, stop=99):
    SS = NSB * 512
    T = SS + 512
    TK = PAST + T
    TP = T + 6 * PAD
    NE = (DEPTH + 1) // 2
    NO = DEPTH // 2
    nc = bass.Bass("TRN2", target_bir_lowering=False)

    def din(name, shape, dt=F32):
        return nc.dram_tensor(name, list(shape), dt, kind="ExternalInput").ap()

    def dout(name, shape, dt=F32):
        return nc.dram_tensor(name, list(shape), dt, kind="ExternalOutput").ap()

    def dscr(name, shape, dt):
        return nc.dram_tensor(name, list(shape), dt, kind="ExternalOutput" if dbg else "Internal").ap()

    xT = din("xT", [D, T])
    condT = din("condT", [128, KC, 2])
    cckvT = din("cckvT", [NE, 512, PAST])
    ckpeT = din("ckpeT", [NE, 64, PAST])
    n1w = din("n1w", [DEPTH, 128, KC])
    n2w = din("n2w", [DEPTH, 128, KC])
    bmodX = din("bmodX", [DEPTH, 128, 96, 2])
    w_mod = din("w_mod", [DEPTH, D, 6 * D])
    w_in = din("w_in", [NE, D, 3392])
    cdw = din("cdw", [NE, 128, 8, CONV_K])
    cdb = din("cdb", [NE, 128, 8])
    clw = din("clw", [NE, 128, 8])
    clb = din("clb", [NE, 128, 8])
    qanw = din("qanw", [NE, 128, 6])
    w_qb = din("w_qb", [NE, 768, 1536])
    kvanw = din("kvanw", [NE, 128, 4])
    w_kvb = din("w_kvb", [NE, 512, 2048])
    qnw = din("qnw", [NE, 128, 4])
    knw = din("knw", [NE, 128, 4])
    w_out = din("w_out", [NE, D, D])
    pool_w = din("pool_w", [max(NO, 1), 4 * 512, 512])
    pscale = din("pscale", [max(NO, 1), 128, KC])
    w_gate = din("w_gate", [DEPTH, D, DFF])
    w_up = din("w_up", [DEPTH, D, DFF])
    w_down = din("w_down", [DEPTH, DFF, D])
    ropeC = din("ropeC", [64, SS])
    ropeS = din("ropeS", [64, SS])
    invcntB = din("invcntB", [128, 4, T])
    ident_in = din("ident_in", [128, 128])
    yT = dout("yT", [D, T])
    ckv_out = dout("ckv_out", [NE, 512, 512])
    kpe_out = dout("kpe_out", [NE, 64, 512])
    xA = dscr("xA", [D, T], F32)
    xB = dscr("xB", [D, T], F32)
    hpad = dscr("hpad", [D, TP], F32)
    glu = dscr("glu", [1024, TP], BF16)
    catT = dscr("catT", [D, T], BF16)
    QnT = dscr("QnT", [NH, 128, T], BF16)
    QrT = dscr("QrT", [NH, 64, T], BF16)
    KnT = dscr("KnT", [NH, 128, TK], BF16)
    KrT = dscr("KrT", [NH, 64, TK], BF16)
    Vs = dscr("Vs", [TK, 1024], BF16)
    Wgu = dscr("Wgu", [DEPTH, 22, 128, 2, KC, 256], BF16)
    Wdn = dscr("Wdn", [DEPTH, 16, 128, FC, 128], BF16)
    Win = dscr("Win", [NE, 14, 128, KC, 256], BF16)
    Wqb = dscr("Wqb", [NE, NH, 128, 6, 384], BF16)
    Wkv = dscr("Wkv", [NE, 512, 2048], BF16)
    Wo = dscr("Wo", [NE, 8, 128, KC, 256], BF16)
    Wpl = dscr("Wpl", [max(NO, 1), 2048, 512], BF16)

    seqs = [(0, SS, PAD), (SS, 256, 3 * PAD + SS), (SS + 256, 256, 5 * PAD + SS + 256)]
    blocks = []
    for b in range(NSB):
        blocks.append((b * 512, 512, 0))
    blocks.append((SS, 512, 1))
    NB = len(blocks)

    def pcol(c):
        if c < SS:
            return c + PAD
        if c < SS + 256:
            return c + 3 * PAD
        return c + 5 * PAD

    dsl = {}

    def DS(*key):
        s = dsl.get(key)
        if s is None:
            s = Slot()
            dsl[key] = s
        return s

    with ExitStack() as st:
        S = Sched(nc, st)

        def dma(q, out, in_, reads=(), writes=()):
            return S.op(q, lambda e: e.dma_start(out=out, in_=in_), reads, writes, dma=True)

        ident = st.enter_context(SBT(nc, "ident", [128, 128], F32))
        identb = st.enter_context(SBT(nc, "identb", [128, 128], BF16))
        ones_f = st.enter_context(SBT(nc, "ones_f", [128, 128], F32))
        ones_b = st.enter_context(SBT(nc, "ones_b", [128, 128], BF16))
        zeros_b = st.enter_context(SBT(nc, "zeros_b", [128, 2 * PAD], BF16))
        zeros_f = st.enter_context(SBT(nc, "zeros_f", [128, 2 * PAD], F32))
        MODV = st.enter_context(SBT(nc, "MODV", [128, DEPTH, 6, KC, 2], F32))
        s_const = Slot()
        s_modv_l = [Slot() for _ in range(DEPTH)]
        NPADM = 64
        scnb = st.enter_context(SBT(nc, "scnb", [128, KC, NPADM], BF16))
        bm = st.enter_context(SBT(nc, "bm", [128, 96, 2], F32))
        nw = st.enter_context(SBT(nc, "nw", [128, 2, KC], F32))
        psc = st.enter_context(SBT(nc, "psc", [128, KC], F32))
        s_scn, s_bm, s_nw = Slot(), Slot(), Slot()
        PSB = [st.enter_context(nc.psum_tensor("psb%d" % i, [128, 512], F32)) for i in range(8)]
        ps_slots = [Slot() for _ in range(8)]
        ps_i = [0]

        def PS():
            i = ps_i[0]
            ps_i[0] = (i + 1) % 6
            return PSB[i], ps_slots[i]

        def ACC(i):
            return PSB[6 + i], ps_slots[6 + i]

        rsr = Ring(nc, st, "rsr", [512], F32, 4)

        f32t = Ring(nc, st, "f32t", [512], F32, 8)
        b16t = Ring(nc, st, "b16t", [512], BF16, 8)
        wA = Ring(nc, st, "wA", [8192], BF16, 2)
        xs = st.enter_context(SBT(nc, "xs", [128, KC, 512], F32))
        xs_slots = [Slot() for _ in range(KC)]
        hTr = Ring(nc, st, "hT", [KC, 512], BF16, 2)

        dma("sp", ident[:], ident_in[:, :], writes=(s_const,))
        S.op("dve", lambda e: e.memset(ones_f[:], 1.0), writes=(s_const,))
        S.op("dve", lambda e: e.memset(ones_b[:], 1.0), writes=(s_const,))
        S.op("dve", lambda e: e.memset(zeros_b[:], 0.0), writes=(s_const,))
        S.op("dve", lambda e: e.memset(zeros_f[:], 0.0), writes=(s_const,))
        S.op("dve", lambda e: e.tensor_copy(out=identb[:], in_=ident[:]), reads=(s_const,), writes=(s_const,))
        S.barrier()
        for (c0, n, p0) in (seqs if stop >= 1 else []):
            for pc in (p0 - PAD, p0 + n):
                for r in range(8):
                    dma("pool", glu[r * 128:(r + 1) * 128, pc:pc + PAD], zeros_b[:, 0:PAD], reads=(s_const,))
                for r in range(16):
                    dma("pool", hpad[r * 128:(r + 1) * 128, pc:pc + PAD], zeros_f[:, 0:PAD], reads=(s_const,))

        bg = []

        def bg_step(k=1):
            for _ in range(k):
                while bg:
                    try:
                        next(bg[0])
                        break
                    except StopIteration:
                        bg.pop(0)

        def bg_drain():
            while bg:
                bg_step()

        def run_chunks(chunks, lag):
            pend = []
            for (load, mid, fin) in chunks:
                for p in pend:
                    p[2] += 1
                while pend and pend[0][2] >= lag:
                    p = pend.pop(0)
                    if p[0] is not None:
                        p[0]()
                    p[1]()
                for p in pend:
                    if p[0] is not None and p[2] >= 1:
                        p[0]()
                        p[0] = None
                load()
                pend.append([mid, fin, 0])
                yield
            for p in pend:
                if p[0] is not None:
                    p[0]()
                p[1]()
            yield

        def cast_chunk(stg, src, r0, c0, ncol, stores, mid=None, pre=None):
            box = {}

            def load():
                if pre is not None:
                    pre()
                t, s_ = stg.get()
                S.op("pool", lambda e: e.dma_start(out=t[:, 0:ncol], in_=src[r0:r0 + 128, c0:c0 + ncol]), writes=(s_,), dma=True)
                box["t"], box["s"] = t, s_

            def fin():
                stores(box["t"], box["s"], box)
            m = None
            if mid is not None:
                def m():
                    mid(box["t"], box["s"], box)
            return (load, m, fin)

        def chunks_ffn(l, stg):
            for gi, wsrc in enumerate((w_gate, w_up)):
                for kc in range(KC):
                    for (g0, ng) in ((0, 8), (8, 8), (16, 6)):
                        def stores(t, s_, box, gi=gi, kc=kc, g0=g0, ng=ng):
                            dma("sp", Wgu[l, g0:g0 + ng, :, gi, kc, :].rearrange("g p c -> p g c"),
                                t[:, 0:ng * 256].rearrange("p (g c) -> p g c", c=256), reads=(s_,))
                        yield cast_chunk(stg, wsrc[l], kc * 128, g0 * 256, ng * 256, stores)
            for fc in range(FC):
                def stores(t, s_, box, fc=fc):
                    dma("sp", Wdn[l, :, :, fc, :].rearrange("g p c -> p g c"),
                        t[:, 0:2048].rearrange("p (g c) -> p g c", c=128), reads=(s_,))
                yield cast_chunk(stg, w_down[l], fc * 128, 0, 2048, stores)

        def chunks_pool(o_, stg):
            for kc in range(16):
                def stores(t, s_, box, kc=kc):
                    dma("sp", Wpl[o_, kc * 128:(kc + 1) * 128, :], t[:, 0:512], reads=(s_,))
                yield cast_chunk(stg, pool_w[o_], kc * 128, 0, 512, stores)

        def chunks_even(e_, stg, stg2):
            for kc in range(KC):
                def stores(t, s_, box, kc=kc):
                    dma("sp", Win[e_, 0:8, :, kc, 0:128].rearrange("g p c -> p g c"), t[:, 0:1024].rearrange("p (g c) -> p g c", c=128), reads=(s_,))
                    dma("sp", Win[e_, 0:8, :, kc, 128:256].rearrange("g p c -> p g c"), t[:, 1024:2048].rearrange("p (g c) -> p g c", c=128), reads=(s_,))
                yield cast_chunk(stg, w_in[e_], kc * 128, 0, 2048, stores)

                def mid(t, s_, box):
                    t2, s2 = stg2.get()
                    box["t2"], box["s2"] = t2, s2
                    for (o0, swp) in ((0, False), (64, True), (128, True), (192, False)):
                        if not swp:
                            S.op("dve", lambda e, t=t, t2=t2, o0=o0: e.tensor_copy(out=t2[:, o0:o0 + 64], in_=t[:, 1280:1344]), reads=(s_,), writes=(s2,))
                        else:
                            S.op("dve", lambda e, t=t, t2=t2, o0=o0: e.tensor_copy(out=t2[:, o0:o0 + 64:2], in_=t[:, 1281:1344:2]), reads=(s_,), writes=(s2,))
                            S.op("dve", lambda e, t=t, t2=t2, o0=o0: e.tensor_copy(out=t2[:, o0 + 1:o0 + 64:2], in_=t[:, 1280:1344:2]), reads=(s_,), writes=(s2,))

                def stores(t, s_, box, kc=kc):
                    dma("sp", Win[e_, 8:13, :, kc, :].rearrange("g p c -> p g c"), t[:, 0:1280].rearrange("p (g c) -> p g c", c=256), reads=(s_,))
                    dma("sp", Win[e_, 13, :, kc, 0:256], box["t2"][:, 0:256], reads=(box["s2"],))
                yield cast_chunk(stg, w_in[e_], kc * 128, 2048, 1344, stores, mid=mid)
            for kc in range(6):
                def mid(t, s_, box):
                    t2, s2 = stg2.get()
                    box["t2"], box["s2"] = t2, s2
                    tv = t[:, 0:1536].rearrange("p (h c) -> p h c", c=192)
                    t2v = t2[:, 0:2048].rearrange("p (h c) -> p h c", c=256)
                    for (o0, swp) in ((0, False), (64, True), (128, True), (192, False)):
                        if not swp:
                            S.op("dve", lambda e, tv=tv, t2v=t2v, o0=o0: e.tensor_copy(out=t2v[:, :, o0:o0 + 64], in_=tv[:, :, 128:192]), reads=(s_,), writes=(s2,))
                        else:
                            S.op("dve", lambda e, tv=tv, t2v=t2v, o0=o0: e.tensor_copy(out=t2v[:, :, o0:o0 + 64:2], in_=tv[:, :, 129:192:2]), reads=(s_,), writes=(s2,))
                            S.op("dve", lambda e, tv=tv, t2v=t2v, o0=o0: e.tensor_copy(out=t2v[:, :, o0 + 1:o0 + 64:2], in_=tv[:, :, 128:192:2]), reads=(s_,), writes=(s2,))

                def stores(t, s_, box, kc=kc):
                    tv = t[:, 0:1536].rearrange("p (h c) -> p h c", c=192)
                    t2v = box["t2"][:, 0:2048].rearrange("p (h c) -> p h c", c=256)
                    dma("sp", Wqb[e_, :, :, kc, 0:128].rearrange("h p c -> p h c"), tv[:, :, 0:128], reads=(s_,))
                    dma("sp", Wqb[e_, :, :, kc, 128:384].rearrange("h p c -> p h c"), t2v, reads=(box["s2"],))
                yield cast_chunk(stg, w_qb[e_], kc * 128, 0, 1536, stores, mid=mid)
            for kc in range(4):
                def stores(t, s_, box, kc=kc):
                    dma("sp", Wkv[e_, kc * 128:(kc + 1) * 128, :], t[:, 0:2048], reads=(s_,))
                yield cast_chunk(stg, w_kvb[e_], kc * 128, 0, 2048, stores)
            for kc in range(KC):
                def stores(t, s_, box, kc=kc):
                    dma("sp", Wo[e_, :, :, kc, :].rearrange("g p c -> p g c"), t[:, 0:2048].rearrange("p (g c) -> p g c", c=256), reads=(s_,))
                yield cast_chunk(stg, w_out[e_], kc * 128, 0, 2048, stores)

        def chunks_mod(l, stg):
            accs = [ACC(0), ACC(1)]
            sm = s_modv_l[l]
            mvf = MODV[:, l].rearrange("p w k c -> p (w k c)")
            bmf = bm[:].rearrange("p j c -> p (j c)")

            def pre():
                dma("sp", bm[:], bmodX[l], writes=(s_bm,))
                dma("sp", nw[:, 0, :], n1w[l], writes=(s_nw,))
                dma("sp", nw[:, 1, :], n2w[l], writes=(s_nw,))
                if l % 2 == 1:
                    dma("sp", psc[:], pscale[l // 2], writes=(s_nw,))
                S.op("dve", lambda e: e.tensor_copy(out=mvf, in_=bmf), reads=(s_bm,), writes=(sm,))

            def epilogue():
                for wi, ni in ((1, 0), (4, 1)):
                    for ci in range(2):
                        S.op("dve", lambda e, wi=wi, ni=ni, ci=ci: e.scalar_tensor_tensor(
                            out=MODV[:, l, wi, :, ci], in0=MODV[:, l, wi, :, ci], scalar=1.0, in1=nw[:, ni, :], op0=ALU.add, op1=ALU.mult),
                            reads=(sm, s_nw), writes=(sm,))
                if l % 2 == 1:
                    for ci in range(2):
                        S.op("dve", lambda e, ci=ci: e.tensor_tensor(out=MODV[:, l, 2, :, ci], in0=MODV[:, l, 2, :, ci], in1=psc[:], op=ALU.mult),
                             reads=(sm, s_nw), writes=(sm,))

            for rng in range(6):
                for kc in range(KC):
                    first = (rng == 0 and kc == 0)
                    last_ = (rng == 5 and kc == KC - 1)

                    def stores(t, s_, box, rng=rng, kc=kc, last_=last_):
                        def mm(e):
                            lastm = None
                            for j in range(16):
                                ps_ = accs[j // 8][0]
                                jj = j % 8
                                lastm = e.matmul(ps_[:, jj * NPADM:(jj + 1) * NPADM], lhsT=t[:, j * 128:(j + 1) * 128], rhs=scnb[:, kc, :],
                                                 start=True, stop=True)
                            return lastm
                        S.op("pe", mm, reads=(s_, s_scn), writes=(accs[0][1], accs[1][1]))
                        if True:
                            for hb in range(2):
                                ps_, ps_s_ = accs[hb]
                                c0_ = rng * 32 + hb * 16
                                S.op("dve", lambda e, ps_=ps_, c0_=c0_: e.tensor_tensor(
                                    out=mvf[:, c0_:c0_ + 16].rearrange("p (j c) -> p j c", c=2),
                                    in0=ps_[:, 0:8 * NPADM].rearrange("p (j c) -> p j c", c=NPADM)[:, :, 0:2],
                                    in1=mvf[:, c0_:c0_ + 16].rearrange("p (j c) -> p j c", c=2), op=ALU.add),
                                    reads=(ps_s_, sm), writes=(sm,))
                        if last_:
                            epilogue()
                    yield cast_chunk(stg, w_mod[l], kc * 128, rng * 2048, 2048, stores, pre=(pre if first else None))

        def mcol(l, wi, kc, ci):
            return MODV[:, l, wi, kc, ci:ci + 1]

        with ExitStack() as ph:
            cnd = ph.enter_context(SBT(nc, "cnd", [128, KC, 2], F32))
            stgP = Ring(nc, ph, "stgP", [2048], BF16, 3)
            stg2P = Ring(nc, ph, "stg2P", [2048], BF16, 3)
            s_cnd = Slot()
            dma("sp", cnd[:], condT[:, :, :], writes=(s_cnd,))
            S.op("dve", lambda e: e.memset(scnb[:], 0.0), writes=(s_scn,))
            S.op("act", lambda e: e.activation(out=scnb[:, :, 0:2], in_=cnd[:], func=AF.Silu), reads=(s_cnd, s_scn), writes=(s_scn,))
            bg.append(run_chunks(chunks_mod(0, stgP), 2))
            bg.append(run_chunks(chunks_even(0, stgP, stg2P), 2))
            bg_drain()
            S.barrier()

        def rstd_chain(stat_ps, stat_s, n, npart, inv_n, out_t, out_s):
            S.op("dve", lambda e: e.tensor_scalar(out=out_t[0:npart, 0:n], in0=stat_ps[0:npart, 0:n], scalar1=inv_n, scalar2=EPS,
                                                  op0=ALU.mult, op1=ALU.add), reads=(stat_s,), writes=(out_s,))
            S.op("act", lambda e: e.activation(out=out_t[0:npart, 0:n], in_=out_t[0:npart, 0:n], func=AF.Sqrt), reads=(out_s,), writes=(out_s,))
            S.op("dve", lambda e: e.reciprocal(out=out_t[0:npart, 0:n], in_=out_t[0:npart, 0:n]), reads=(out_s,), writes=(out_s,))

        def norm_block(xsrc, xname, b, l, which, emit_h):
            c0, n, ci = blocks[b]
            wi_a, wi_b = (1, 0) if which == 1 else (4, 3)
            st_ps, st_s = PS()
            for kc in range(KC):
                dma("sp", xs[:, kc, 0:n], xsrc[kc * 128:(kc + 1) * 128, c0:c0 + n], reads=(DS(xname, b, kc),), writes=(xs_slots[kc],))
                sq, sq_s = f32t.get()
                S.op("act", lambda e, kc=kc, sq=sq: e.activation(out=sq[:, 0:n], in_=xs[:, kc, 0:n], func=AF.Square), reads=(xs_slots[kc],), writes=(sq_s,))
                S.op("pe", lambda e, kc=kc, sq=sq: e.matmul(st_ps[:, 0:n], lhsT=ones_f[:], rhs=sq[:, 0:n], start=(kc == 0), stop=(kc == KC - 1)),
                     reads=(sq_s, s_const), writes=(st_s,))
            rs, rs_s = rsr.get()
            rstd_chain(st_ps, st_s, n, 128, 1.0 / D, rs, rs_s)
            for kc in range(KC):
                tmp, tmp_s = f32t.get()
                S.op("dve", lambda e, kc=kc, tmp=tmp: e.scalar_tensor_tensor(out=tmp[:, 0:n], in0=xs[:, kc, 0:n], scalar=mcol(l, wi_a, kc, ci),
                                                                           in1=rs[:, 0:n], op0=ALU.mult, op1=ALU.mult),
                     reads=(xs_slots[kc], rs_s, s_modv_l[l]), writes=(tmp_s,))
                emit_h(kc, tmp, tmp_s, mcol(l, wi_b, kc, ci))

        def norm_to_hT(xsrc, xname, b, l, which):
            c0, n, ci = blocks[b]
            hT, hT_s = hTr.get()

            def emit_h(kc, tmp, tmp_s, bcol):
                S.op("act", lambda e: e.activation(out=hT[:, kc, 0:n], in_=tmp[:, 0:n], func=AF.Identity, bias=bcol, scale=1.0),
                     reads=(tmp_s, s_modv_l[l]), writes=(hT_s,))
            norm_block(xsrc, xname, b, l, which, emit_h)
            return hT, hT_s

        def residual_store(ps, ps_s, dc, gcolap, xsrc, xname, xdst, dname, b, l):
            c0, n, ci = blocks[b]
            xr, xr_s = f32t.get()
            dma("sp", xr[:, 0:n], xsrc[dc * 128:(dc + 1) * 128, c0:c0 + n], reads=(DS(xname, b, dc),), writes=(xr_s,))
            yo, yo_s = f32t.get()
            S.op("dve", lambda e: e.scalar_tensor_tensor(out=yo[:, 0:n], in0=ps[:, 0:n], scalar=gcolap, in1=xr[:, 0:n], op0=ALU.mult, op1=ALU.add),
                 reads=(ps_s, xr_s, s_modv_l[l]), writes=(yo_s,))
            dma("pool", xdst[dc * 128:(dc + 1) * 128, c0:c0 + n], yo[:, 0:n], reads=(yo_s,), writes=(DS(dname, b, dc),))

        def ffn_phase(l, xsrc, xname, xdst, dname):
            with ExitStack() as ph:
                actT = ph.enter_context(SBT(nc, "actT", [128, FC, 512], BF16))
                act_s = [Slot() for _ in range(FC)]
                if l + 1 < DEPTH:
                    stgF = Ring(nc, ph, "stgF", [2048], BF16, 3)
                    bg.append(run_chunks(chunks_mod(l + 1, stgF), 2))
                    if (l + 1) % 2 == 0:
                        stg2F = Ring(nc, ph, "stg2F", [2048], BF16, 3)
                        bg.append(run_chunks(chunks_even((l + 1) // 2, stgF, stg2F), 2))
                    else:
                        bg.append(run_chunks(chunks_pool((l + 1) // 2, stgF), 2))
                        bg.append(run_chunks(chunks_ffn(l + 1, stgF), 2))
                hT_next = norm_to_hT(xsrc, xname, 0, l, 2)
                for b in range(NB):
                    c0, n, ci = blocks[b]
                    hT, hT_s = hT_next
                    for fg in range(22):
                        wt, wts = wA.get()
                        wv = wt[:, 0:8192].rearrange("p (g k c) -> p g k c", g=2, k=KC)
                        dma("sp", wv, Wgu[l, fg], reads=(DS("Wgu", l),), writes=(wts,))
                        for j in range(2):
                            fc = fg * 2 + j
                            pg, pg_s = PS()
                            pu, pu_s = PS()

                            def mm(e, wv=wv, j=j, pg=pg, pu=pu, hT=hT):
                                for kc in range(KC):
                                    e.matmul(pg[:, 0:n], lhsT=wv[:, 0, kc, j * 128:(j + 1) * 128], rhs=hT[:, kc, 0:n], start=(kc == 0), stop=(kc == KC - 1))
                                last = None
                                for kc in range(KC):
                                    last = e.matmul(pu[:, 0:n], lhsT=wv[:, 1, kc, j * 128:(j + 1) * 128], rhs=hT[:, kc, 0:n], start=(kc == 0), stop=(kc == KC - 1))
                                return last
                            S.op("pe", mm, reads=(wts, hT_s), writes=(pg_s, pu_s))
                            sg, sg_s = f32t.get()
                            S.op("act", lambda e, pg=pg, sg=sg: e.activation(out=sg[:, 0:n], in_=pg[:, 0:n], func=AF.Silu), reads=(pg_s,), writes=(sg_s,))
                            S.op("dve", lambda e, pu=pu, sg=sg, fc=fc: e.tensor_tensor(out=actT[:, fc, 0:n], in0=pu[:, 0:n], in1=sg[:, 0:n], op=ALU.mult),
                                 reads=(pu_s, sg_s), writes=(act_s[fc],))
                        bg_step()
                    if b + 1 < NB:
                        hT_next = norm_to_hT(xsrc, xname, b + 1, l, 2)
                    for dc in range(KC):
                        wt, wts = wA.get()
                        wv = wt[:, 0:FC * 128].rearrange("p (k c) -> p k c", k=FC)
                        dma("sp", wv, Wdn[l, dc], reads=(DS("Wdn", l),), writes=(wts,))
                        po, po_s = PS()

                        def mm2(e, wv=wv, po=po):
                            last = None
                            for fc in range(FC):
                                last = e.matmul(po[:, 0:n], lhsT=wv[:, fc, :], rhs=actT[:, fc, 0:n], start=(fc == 0), stop=(fc == FC - 1))
                            return last
                        S.op("pe", mm2, reads=[wts] + act_s, writes=(po_s,))
                        residual_store(po, po_s, dc, mcol(l, 5, dc, ci), xsrc, xname, xdst, dname, b, l)
                        bg_step()
                bg_drain()
                S.barrier()

        def pool_phase(l, xsrc, xname, xdst, dname):
            o_ = l // 2
            for b in range(NB):
                c0, n, ci = blocks[b]

                def emit_h(kc, tmp, tmp_s, bcol, b=b, c0=c0, n=n):
                    ho, ho_s = f32t.get()
                    S.op("act", lambda e: e.activation(out=ho[:, 0:n], in_=tmp[:, 0:n], func=AF.Identity, bias=bcol, scale=1.0),
                         reads=(tmp_s, s_modv_l[l]), writes=(ho_s,))
                    if b < NSB:
                        dma("pool", hpad[kc * 128:(kc + 1) * 128, pcol(c0):pcol(c0) + n], ho[:, 0:n], reads=(ho_s,), writes=(DS("hpad", b, kc),))
                    else:
                        for hf in range(2):
                            cc = c0 + hf * 256
                            dma("pool", hpad[kc * 128:(kc + 1) * 128, pcol(cc):pcol(cc) + 256], ho[:, hf * 256:(hf + 1) * 256], reads=(ho_s,),
                                writes=(DS("hpad", b, kc, hf),))
                norm_block(xsrc, xname, b, l, 1, emit_h)
            with ExitStack() as ph:
                PW = ph.enter_context(SBT(nc, "PW", [128, 16, 512], BF16))
                pw_s = Slot()
                dma("sp", PW[:], Wpl[o_].rearrange("(k p) c -> p k c", p=128), reads=(DS("Wpl", o_),), writes=(pw_s,))
                hp = Ring(nc, ph, "hp", [512 + 2 * PAD], F32, 3)
                pa = Ring(nc, ph, "pa", [512 + 2 * PAD], F32, 4)
                icr = Ring(nc, ph, "icr", [512], F32, 2)
                subs = [(b, b * 512, 512) for b in range(NSB)] + [(NSB, SS, 256), (NSB, SS + 256, 256)]
                for (b, c0, n) in subs:
                    ci = blocks[b][2]
                    mixT, mix_s = hTr.get()
                    for g in range(4):
                        w = POOL_W[g]
                        ic, ic_s = icr.get()
                        dma("sp", ic[:, 0:n], invcntB[:, g, c0:c0 + n], writes=(ic_s,))
                        for cc in range(4):
                            kc = g * 4 + cc
                            a, a_s = hp.get()
                            nb_lo = max(b - 1, 0)
                            rd = [DS("hpad", bb, kc) for bb in range(max(b - 1, 0), min(b + 1, NSB - 1) + 1)] if b < NSB else \
                                [DS("hpad", b, kc, 0), DS("hpad", b, kc, 1)]
                            p0 = pcol(c0) - PAD
                            L = n + 2 * PAD
                            dma("sp", a[:, 0:L], hpad[kc * 128:(kc + 1) * 128, p0:p0 + L], reads=rd, writes=(a_s,))
                            cur, cur_s, width = a, a_s, 1
                            eng_i = 0
                            while width < w:
                                nx, nx_s = pa.get()
                                ln = L - 2 * width + 1
                                S.op("dve", lambda e, cur=cur, nx=nx, width=width, ln=ln: e.tensor_tensor(
                                    out=nx[:, 0:ln], in0=cur[:, 0:ln], in1=cur[:, width:width + ln], op=ALU.add), reads=(cur_s,), writes=(nx_s,))
                                cur, cur_s = nx, nx_s
                                width *= 2
                            off = PAD - w // 2
                            t1, t1_s = pa.get()
                            S.op("pool", lambda e, cur=cur, t1=t1, ic=ic, off=off, n=n: e.tensor_tensor(
                                out=t1[:, 0:n], in0=cur[:, off:off + n], in1=ic[:, 0:n], op=ALU.mult), reads=(cur_s, ic_s), writes=(t1_s,))
                            S.op("dve", lambda e, t1=t1, a=a, kc=kc, mixT=mixT, n=n: e.tensor_tensor(
                                out=mixT[:, kc, 0:n], in0=t1[:, 0:n], in1=a[:, PAD:PAD + n], op=ALU.subtract), reads=(t1_s, a_s), writes=(mix_s,))
                    for g in range(4):
                        for ncb in range(4):
                            dc = g * 4 + ncb
                            po, po_s = PS()

                            def mm(e, g=g, ncb=ncb, po=po, mixT=mixT, n=n):
                                last = None
                                for cc in range(4):
                                    last = e.matmul(po[:, 0:n], lhsT=PW[:, g * 4 + cc, ncb * 128:(ncb + 1) * 128], rhs=mixT[:, g * 4 + cc, 0:n],
                                                    start=(cc == 0), stop=(cc == 3))
                                return last
                            S.op("pe", mm, reads=(pw_s, mix_s), writes=(po_s,))
                            xr, xr_s = f32t.get()
                            dma("sp", xr[:, 0:n], xsrc[dc * 128:(dc + 1) * 128, c0:c0 + n], reads=(DS(xname, b, dc),), writes=(xr_s,))
                            yo, yo_s = f32t.get()
                            S.op("dve", lambda e, po=po, xr=xr, yo=yo, dc=dc, n=n, ci=ci: e.scalar_tensor_tensor(
                                out=yo[:, 0:n], in0=po[:, 0:n], scalar=mcol(l, 2, dc, ci), in1=xr[:, 0:n], op0=ALU.mult, op1=ALU.add),
                                reads=(po_s, xr_s, s_modv_l[l]), writes=(yo_s,))
                            dma("pool", xdst[dc * 128:(dc + 1) * 128, c0:c0 + n], yo[:, 0:n], reads=(yo_s,), writes=(DS(dname, b, dc),))
                S.barrier()

        def even_phase(l, xsrc, xname, xdst, dname):
            e_ = l // 2
            with ExitStack() as ph:
                qa = ph.enter_context(SBT(nc, "qa", [128, 6, 512], F32))
                qan = ph.enter_context(SBT(nc, "qan", [128, 6, 512], BF16))
                ckv = ph.enter_context(SBT(nc, "ckv", [128, 4, 512], F32))
                ckvn = ph.enter_context(SBT(nc, "ckvn", [128, 4, 512], BF16))
                kpesq = ph.enter_context(SBT(nc, "kpesq", [64, 512], F32))
                krb = ph.enter_context(SBT(nc, "krb", [64, 512], F32))
                rC = ph.enter_context(SBT(nc, "rC", [64, 512], F32))
                rSn = ph.enter_context(SBT(nc, "rSn", [64, 512], F32))
                Wk = ph.enter_context(SBT(nc, "Wk", [128, 4, 1024], BF16))
                Wv = ph.enter_context(SBT(nc, "Wv", [128, 4, 1024], BF16))
                vec = ph.enter_context(SBT(nc, "vec", [128, 32], F32))
                s_qa, s_qan, s_ckv, s_ckvn, s_kpesq, s_krb, s_rope, s_wkv, s_vec = [Slot() for _ in range(9)]
                dma("sp", vec[:, 0:6], qanw[e_], writes=(s_vec,))
                dma("sp", vec[:, 6:10], kvanw[e_], writes=(s_vec,))
                dma("sp", vec[:, 10:14], qnw[e_], writes=(s_vec,))
                dma("sp", vec[:, 14:18], knw[e_], writes=(s_vec,))
                wkv_v = Wkv[e_].rearrange("(k p) (h t c) -> p k h t c", p=128, h=NH, t=2)
                for kc in range(4):
                    dma("sp", Wk[:, kc, :].rearrange("p (h c) -> p h c", h=NH), wkv_v[:, kc, :, 0, :], reads=(DS("Wkv", e_),), writes=(s_wkv,))
                    dma("sp", Wv[:, kc, :].rearrange("p (h c) -> p h c", h=NH), wkv_v[:, kc, :, 1, :], reads=(DS("Wkv", e_),), writes=(s_wkv,))

                def kv_path(n, key0, kname, rope, c0):
                    for h in range(NH):
                        pk, pk_s = PS()

                        def mm(e, h=h, pk=pk):
                            last = None
                            for rc in range(4):
                                last = e.matmul(pk[:, 0:n], lhsT=Wk[:, rc, h * 128:(h + 1) * 128], rhs=ckvn[:, rc, 0:n], start=(rc == 0), stop=(rc == 3))
                            return last
                        S.op("pe", mm, reads=(s_wkv, s_ckvn), writes=(pk_s,))
                        sq, sq_s = f32t.get()
                        S.op("act", lambda e, pk=pk, sq=sq: e.activation(out=sq[:, 0:n], in_=pk[:, 0:n], func=AF.Square), reads=(pk_s,), writes=(sq_s,))
                        stp, stp_s = PS()

                        def mm2(e, sq=sq, stp=stp):
                            e.matmul(stp[:, 0:n], lhsT=ones_f[:], rhs=sq[:, 0:n], start=True, stop=False)
                            return e.matmul(stp[:, 0:n], lhsT=ones_f[0:64, :], rhs=kpesq[:, 0:n], start=False, stop=True)
                        S.op("pe", mm2, reads=(sq_s, s_kpesq, s_const), writes=(stp_s,))
                        rs, rs_s = rsr.get()
                        rstd_chain(stp, stp_s, n, 128, 1.0 / 192, rs, rs_s)
                        ko, ko_s = b16t.get()
                        S.op("dve", lambda e, pk=pk, rs=rs, ko=ko: e.scalar_tensor_tensor(out=ko[:, 0:n], in0=pk[:, 0:n], scalar=vec[:, 14:15], in1=rs[:, 0:n],
                                                                                   op0=ALU.mult, op1=ALU.mult), reads=(pk_s, rs_s, s_vec), writes=(ko_s,))
                        dma("pool", KnT[h, :, key0:key0 + n], ko[:, 0:n], reads=(ko_s,), writes=(DS("Kn", kname, h),))
                        kr, kr_s = b16t.get()
                        S.op("pool", lambda e, rs=rs, kr=kr: e.tensor_tensor(out=kr[0:64, 0:n], in0=krb[:, 0:n], in1=rs[0:64, 0:n], op=ALU.mult),
                             reads=(rs_s, s_krb), writes=(kr_s,))
                        dma("pool", KrT[h, :, key0:key0 + n], kr[0:64, 0:n], reads=(kr_s,), writes=(DS("Kr", kname, h),))
                    for i in range(n // 128):
                        for hf in range(2):
                            pv, pv_s = PS()

                            def mm3(e, i=i, hf=hf, pv=pv):
                                last = None
                                for rc in range(4):
                                    last = e.matmul(pv[:, 0:512], lhsT=ckvn[:, rc, i * 128:(i + 1) * 128], rhs=Wv[:, rc, hf * 512:(hf + 1) * 512],
                                                    start=(rc == 0), stop=(rc == 3))
                                return last
                            S.op("pe", mm3, reads=(s_wkv, s_ckvn), writes=(pv_s,))
                            vo, vo_s = b16t.get()
                            S.op("act", lambda e, pv=pv, vo=vo: e.activation(out=vo[:, 0:512], in_=pv[:, 0:512], func=AF.Identity), reads=(pv_s,), writes=(vo_s,))
                            dma("pool", Vs[key0 + i * 128:key0 + (i + 1) * 128, hf * 512:(hf + 1) * 512], vo[:, 0:512], reads=(vo_s,),
                                writes=(DS("V", kname, i, hf),))

                def rope_apply(src, src_s, srcsw, srcsw_s, wcol, wswcol, n, out, out_s, npart=64):
                    t1, t1_s = f32t.get()
                    S.op("dve", lambda e: e.scalar_tensor_tensor(out=t1[0:64, 0:n], in0=src[0:64, 0:n], scalar=wcol, in1=rC[:, 0:n], op0=ALU.mult, op1=ALU.mult),
                         reads=(src_s, s_rope, s_vec), writes=(t1_s,))
                    t2, t2_s = f32t.get()
                    S.op("dve", lambda e: e.scalar_tensor_tensor(out=t2[0:64, 0:n], in0=srcsw[0:64, 0:n], scalar=wswcol, in1=rSn[:, 0:n], op0=ALU.mult, op1=ALU.mult),
                         reads=(srcsw_s, s_rope, s_vec), writes=(t2_s,))
                    S.op("pool", lambda e: e.tensor_tensor(out=out[0:64, 0:n], in0=t1[0:64, 0:n], in1=t2[0:64, 0:n], op=ALU.add), reads=(t1_s, t2_s), writes=(out_s,))

                cst = ph.enter_context(SBT(nc, "cst", [128, 4, PAST], F32))
                ckp = ph.enter_context(SBT(nc, "ckp", [64, PAST], F32))
                s_cst, s_ckp = Slot(), Slot()
                dma("sp", cst[:], cckvT[e_].rearrange("(k p) t -> p k t", p=128), writes=(s_cst,))
                dma("sp", ckp[:], ckpeT[e_], writes=(s_ckp,))
                S.op("act", lambda e: e.activation(out=ckvn[:, :, 0:PAST], in_=cst[:], func=AF.Identity), reads=(s_cst,), writes=(s_ckvn,))
                S.op("act", lambda e: e.activation(out=kpesq[:, 0:PAST], in_=ckp[:], func=AF.Square), reads=(s_ckp,), writes=(s_kpesq,))
                S.op("dve", lambda e: e.tensor_scalar(out=krb[:, 0:PAST], in0=ckp[:], scalar1=vec[0:64, 15:16], scalar2=None, op0=ALU.mult),
                     reads=(s_ckp, s_vec), writes=(s_krb,))
                E1S = 9
                kv_path(PAST, 0, "ctx", False, 0)

                hT_next = norm_to_hT(xsrc, xname, 0, l, 1)
                for b in range(NB):
                    c0, n, ci = blocks[b]
                    is_s = (ci == 0)
                    hT, hT_s = hT_next
                    if is_s:
                        dma("sp", rC[:, 0:n], ropeC[:, c0:c0 + n], writes=(s_rope,))
                        dma("sp", rSn[:, 0:n], ropeS[:, c0:c0 + n], writes=(s_rope,))
                    qst, qst_s = ACC(0)
                    cst_ps, cst_ps_s = ACC(1)
                    for mg in range(14):
                        wt, wts = wA.get()
                        ncol = 256
                        wv = wt[:, 0:KC * 256].rearrange("p (k c) -> p k c", k=KC)
                        dma("sp", wv[:, :, 0:ncol], Win[e_, mg, :, :, 0:ncol], reads=(DS("Win", e_),), writes=(wts,))

                        def chain(ps, lo, m, wv=wv, hT=hT):
                            def f(e):
                                last = None
                                for kc in range(KC):
                                    last = e.matmul(ps[0:m, 0:n], lhsT=wv[:, kc, lo:lo + m], rhs=hT[:, kc, 0:n], start=(kc == 0), stop=(kc == KC - 1))
                                return last
                            return f
                        if mg < 8:
                            pa_, pa_s = PS()
                            pg_, pg_s = PS()
                            S.op("pe", chain(pa_, 0, 128), reads=(wts, hT_s), writes=(pa_s,))
                            S.op("pe", chain(pg_, 128, 128), reads=(wts, hT_s), writes=(pg_s,))
                            sg, sg_s = f32t.get()
                            S.op("act", lambda e, pg_=pg_, sg=sg: e.activation(out=sg[:, 0:n], in_=pg_[:, 0:n], func=AF.Sigmoid), reads=(pg_s,), writes=(sg_s,))
                            go, go_s = b16t.get()
                            S.op("dve", lambda e, pa_=pa_, sg=sg, go=go: e.tensor_tensor(out=go[:, 0:n], in0=pa_[:, 0:n], in1=sg[:, 0:n], op=ALU.mult),
                                 reads=(pa_s, sg_s), writes=(go_s,))
                            if is_s:
                                dma("pool", glu[mg * 128:(mg + 1) * 128, pcol(c0):pcol(c0) + n], go[:, 0:n], reads=(go_s,), writes=(DS("glu", b, mg),))
                            else:
                                for hf in range(2):
                                    cc = c0 + hf * 256
                                    dma("pool", glu[mg * 128:(mg + 1) * 128, pcol(cc):pcol(cc) + 256], go[:, hf * 256:(hf + 1) * 256], reads=(go_s,),
                                        writes=(DS("glu", b, mg, hf),))
                        elif mg < 13:
                            for j in range(2):
                                mc = (mg - 8) * 2 + j
                                pp, pp_s = PS()
                                S.op("pe", chain(pp, j * 128, 128), reads=(wts, hT_s), writes=(pp_s,))
                                if mc < 6:
                                    dst, dst_s, stp, stp_s, k, last = qa[:, mc, 0:n], s_qa, qst, qst_s, mc, 5
                                else:
                                    dst, dst_s, stp, stp_s, k, last = ckv[:, mc - 6, 0:n], s_ckv, cst_ps, cst_ps_s, mc - 6, 3
                                S.op("act", lambda e, pp=pp, dst=dst: e.activation(out=dst, in_=pp[:, 0:n], func=AF.Identity), reads=(pp_s,), writes=(dst_s,))
                                sq, sq_s = f32t.get()
                                S.op("act", lambda e, pp=pp, sq=sq: e.activation(out=sq[:, 0:n], in_=pp[:, 0:n], func=AF.Square), reads=(pp_s,), writes=(sq_s,))
                                S.op("pe", lambda e, sq=sq, stp=stp, k=k, last=last: e.matmul(stp[:, 0:n], lhsT=ones_f[:], rhs=sq[:, 0:n], start=(k == 0), stop=(k == last)),
                                     reads=(sq_s, s_const), writes=(stp_s,))
                        else:
                            pk1, pk1_s = PS()
                            pk2, pk2_s = PS()
                            S.op("pe", chain(pk1, 0, 128), reads=(wts, hT_s), writes=(pk1_s,))
                            S.op("pe", chain(pk2, 128, 128), reads=(wts, hT_s), writes=(pk2_s,))
                            ko, ko_s = f32t.get()
                            S.op("act", lambda e, pk1=pk1, ko=ko: e.activation(out=ko[:, 0:n], in_=pk1[:, 0:n], func=AF.Identity), reads=(pk1_s,), writes=(ko_s,))
                            ko2, ko2_s = f32t.get()
                            S.op("act", lambda e, pk2=pk2, ko2=ko2: e.activation(out=ko2[:, 0:n], in_=pk2[:, 0:n], func=AF.Identity), reads=(pk2_s,), writes=(ko2_s,))
                            S.op("act", lambda e, ko=ko: e.activation(out=kpesq[:, 0:n], in_=ko[0:64, 0:n], func=AF.Square), reads=(ko_s,), writes=(s_kpesq,))
                            if not is_s:
                                dma("pool", kpe_out[e_, :, :], ko[0:64, 0:n], reads=(ko_s,))
                                S.op("dve", lambda e, ko=ko: e.tensor_scalar(out=krb[:, 0:n], in0=ko[0:64, 0:n], scalar1=vec[0:64, 15:16], scalar2=None, op0=ALU.mult),
                                     reads=(ko_s, s_vec), writes=(s_krb,))
                            else:
                                rope_apply(ko, ko_s, ko2, ko2_s, vec[0:64, 15:16], vec[0:64, 16:17], n, krb, s_krb)
                    if E1S < 5:
                        continue
                    rs, rs_s = rsr.get()
                    rstd_chain(qst, qst_s, n, 128, 1.0 / 768, rs, rs_s)
                    for kc in range(6):
                        S.op("dve", lambda e, kc=kc, rs=rs: e.scalar_tensor_tensor(out=qan[:, kc, 0:n], in0=qa[:, kc, 0:n], scalar=vec[:, kc:kc + 1], in1=rs[:, 0:n],
                                                                                 op0=ALU.mult, op1=ALU.mult), reads=(s_qa, rs_s, s_vec), writes=(s_qan,))
                    rs2, rs2_s = rsr.get()
                    rstd_chain(cst_ps, cst_ps_s, n, 128, 1.0 / 512, rs2, rs2_s)
                    for kc in range(4):
                        cf, cf_s = f32t.get()
                        S.op("dve", lambda e, kc=kc, rs2=rs2, cf=cf: e.scalar_tensor_tensor(out=cf[:, 0:n], in0=ckv[:, kc, 0:n], scalar=vec[:, 6 + kc:7 + kc], in1=rs2[:, 0:n],
                                                                                         op0=ALU.mult, op1=ALU.mult), reads=(s_ckv, rs2_s, s_vec), writes=(cf_s,))
                        S.op("act", lambda e, kc=kc, cf=cf: e.activation(out=ckvn[:, kc, 0:n], in_=cf[:, 0:n], func=AF.Identity), reads=(cf_s,), writes=(s_ckvn,))
                        if not is_s:
                            dma("pool", ckv_out[e_, kc * 128:(kc + 1) * 128, :], cf[:, 0:n], reads=(cf_s,))
                    if b + 1 < NB:
                        hT_next = norm_to_hT(xsrc, xname, b + 1, l, 1)
                    for h in range(NH if E1S >= 6 else 0):
                        wt, wts = wA.get()
                        wq = wt[:, 0:6 * 384].rearrange("p (k c) -> p k c", k=6)
                        dma("sp", wq, Wqb[e_, h], reads=(DS("Wqb", e_),), writes=(wts,))

                        def qchain(ps, lo, m, wq=wq):
                            def f(e):
                                last = None
                                for kc in range(6):
                                    last = e.matmul(ps[0:m, 0:n], lhsT=wq[:, kc, lo:lo + m], rhs=qan[:, kc, 0:n], start=(kc == 0), stop=(kc == 5))
                                return last
                            return f
                        pn, pn_s = PS()
                        pr, pr_s = PS()
                        S.op("pe", qchain(pn, 0, 128), reads=(wts, s_qan), writes=(pn_s,))
                        S.op("pe", qchain(pr, 128, 128), reads=(wts, s_qan), writes=(pr_s,))
                        if is_s:
                            prs, prs_s = PS()
                            S.op("pe", qchain(prs, 256, 128), reads=(wts, s_qan), writes=(prs_s,))
                        sq1, sq1_s = f32t.get()
                        S.op("act", lambda e, pn=pn, sq1=sq1: e.activation(out=sq1[:, 0:n], in_=pn[:, 0:n], func=AF.Square), reads=(pn_s,), writes=(sq1_s,))
                        prc, prc_s = f32t.get()
                        S.op("act", lambda e, pr=pr, prc=prc: e.activation(out=prc[:, 0:n], in_=pr[:, 0:n], func=AF.Identity), reads=(pr_s,), writes=(prc_s,))
                        pr, pr_s = prc, prc_s
                        if is_s:
                            prsc, prsc_s = f32t.get()
                            S.op("act", lambda e, prs=prs, prsc=prsc: e.activation(out=prsc[:, 0:n], in_=prs[:, 0:n], func=AF.Identity), reads=(prs_s,), writes=(prsc_s,))
                            prs, prs_s = prsc, prsc_s
                        sq2, sq2_s = f32t.get()
                        S.op("act", lambda e, pr=pr, sq2=sq2: e.activation(out=sq2[0:64, 0:n], in_=pr[0:64, 0:n], func=AF.Square), reads=(pr_s,), writes=(sq2_s,))
                        stp, stp_s = PS()

                        def mmq(e, sq1=sq1, sq2=sq2, stp=stp):
                            e.matmul(stp[:, 0:n], lhsT=ones_f[:], rhs=sq1[:, 0:n], start=True, stop=False)
                            return e.matmul(stp[:, 0:n], lhsT=ones_f[0:64, :], rhs=sq2[0:64, 0:n], start=False, stop=True)
                        S.op("pe", mmq, reads=(sq1_s, sq2_s, s_const), writes=(stp_s,))
                        rq, rq_s = rsr.get()
                        rstd_chain(stp, stp_s, n, 128, 1.0 / 192, rq, rq_s)
                        qo, qo_s = b16t.get()
                        S.op("dve", lambda e, pn=pn, rq=rq, qo=qo: e.scalar_tensor_tensor(out=qo[:, 0:n], in0=pn[:, 0:n], scalar=vec[:, 10:11], in1=rq[:, 0:n],
                                                                                       op0=ALU.mult, op1=ALU.mult), reads=(pn_s, rq_s, s_vec), writes=(qo_s,))
                        dma("pool", QnT[h, :, c0:c0 + n], qo[:, 0:n], reads=(qo_s,), writes=(DS("Qn", b, h),))
                        qr, qr_s = b16t.get()
                        if is_s:
                            rt, rt_s = f32t.get()
                            rope_apply(pr, pr_s, prs, prs_s, vec[0:64, 11:12], vec[0:64, 12:13], n, rt, rt_s)
                            S.op("dve", lambda e, rt=rt, rq=rq, qr=qr: e.tensor_tensor(out=qr[0:64, 0:n], in0=rt[0:64, 0:n], in1=rq[0:64, 0:n], op=ALU.mult),
                                 reads=(rt_s, rq_s), writes=(qr_s,))
                        else:
                            S.op("dve", lambda e, pr=pr, rq=rq, qr=qr: e.scalar_tensor_tensor(out=qr[0:64, 0:n], in0=pr[0:64, 0:n], scalar=vec[0:64, 11:12], in1=rq[0:64, 0:n],
                                                                                           op0=ALU.mult, op1=ALU.mult), reads=(pr_s, rq_s, s_vec), writes=(qr_s,))
                        dma("pool", QrT[h, :, c0:c0 + n], qr[0:64, 0:n], reads=(qr_s,), writes=(DS("Qr", b, h),))
                    if E1S >= 7:
                        kv_path(n, PAST + c0, b, is_s, c0)
                S.barrier()

            if stop == 4:
                return
            with ExitStack() as ph:
                Dg = ph.enter_context(SBT(nc, "Dg", [128, 8, CONV_K, 128], BF16))
                cw = ph.enter_context(SBT(nc, "cw", [128, 8, CONV_K], F32))
                cv = ph.enter_context(SBT(nc, "cv", [128, 3, 8], F32))
                s_dg, s_dg2, s_cw = Slot(), Slot(), Slot()
                dma("sp", cw[:], cdw[e_], writes=(s_cw,))
                dma("sp", cv[:, 0, :], cdb[e_], writes=(s_cw,))
                dma("sp", cv[:, 1, :], clw[e_], writes=(s_cw,))
                dma("sp", cv[:, 2, :], clb[e_], writes=(s_cw,))
                for c in range(8):
                    for k in range(CONV_K):
                        S.op("dve" if (k % 2 == 0) else "pool", lambda e, c=c, k=k: e.tensor_scalar(out=Dg[:, c, k, :], in0=ident[:], scalar1=cw[:, c, k:k + 1], scalar2=None,
                                                                                                op0=ALU.mult), reads=(s_cw, s_const), writes=((s_dg,) if (k % 2 == 0) else (s_dg2,)))
                gp = Ring(nc, ph, "gp", [512 + 2 * PAD], BF16, 3)
                cr = xs
                cr_s = xs_slots
                subs = [(b, b * 512, 512) for b in range(NSB)] + [(NSB, SS, 256), (NSB, SS + 256, 256)]
                for (b, c0, n) in subs:
                    s1, s1_s = ACC(0)
                    s2, s2_s = ACC(1)
                    for c in range(8):
                        g_, g_s = gp.get()
                        rd = [DS("glu", bb, c) for bb in range(max(b - 1, 0), min(b + 1, NSB - 1) + 1)] if b < NSB else \
                            [DS("glu", b, c, 0), DS("glu", b, c, 1)]
                        p0 = pcol(c0) - PAD
                        L = n + 2 * PAD
                        dma("sp", g_[:, 0:L], glu[c * 128:(c + 1) * 128, p0:p0 + L], reads=rd, writes=(g_s,))
                        pc, pc_s = PS()

                        def mm(e, c=c, g_=g_, pc=pc, n=n):
                            last = None
                            for k in range(CONV_K):
                                o = PAD - 15 + k
                                last = e.matmul(pc[:, 0:n], lhsT=Dg[:, c, k, :], rhs=g_[:, o:o + n], start=(k == 0), stop=(k == CONV_K - 1))
                            return last
                        S.op("pe", mm, reads=(g_s, s_dg, s_dg2), writes=(pc_s,))
                        S.op("act", lambda e, c=c, pc=pc, n=n: e.activation(out=cr[:, c, 0:n], in_=pc[:, 0:n], func=AF.Identity, bias=cv[:, 0, c:c + 1], scale=1.0),
                             reads=(pc_s, s_cw), writes=(cr_s[c],))
                        sq, sq_s = f32t.get()
                        S.op("act", lambda e, c=c, sq=sq, n=n: e.activation(out=sq[:, 0:n], in_=cr[:, c, 0:n], func=AF.Square), reads=(cr_s[c],), writes=(sq_s,))

                        def mms(e, c=c, sq=sq, n=n, s1=s1, s2=s2):
                            e.matmul(s1[:, 0:n], lhsT=ones_f[:], rhs=cr[:, c, 0:n], start=(c == 0), stop=(c == 7))
                            return e.matmul(s2[:, 0:n], lhsT=ones_f[:], rhs=sq[:, 0:n], start=(c == 0), stop=(c == 7))
                        S.op("pe", mms, reads=(cr_s[c], sq_s, s_const), writes=(s1_s, s2_s))
                    mu, mu_s = rsr.get()
                    S.op("dve", lambda e, mu=mu, s1=s1, n=n: e.tensor_scalar(out=mu[:, 0:n], in0=s1[:, 0:n], scalar1=1.0 / 1024, scalar2=None, op0=ALU.mult),
                         reads=(s1_s,), writes=(mu_s,))
                    m2, m2_s = f32t.get()
                    S.op("dve", lambda e, mu=mu, m2=m2, n=n: e.tensor_tensor(out=m2[:, 0:n], in0=mu[:, 0:n], in1=mu[:, 0:n], op=ALU.mult), reads=(mu_s,), writes=(m2_s,))
                    var, var_s = rsr.get()
                    S.op("dve", lambda e, var=var, s2=s2, m2=m2, n=n: e.scalar_tensor_tensor(out=var[:, 0:n], in0=s2[:, 0:n], scalar=1.0 / 1024, in1=m2[:, 0:n],
                                                                                           op0=ALU.mult, op1=ALU.subtract), reads=(s2_s, m2_s), writes=(var_s,))
                    S.op("dve", lambda e, var=var, n=n: e.tensor_scalar(out=var[:, 0:n], in0=var[:, 0:n], scalar1=EPS, scalar2=None, op0=ALU.add), reads=(var_s,), writes=(var_s,))
                    S.op("act", lambda e, var=var, n=n: e.activation(out=var[:, 0:n], in_=var[:, 0:n], func=AF.Sqrt), reads=(var_s,), writes=(var_s,))
                    S.op("dve", lambda e, var=var, n=n: e.reciprocal(out=var[:, 0:n], in_=var[:, 0:n]), reads=(var_s,), writes=(var_s,))
                    for c in range(8):
                        t1, t1_s = f32t.get()
                        S.op("pool", lambda e, c=c, t1=t1, mu=mu, n=n: e.tensor_tensor(out=t1[:, 0:n], in0=cr[:, c, 0:n], in1=mu[:, 0:n], op=ALU.subtract),
                             reads=(cr_s[c], mu_s), writes=(t1_s,))
                        S.op("dve", lambda e, t1=t1, var=var, n=n: e.tensor_tensor(out=t1[:, 0:n], in0=t1[:, 0:n], in1=var[:, 0:n], op=ALU.mult),
                             reads=(t1_s, var_s), writes=(t1_s,))
                        co, co_s = b16t.get()
                        S.op("act", lambda e, c=c, t1=t1, co=co, n=n: e.activation(out=co[:, 0:n], in_=t1[:, 0:n], func=AF.Silu, bias=cv[:, 2, c:c + 1], scale=cv[:, 1, c:c + 1]),
                             reads=(t1_s, s_cw), writes=(co_s,))
                        dma("pool", catT[c * 128:(c + 1) * 128, c0:c0 + n], co[:, 0:n], reads=(co_s,), writes=(DS("cat", c0, c),))
                S.barrier()

            if stop == 5:
                return
            with ExitStack() as ph:
                nkmax = PAST + SS
                Knr = Ring(nc, ph, "Knr", [nkmax], BF16, 2)
                Krr = Ring(nc, ph, "Krr", [nkmax], BF16, 2)
                Vr = Ring(nc, ph, "Vr", [nkmax // 128, 128], BF16, 2)
                qnr = Ring(nc, ph, "qnr", [512], BF16, 2)
                qrr = Ring(nc, ph, "qrr", [512], BF16, 2)
                ptr = Ring(nc, ph, "ptr", [512], BF16, 4)
                stgA = Ring(nc, ph, "stgA", [2048], BF16, 2)
                bg.append(run_chunks(chunks_ffn(l, stgA), 1))
                sc = 1.0 / np.sqrt(192.0)
                groups = [(0, PAST + SS, [(b * 512, 512, b) for b in range(NSB)]),
                          (PAST + SS, 256, [(SS, 256, NSB)]),
                          (PAST + SS + 256, 256, [(SS + 256, 256, NSB)])]
                for (key0, nk, qbs) in groups:
                    nkc = nk // 128
                    if key0 == 0:
                        krd = lambda nm, h: [DS(nm, "ctx", h)] + [DS(nm, b, h) for b in range(NSB)]
                        vrd = [DS("V", "ctx", i, hf) for i in range(2) for hf in range(2)] + [DS("V", b, i, hf) for b in range(NSB) for i in range(4) for hf in range(2)]
                    else:
                        krd = lambda nm, h: [DS(nm, NSB, h)]
                        vrd = [DS("V", NSB, i, hf) for i in range(4) for hf in range(2)]
                    for h in range(NH):
                        Kn, Kn_s = Knr.get()
                        Kr, Kr_s = Krr.get()
                        Vh, Vh_s = Vr.get()
                        dma("sp", Kn[:, 0:nk], KnT[h, :, key0:key0 + nk], reads=krd("Kn", h), writes=(Kn_s,))
                        dma("sp", Kr[0:64, 0:nk], KrT[h, :, key0:key0 + nk], reads=krd("Kr", h), writes=(Kr_s,))
                        for v0 in range(0, nkc, 8):
                            v1 = min(v0 + 8, nkc)
                            dma("sp", Vh[:, v0:v1, :], Vs[key0 + v0 * 128:key0 + v1 * 128, h * 128:(h + 1) * 128].rearrange("(c p) d -> p c d", p=128),
                                reads=vrd, writes=(Vh_s,))
                        for (c0, n, b) in qbs:
                            qn, qn_s = qnr.get()
                            qr, qr_s = qrr.get()
                            dma("sp", qn[:, 0:n], QnT[h, :, c0:c0 + n], reads=(DS("Qn", b, h),), writes=(qn_s,))
                            dma("sp", qr[0:64, 0:n], QrT[h, :, c0:c0 + n], reads=(DS("Qr", b, h),), writes=(qr_s,))
                            po, po_s = ACC(0)
                            pz, pz_s = ACC(1)
                            pend = []
                            for kc in range(nkc + 2):
                                if kc < nkc:
                                    pss, pss_s = PS()

                                    def mmS(e, kc=kc, pss=pss, Kn=Kn, Kr=Kr, qn=qn, qr=qr, n=n):
                                        e.matmul(pss[:, 0:n], lhsT=Kn[:, kc * 128:(kc + 1) * 128], rhs=qn[:, 0:n], start=True, stop=False)
                                        return e.matmul(pss[:, 0:n], lhsT=Kr[0:64, kc * 128:(kc + 1) * 128], rhs=qr[0:64, 0:n], start=False, stop=True)
                                    S.op("pe", mmS, reads=(Kn_s, Kr_s, qn_s, qr_s), writes=(pss_s,))
                                    pt, pt_s = ptr.get()
                                    S.op("act", lambda e, pss=pss, pt=pt, n=n: e.activation(out=pt[:, 0:n], in_=pss[:, 0:n], func=AF.Exp, scale=float(sc)),
                                         reads=(pss_s,), writes=(pt_s,))
                                    pend.append((kc, pt, pt_s))
                                if kc >= 2:
                                    k2, pt2, pt2_s = pend.pop(0)

                                    def mmP(e, k2=k2, pt2=pt2, Vh=Vh, po=po, pz=pz, n=n, nkc=nkc):
                                        e.matmul(po[:, 0:n], lhsT=Vh[:, k2, :], rhs=pt2[:, 0:n], start=(k2 == 0), stop=(k2 == nkc - 1))
                                        return e.matmul(pz[:, 0:n], lhsT=ones_b[:], rhs=pt2[:, 0:n], start=(k2 == 0), stop=(k2 == nkc - 1))
                                    S.op("pe", mmP, reads=(Vh_s, pt2_s, s_const), writes=(po_s, pz_s))
                            rz, rz_s = f32t.get()
                            S.op("dve", lambda e, rz=rz, pz=pz, n=n: e.reciprocal(out=rz[:, 0:n], in_=pz[:, 0:n]), reads=(pz_s,), writes=(rz_s,))
                            ao, ao_s = b16t.get()
                            S.op("dve", lambda e, ao=ao, po=po, rz=rz, n=n: e.tensor_tensor(out=ao[:, 0:n], in0=po[:, 0:n], in1=rz[:, 0:n], op=ALU.mult),
                                 reads=(po_s, rz_s), writes=(ao_s,))
                            dma("pool", catT[1024 + h * 128:1024 + (h + 1) * 128, c0:c0 + n], ao[:, 0:n], reads=(ao_s,), writes=(DS("cat", c0, 8 + h),))
                            bg_step(2)
                bg_drain()
                S.barrier()

            if stop == 6:
                return
            for b in range(NB):
                c0, n, ci = blocks[b]
                cT, cT_s = hTr.get()
                if b < NSB:
                    rd = [DS("cat", c0, r) for r in range(16)]
                else:
                    rd = [DS("cat", c0 + hf * 256, r) for r in range(16) for hf in range(2)]
                dma("sp", cT[:, :, 0:n], catT[:, c0:c0 + n].rearrange("(k p) t -> p k t", p=128), reads=rd, writes=(cT_s,))
                for dg in range(8):
                    wt, wts = wA.get()
                    wv = wt[:, 0:KC * 256].rearrange("p (k c) -> p k c", k=KC)
                    dma("sp", wv, Wo[e_, dg], reads=(DS("Wo", e_),), writes=(wts,))
                    for j in range(2):
                        dc = dg * 2 + j
                        po, po_s = PS()

                        def mm(e, wv=wv, j=j, po=po, cT=cT, n=n):
                            last = None
                            for kc in range(KC):
                                last = e.matmul(po[:, 0:n], lhsT=wv[:, kc, j * 128:(j + 1) * 128], rhs=cT[:, kc, 0:n], start=(kc == 0), stop=(kc == KC - 1))
                            return last
                        S.op("pe", mm, reads=(wts, cT_s), writes=(po_s,))
                        residual_store(po, po_s, dc, mcol(l, 2, dc, ci), xsrc, xname, xdst, dname, b, l)
            S.barrier()

        cur, cname = xT, "x_in"
        for l in range(DEPTH if stop >= 4 else 0):
            if l % 2 == 0:
                even_phase(l, cur, cname, xA, "xA%d" % l)
            else:
                pool_phase(l, cur, cname, xA, "xA%d" % l)
            last = (l == DEPTH - 1)
            dst, dn = (yT, "y") if last else (xB, "xB%d" % l)
            if stop >= 8:
                ffn_phase(l, xA, "xA%d" % l, dst, dn)
            cur, cname = dst, dn
        S.emit()
    return nc


def _colvec(v, nchunk):
    return np.ascontiguousarray(np.asarray(v, np.float32).reshape(nchunk, 128).T)


def _rope_tables(rows):
    row = np.broadcast_to(np.arange(rows, dtype=np.float32)[:, None], (rows, GRID_W)).reshape(-1)
    col = np.broadcast_to(np.arange(GRID_W, dtype=np.float32)[None, :], (rows, GRID_W)).reshape(-1)
    n_freq = 16
    inv_freq = (1.0 / (np.float32(10000.0) ** (np.arange(n_freq, dtype=np.float32) / np.float32(n_freq)))).astype(np.float32)
    ang = np.concatenate([row[:, None] * inv_freq, col[:, None] * inv_freq], axis=-1).astype(np.float32)
    c, s = np.cos(ang).astype(np.float32), np.sin(ang).astype(np.float32)
    C = np.repeat(c.T, 2, axis=0)
    Sg = np.repeat(s.T, 2, axis=0).copy()
    Sg[0::2] *= -1.0
    return np.ascontiguousarray(C), np.ascontiguousarray(Sg)


def _invcnt(s):
    t = np.arange(s)
    out = np.zeros((4, s), np.float32)
    for gi, w in enumerate(POOL_W):
        lo = np.clip(t - w // 2, 0, s)
        hi = np.clip(t + w // 2, 0, s)
        out[gi] = 1.0 / (hi - lo).astype(np.float32)
    return out


def _pairswap(v):
    v = np.asarray(v)
    o = v.copy()
    o[0::2] = v[1::2]
    o[1::2] = v[0::2]
    return o


def make_in_maps(inp, NSB, DEPTH, ncores):
    SS = NSB * 512
    NE = (DEPTH + 1) // 2
    NO = DEPTH // 2
    f = lambda a: np.ascontiguousarray(np.asarray(a, np.float32))
    shared = {}
    shared["n1w"] = np.stack([_colvec(inp["norm1_w"][l], KC) for l in range(DEPTH)])
    shared["n2w"] = np.stack([_colvec(inp["norm2_w"][l], KC) for l in range(DEPTH)])
    bm = np.stack([_colvec(inp["b_mod"][l], 96) for l in range(DEPTH)])
    shared["bmodX"] = np.ascontiguousarray(np.repeat(bm[:, :, :, None], 2, axis=3))
    shared["w_mod"] = f(inp["w_mod"][:DEPTH])
    shared["w_in"] = f(inp["w_in"][:NE])
    cdw = np.asarray(inp["conv_dw_w"], np.float32)[:NE]
    shared["cdw"] = np.ascontiguousarray(cdw.reshape(NE, CONV_K, 8, 128).transpose(0, 3, 2, 1))
    for nm, key in (("cdb", "conv_dw_b"), ("clw", "conv_ln_w"), ("clb", "conv_ln_b")):
        shared[nm] = np.stack([_colvec(inp[key][e], 8) for e in range(NE)])
    shared["qanw"] = np.stack([_colvec(inp["q_a_norm_w"][e], 6) for e in range(NE)])
    shared["kvanw"] = np.stack([_colvec(inp["kv_a_norm_w"][e], 4) for e in range(NE)])
    shared["w_qb"] = f(inp["w_q_b"][:NE])
    shared["w_kvb"] = f(inp["w_kv_b"][:NE])

    def headvec(w):
        o = np.zeros((128, 4), np.float32)
        w = np.asarray(w, np.float32)
        o[:, 0] = w[:128]
        o[:64, 1] = w[128:192]
        o[:64, 2] = _pairswap(w[128:192])
        return o
    shared["qnw"] = np.stack([headvec(inp["q_norm_w"][e]) for e in range(NE)])
    shared["knw"] = np.stack([headvec(inp["k_norm_w"][e]) for e in range(NE)])
    shared["w_out"] = f(inp["w_out"][:NE])
    if NO > 0:
        shared["pool_w"] = f(np.asarray(inp["pool_w"])[:NO].reshape(NO, 4 * 512, 512))
        shared["pscale"] = np.stack([_colvec(inp["pool_scale"][o], KC) for o in range(NO)])
    else:
        shared["pool_w"] = np.zeros((1, 2048, 512), np.float32)
        shared["pscale"] = np.zeros((1, 128, KC), np.float32)
    shared["w_gate"] = f(inp["ffn_w_gate"][:DEPTH])
    shared["w_up"] = f(inp["ffn_w_up"][:DEPTH])
    shared["w_down"] = f(inp["ffn_w_down"][:DEPTH])
    C, Sg = _rope_tables(SS // GRID_W)
    shared["ropeC"], shared["ropeS"] = C, Sg
    ic_ = np.concatenate([_invcnt(SS), _invcnt(256), _invcnt(256)], axis=1)
    shared["invcntB"] = np.ascontiguousarray(np.broadcast_to(ic_[None], (128,) + ic_.shape))
    shared["ident_in"] = np.eye(128, dtype=np.float32)
    xp = np.asarray(inp["x_prompt"], np.float32)
    xs_ = np.asarray(inp["x_sample"], np.float32)
    cc = np.asarray(inp["c"], np.float32)
    cctx = np.asarray(inp["c_ctx"], np.float32)
    cckv = np.asarray(inp["cache_ckv"], np.float32)
    ckpe = np.asarray(inp["cache_kpe"], np.float32)
    maps = []
    for i in range(ncores):
        m = dict(shared)
        xt = np.concatenate([xs_[i].T, xp[2 * i].T, xp[2 * i + 1].T], axis=1)
        m["xT"] = np.ascontiguousarray(xt)
        cnd = np.stack([cc[i], cctx], axis=1)
        m["condT"] = np.ascontiguousarray(cnd.reshape(KC, 128, 2).transpose(1, 0, 2))
        m["cckvT"] = np.ascontiguousarray(cckv[i, :NE].transpose(0, 2, 1))
        m["ckpeT"] = np.ascontiguousarray(ckpe[i, :NE].transpose(0, 2, 1))
        maps.append(m)
    return maps


def assemble(results, NSB, DEPTH, ncores):
    SS = NSB * 512
    NE = (DEPTH + 1) // 2
    y_s = np.zeros((ncores, SS, D), np.float32)
    y_p = np.zeros((2 * ncores, 256, D), np.float32)
    n_ckv = np.zeros((2 * ncores, NE, 256, 512), np.float32)
    n_kpe = np.zeros((2 * ncores, NE, 256, 64), np.float32)
    for i, r in enumerate(results):
        yT = np.asarray(r["yT"])
        y_s[i] = yT[:, :SS].T
        y_p[2 * i] = yT[:, SS:SS + 256].T
        y_p[2 * i + 1] = yT[:, SS + 256:SS + 512].T
        ck = np.asarray(r["ckv_out"])
        kp = np.asarray(r["kpe_out"])
        for j in range(2):
            n_ckv[2 * i + j] = ck[:, :, j * 256:(j + 1) * 256].transpose(0, 2, 1)
            n_kpe[2 * i + j] = kp[:, :, j * 256:(j + 1) * 256].transpose(0, 2, 1)
    return y_p, y_s, n_ckv, n_kpe


def run(inp, NSB=8, DEPTH=4, ncores=8, dbg=False):
    nc = build(NSB, DEPTH, dbg)
    maps = make_in_maps(inp, NSB, DEPTH, ncores)
    res = run_bass_kernel_spmd(nc, maps, core_ids=list(range(ncores)))
    return res


def kernel(**inputs):
    res = run(inputs, 8, 4, 8)
    return assemble(res.results, 8, 4, 8)
```

```python
import numpy as np
from contextlib import ExitStack
import concourse.bass as bass
import concourse.mybir as mybir
from concourse.bass_utils import run_bass_kernel_spmd

F32 = mybir.dt.float32
BF16 = mybir.dt.bfloat16
AF = mybir.ActivationFunctionType
ALU = mybir.AluOpType

D = 2048
KC = 16
DFF = 5632
FC = 44
CONV_K = 31
NH = 8
EPS = 1e-6
POOL_W = (2, 4, 8, 16)
GRID_W = 64
PAST = 256
PAD = 16


class Tok:
    __slots__ = ("sem", "val", "eng")

    def __init__(self, sem, val, eng):
        self.sem, self.val, self.eng = sem, val, eng


class Slot:
    __slots__ = ("w", "r")

    def __init__(self):
        self.w = None
        self.r = {}


class Sched:
    COMPUTE = ("pe", "act", "dve", "pool")
    NDMA = 12

    def __init__(self, nc, stack):
        self.nc = nc
        self.streams = {e: [] for e in ("pe", "act", "dve", "pool", "sp")}
        self.sem = {}
        self.cnt = {}
        for e in self.COMPUTE:
            self.sem[e] = stack.enter_context(nc.semaphore("s_" + e))
            self.cnt[e] = 0
        self.dsem = {}
        self.dcnt = {}
        self.dptr = {}
        for q in ("sp", "pool"):
            self.dsem[q] = [stack.enter_context(nc.semaphore("d_%s%d" % (q, i))) for i in range(self.NDMA)]
            self.dcnt[q] = [0] * self.NDMA
            self.dptr[q] = 0
        self.known = {e: {} for e in self.streams}
        self.nops = 0

    def _wait(self, eng, tok):
        k = self.known[eng]
        key = id(tok.sem)
        if k.get(key, 0) >= tok.val:
            return
        k[key] = tok.val
        sem, val = tok.sem, tok.val
        self.streams[eng].append(lambda e: e.wait_ge(sem, val))

    def op(self, eng, fn, reads=(), writes=(), dma=False):
        deps = []
        for s in reads:
            if s.w is not None:
                deps.append((s.w, True))
        for s in writes:
            if s.w is not None:
                deps.append((s.w, False))
            for t in s.r.values():
                deps.append((t, False))
        for t, raw in deps:
            if (not dma) and t.eng == eng:
                if eng == "pe" or not raw:
                    continue
            self._wait(eng, t)
        if dma:
            q = eng
            i = self.dptr[q]
            self.dptr[q] = (i + 1) % self.NDMA
            sem = self.dsem[q][i]
            if self.dcnt[q][i] > 0:
                self._wait(eng, Tok(sem, self.dcnt[q][i], "dma_" + q))
            self.dcnt[q][i] += 16
            tok = Tok(sem, self.dcnt[q][i], "dma_" + q)
            self.streams[eng].append(lambda e: fn(e).then_inc(sem, 16))
        else:
            self.cnt[eng] += 1
            sem = self.sem[eng]
            tok = Tok(sem, self.cnt[eng], eng)
            self.streams[eng].append(lambda e: fn(e).then_inc(sem, 1))
        for s in writes:
            s.w = tok
            s.r = {}
        for s in reads:
            s.r[id(tok.sem)] = tok
        self.nops += 1
        return tok

    def barrier(self):
        toks = []
        for e in self.COMPUTE:
            if self.cnt[e] > 0:
                toks.append(Tok(self.sem[e], self.cnt[e], e))
        for q in ("sp", "pool"):
            for i in range(self.NDMA):
                if self.dcnt[q][i] > 0:
                    toks.append(Tok(self.dsem[q][i], self.dcnt[q][i], "dma_" + q))
        for e in self.streams:
            for t in toks:
                self._wait(e, t)

    def emit(self):
        nc = self.nc
        self.barrier()
        with nc.Block() as block:
            @block.sync
            def _(eng):
                for f in self.streams["sp"]:
                    f(eng)

            @block.tensor
            def _(eng):
                for f in self.streams["pe"]:
                    f(eng)

            @block.scalar
            def _(eng):
                for f in self.streams["act"]:
                    f(eng)

            @block.vector
            def _(eng):
                for f in self.streams["dve"]:
                    f(eng)

            @block.gpsimd
            def _(eng):
                for f in self.streams["pool"]:
                    f(eng)


_UID = [0]


def SBT(nc, name, shape, dtype):
    _UID[0] += 1
    return nc.sbuf_tensor("%s_u%d" % (name, _UID[0]), shape, dtype)


class Ring:
    def __init__(self, nc, st, name, shape, dtype, n):
        self.t = st.enter_context(SBT(nc, name, [128, n] + list(shape), dtype))
        self.slots = [Slot() for _ in range(n)]
        self.i = 0
        self.n = n

    def get(self):
        i = self.i
        self.i = (i + 1) % self.n
        return self.t[:, i], self.slots[i]


def build(NSB=8, DEPTH=4, dbg=False, stop=99):
    SS = NSB * 512
    T = SS + 512
    TK = PAST + T
    TP = T + 6 * PAD
    NE = (DEPTH + 1) // 2
    NO = DEPTH // 2
    nc = bass.Bass("TRN2", target_bir_lowering=False)

    def din(name, shape, dt=F32):
        return nc.dram_tensor(name, list(shape), dt, kind="ExternalInput").ap()

    def dout(name, shape, dt=F32):
        return nc.dram_tensor(name, list(shape), dt, kind="ExternalOutput").ap()

    def dscr(name, shape, dt):
        return nc.dram_tensor(name, list(shape), dt, kind="ExternalOutput" if dbg else "Internal").ap()

    xT = din("xT", [D, T])
    condT = din("condT", [128, KC, 2])
    cckvT = din("cckvT", [NE, 512, PAST])
    ckpeT = din("ckpeT", [NE, 64, PAST])
    n1w = din("n1w", [DEPTH, 128, KC])
    n2w = din("n2w", [DEPTH, 128, KC])
    bmodX = din("bmodX", [DEPTH, 128, 96, 2])
    w_mod = din("w_mod", [DEPTH, D, 6 * D])
    w_in = din("w_in", [NE, D, 3392])
    cdw = din("cdw", [NE, 128, 8, CONV_K])
    cdb = din("cdb", [NE, 128, 8])
    clw = din("clw", [NE, 128, 8])
    clb = din("clb", [NE, 128, 8])
    qanw = din("qanw", [NE, 128, 6])
    w_qb = din("w_qb", [NE, 768, 1536])
    kvanw = din("kvanw", [NE, 128, 4])
    w_kvb = din("w_kvb", [NE, 512, 2048])
    qnw = din("qnw", [NE, 128, 4])
    knw = din("knw", [NE, 128, 4])
    w_out = din("w_out", [NE, D, D])
    pool_w = din("pool_w", [max(NO, 1), 4 * 512, 512])
    pscale = din("pscale", [max(NO, 1), 128, KC])
    w_gate = din("w_gate", [DEPTH, D, DFF])
    w_up = din("w_up", [DEPTH, D, DFF])
    w_down = din("w_down", [DEPTH, DFF, D])
    ropeC = din("ropeC", [64, SS])
    ropeS = din("ropeS", [64, SS])
    invcntB = din("invcntB", [128, 4, T])
    ident_in = din("ident_in", [128, 128])
    yT = dout("yT", [D, T])
    ckv_out = dout("ckv_out", [NE, 512, 512])
    kpe_out = dout("kpe_out", [NE, 64, 512])
    xA = dscr("xA", [D, T], F32)
    xB = dscr("xB", [D, T], F32)
    hpad = dscr("hpad", [D, TP], F32)
    glu = dscr("glu", [1024, TP], BF16)
    catT = dscr("catT", [D, T], BF16)
    QnT = dscr("QnT", [NH, 128, T], BF16)
    QrT = dscr("QrT", [NH, 64, T], BF16)
    KnT = dscr("KnT", [NH, 128, TK], BF16)
    KrT = dscr("KrT", [NH, 64, TK], BF16)
    Vs = dscr("Vs", [TK, 1024], BF16)
    Wgu = dscr("Wgu", [DEPTH, 22, 128, 2, KC, 256], BF16)
    Wdn = dscr("Wdn", [DEPTH, 16, 128, FC, 128], BF16)
    Win = dscr("Win", [NE, 14, 128, KC, 256], BF16)
    Wqb = dscr("Wqb", [NE, NH, 128, 6, 384], BF16)
    Wkv = dscr("Wkv", [NE, 512, 2048], BF16)
    Wo = dscr("Wo", [NE, 8, 128, KC, 256], BF16)
    Wpl = dscr("Wpl", [max(NO, 1), 2048, 512], BF16)

    seqs = [(0, SS, PAD), (SS, 256, 3 * PAD + SS), (SS + 256, 256, 5 * PAD + SS + 256)]
    blocks = []
    for b in range(NSB):
        blocks.append((b * 512, 512, 0))
    blocks.append((SS, 512, 1))
    NB = len(blocks)

    def pcol(c):
        if c < SS:
            return c + PAD
        if c < SS + 256:
            return c + 3 * PAD
        return c + 5 * PAD

    dsl = {}

    def DS(*key):
        s = dsl.get(key)
        if s is None:
            s = Slot()
            dsl[key] = s
        return s

    with ExitStack() as st:
        S = Sched(nc, st)

        def dma(q, out, in_, reads=(), writes=()):
            return S.op(q, lambda e: e.dma_start(out=out, in_=in_), reads, writes, dma=True)

        ident = st.enter_context(SBT(nc, "ident", [128, 128], F32))
        identb = st.enter_context(SBT(nc, "identb", [128, 128], BF16))
        ones_f = st.enter_context(SBT(nc, "ones_f", [128, 128], F32))
        ones_b = st.enter_context(SBT(nc, "ones_b", [128, 128], BF16))
        zeros_b = st.enter_context(SBT(nc, "zeros_b", [128, 2 * PAD], BF16))
        zeros_f = st.enter_context(SBT(nc, "zeros_f", [128, 2 * PAD], F32))
        MODV = st.enter_context(SBT(nc, "MODV", [128, DEPTH, 6, KC, 2], F32))
        s_const = Slot()
        s_modv_l = [Slot() for _ in range(DEPTH)]
        NPADM = 64
        scnb = st.enter_context(SBT(nc, "scnb", [128, KC, NPADM], BF16))
        bm = st.enter_context(SBT(nc, "bm", [128, 96, 2], F32))
        nw = st.enter_context(SBT(nc, "nw", [128, 2, KC], F32))
        psc = st.enter_context(SBT(nc, "psc", [128, KC], F32))
        s_scn, s_bm, s_nw = Slot(), Slot(), Slot()
        PSB = [st.enter_context(nc.psum_tensor("psb%d" % i, [128, 512], F32)) for i in range(8)]
        ps_slots = [Slot() for _ in range(8)]
        ps_i = [0]

        def PS():
            i = ps_i[0]
            ps_i[0] = (i + 1) % 6
            return PSB[i], ps_slots[i]

        def ACC(i):
            return PSB[6 + i], ps_slots[6 + i]

        rsr = Ring(nc, st, "rsr", [512], F32, 4)

        f32t = Ring(nc, st, "f32t", [512], F32, 8)
        b16t = Ring(nc, st, "b16t", [512], BF16, 8)
        wA = Ring(nc, st, "wA", [8192], BF16, 2)
        xs = st.enter_context(SBT(nc, "xs", [128, KC, 512], F32))
        xs_slots = [Slot() for _ in range(KC)]
        hTr = Ring(nc, st, "hT", [KC, 512], BF16, 2)

        dma("sp", ident[:], ident_in[:, :], writes=(s_const,))
        S.op("dve", lambda e: e.memset(ones_f[:], 1.0), writes=(s_const,))
        S.op("dve", lambda e: e.memset(ones_b[:], 1.0), writes=(s_const,))
        S.op("dve", lambda e: e.memset(zeros_b[:], 0.0), writes=(s_const,))
        S.op("dve", lambda e: e.memset(zeros_f[:], 0.0), writes=(s_const,))
        S.op("dve", lambda e: e.tensor_copy(out=identb[:], in_=ident[:]), reads=(s_const,), writes=(s_const,))
        S.barrier()
        for (c0, n, p0) in (seqs if stop >= 1 else []):
            for pc in (p0 - PAD, p0 + n):
                for r in range(8):
                    dma("pool", glu[r * 128:(r + 1) * 128, pc:pc + PAD], zeros_b[:, 0:PAD], reads=(s_const,))
                for r in range(16):
                    dma("pool", hpad[r * 128:(r + 1) * 128, pc:pc + PAD], zeros_f[:, 0:PAD], reads=(s_const,))

        bg = []

        def bg_step(k=1):
            for _ in range(k):
                while bg:
                    try:
                        next(bg[0])
                        break
                    except StopIteration:
                        bg.pop(0)

        def bg_drain():
            while bg:
                bg_step()

        def run_chunks(chunks, lag):
            pend = []
            for (load, mid, fin) in chunks:
                for p in pend:
                    p[2] += 1
                while pend and pend[0][2] >= lag:
                    p = pend.pop(0)
                    if p[0] is not None:
                        p[0]()
                    p[1]()
                for p in pend:
                    if p[0] is not None and p[2] >= 1:
                        p[0]()
                        p[0] = None
                load()
                pend.append([mid, fin, 0])
                yield
            for p in pend:
                if p[0] is not None:
                    p[0]()
                p[1]()
            yield

        def run_batched(chunks, batch):
            pend = []
            it = iter(chunks)
            while True:
                for (mid, fin) in pend:
                    if mid is not None:
                        mid()
                    fin()
                pend = []
                for c in it:
                    c[0]()
                    pend.append((c[1], c[2]))
                    if len(pend) == batch:
                        break
                if not pend:
                    break
                yield

        def cast_chunk(stg, src, r0, c0, ncol, stores, mid=None, pre=None):
            box = {}

            def load():
                if pre is not None:
                    pre()
                t, s_ = stg.get()
                S.op("pool", lambda e: e.dma_start(out=t[:, 0:ncol], in_=src[r0:r0 + 128, c0:c0 + ncol]), writes=(s_,), dma=True)
                box["t"], box["s"] = t, s_

            def fin():
                stores(box["t"], box["s"], box)
            m = None
            if mid is not None:
                def m():
                    mid(box["t"], box["s"], box)
            return (load, m, fin)

        def chunks_ffn(l, stg):
            for gi, wsrc in enumerate((w_gate, w_up)):
                for kc in range(KC):
                    for (g0, ng) in ((0, 8), (8, 8), (16, 6)):
                        def stores(t, s_, box, gi=gi, kc=kc, g0=g0, ng=ng):
                            dma("sp", Wgu[l, g0:g0 + ng, :, gi, kc, :].rearrange("g p c -> p g c"),
                                t[:, 0:ng * 256].rearrange("p (g c) -> p g c", c=256), reads=(s_,))
                        yield cast_chunk(stg, wsrc[l], kc * 128, g0 * 256, ng * 256, stores)
            for fc in range(FC):
                def stores(t, s_, box, fc=fc):
                    dma("sp", Wdn[l, :, :, fc, :].rearrange("g p c -> p g c"),
                        t[:, 0:2048].rearrange("p (g c) -> p g c", c=128), reads=(s_,))
                yield cast_chunk(stg, w_down[l], fc * 128, 0, 2048, stores)

        def chunks_pool(o_, stg):
            for kc in range(16):
                def stores(t, s_, box, kc=kc):
                    dma("sp", Wpl[o_, kc * 128:(kc + 1) * 128, :], t[:, 0:512], reads=(s_,))
                yield cast_chunk(stg, pool_w[o_], kc * 128, 0, 512, stores)

        def chunks_even(e_, stg, stg2):
            for kc in range(KC):
                def stores(t, s_, box, kc=kc):
                    dma("sp", Win[e_, 0:8, :, kc, 0:128].rearrange("g p c -> p g c"), t[:, 0:1024].rearrange("p (g c) -> p g c", c=128), reads=(s_,))
                    dma("sp", Win[e_, 0:8, :, kc, 128:256].rearrange("g p c -> p g c"), t[:, 1024:2048].rearrange("p (g c) -> p g c", c=128), reads=(s_,))
                yield cast_chunk(stg, w_in[e_], kc * 128, 0, 2048, stores)

                def mid(t, s_, box):
                    t2, s2 = stg2.get()
                    box["t2"], box["s2"] = t2, s2
                    for (o0, swp) in ((0, False), (64, True), (128, True), (192, False)):
                        if not swp:
                            S.op("dve", lambda e, t=t, t2=t2, o0=o0: e.tensor_copy(out=t2[:, o0:o0 + 64], in_=t[:, 1280:1344]), reads=(s_,), writes=(s2,))
                        else:
                            S.op("dve", lambda e, t=t, t2=t2, o0=o0: e.tensor_copy(out=t2[:, o0:o0 + 64:2], in_=t[:, 1281:1344:2]), reads=(s_,), writes=(s2,))
                            S.op("dve", lambda e, t=t, t2=t2, o0=o0: e.tensor_copy(out=t2[:, o0 + 1:o0 + 64:2], in_=t[:, 1280:1344:2]), reads=(s_,), writes=(s2,))

                def stores(t, s_, box, kc=kc):
                    dma("sp", Win[e_, 8:13, :, kc, :].rearrange("g p c -> p g c"), t[:, 0:1280].rearrange("p (g c) -> p g c", c=256), reads=(s_,))
                    dma("sp", Win[e_, 13, :, kc, 0:256], box["t2"][:, 0:256], reads=(box["s2"],))
                yield cast_chunk(stg, w_in[e_], kc * 128, 2048, 1344, stores, mid=mid)
            for kc in range(6):
                def mid(t, s_, box):
                    t2, s2 = stg2.get()
                    box["t2"], box["s2"] = t2, s2
                    tv = t[:, 0:1536].rearrange("p (h c) -> p h c", c=192)
                    t2v = t2[:, 0:2048].rearrange("p (h c) -> p h c", c=256)
                    for (o0, swp) in ((0, False), (64, True), (128, True), (192, False)):
                        if not swp:
                            S.op("dve", lambda e, tv=tv, t2v=t2v, o0=o0: e.tensor_copy(out=t2v[:, :, o0:o0 + 64], in_=tv[:, :, 128:192]), reads=(s_,), writes=(s2,))
                        else:
                            S.op("dve", lambda e, tv=tv, t2v=t2v, o0=o0: e.tensor_copy(out=t2v[:, :, o0:o0 + 64:2], in_=tv[:, :, 129:192:2]), reads=(s_,), writes=(s2,))
                            S.op("dve", lambda e, tv=tv, t2v=t2v, o0=o0: e.tensor_copy(out=t2v[:, :, o0 + 1:o0 + 64:2], in_=tv[:, :, 128:192:2]), reads=(s_,), writes=(s2,))

                def stores(t, s_, box, kc=kc):
                    tv = t[:, 0:1536].rearrange("p (h c) -> p h c", c=192)
                    t2v = box["t2"][:, 0:2048].rearrange("p (h c) -> p h c", c=256)
                    dma("sp", Wqb[e_, :, :, kc, 0:128].rearrange("h p c -> p h c"), tv[:, :, 0:128], reads=(s_,))
                    dma("sp", Wqb[e_, :, :, kc, 128:384].rearrange("h p c -> p h c"), t2v, reads=(box["s2"],))
                yield cast_chunk(stg, w_qb[e_], kc * 128, 0, 1536, stores, mid=mid)
            for kc in range(4):
                def stores(t, s_, box, kc=kc):
                    dma("sp", Wkv[e_, kc * 128:(kc + 1) * 128, :], t[:, 0:2048], reads=(s_,))
                yield cast_chunk(stg, w_kvb[e_], kc * 128, 0, 2048, stores)
            for kc in range(KC):
                def stores(t, s_, box, kc=kc):
                    dma("sp", Wo[e_, :, :, kc, :].rearrange("g p c -> p g c"), t[:, 0:2048].rearrange("p (g c) -> p g c", c=256), reads=(s_,))
                yield cast_chunk(stg, w_out[e_], kc * 128, 0, 2048, stores)

        def chunks_mod(l, stg):
            accs = [ACC(0), ACC(1)]
            sm = s_modv_l[l]
            mvf = MODV[:, l].rearrange("p w k c -> p (w k c)")
            bmf = bm[:].rearrange("p j c -> p (j c)")

            def pre():
                dma("sp", bm[:], bmodX[l], writes=(s_bm,))
                dma("sp", nw[:, 0, :], n1w[l], writes=(s_nw,))
                dma("sp", nw[:, 1, :], n2w[l], writes=(s_nw,))
                if l % 2 == 1:
                    dma("sp", psc[:], pscale[l // 2], writes=(s_nw,))
                S.op("dve", lambda e: e.tensor_copy(out=mvf, in_=bmf), reads=(s_bm,), writes=(sm,))

            def epilogue():
                for wi, ni in ((1, 0), (4, 1)):
                    for ci in range(2):
                        S.op("dve", lambda e, wi=wi, ni=ni, ci=ci: e.scalar_tensor_tensor(
                            out=MODV[:, l, wi, :, ci], in0=MODV[:, l, wi, :, ci], scalar=1.0, in1=nw[:, ni, :], op0=ALU.add, op1=ALU.mult),
                            reads=(sm, s_nw), writes=(sm,))
                if l % 2 == 1:
                    for ci in range(2):
                        S.op("dve", lambda e, ci=ci: e.tensor_tensor(out=MODV[:, l, 2, :, ci], in0=MODV[:, l, 2, :, ci], in1=psc[:], op=ALU.mult),
                             reads=(sm, s_nw), writes=(sm,))

            for rng in range(6):
                for kc in range(KC):
                    first = (rng == 0 and kc == 0)
                    last_ = (rng == 5 and kc == KC - 1)

                    def stores(t, s_, box, rng=rng, kc=kc, last_=last_):
                        def mm(e):
                            lastm = None
                            for j in range(16):
                                ps_ = accs[j // 8][0]
                                jj = j % 8
                                lastm = e.matmul(ps_[:, jj * NPADM:(jj + 1) * NPADM], lhsT=t[:, j * 128:(j + 1) * 128], rhs=scnb[:, kc, :],
                                                 start=True, stop=True)
                            return lastm
                        S.op("pe", mm, reads=(s_, s_scn), writes=(accs[0][1], accs[1][1]))
                        if True:
                            for hb in range(2):
                                ps_, ps_s_ = accs[hb]
                                c0_ = rng * 32 + hb * 16
                                S.op("dve", lambda e, ps_=ps_, c0_=c0_: e.tensor_tensor(
                                    out=mvf[:, c0_:c0_ + 16].rearrange("p (j c) -> p j c", c=2),
                                    in0=ps_[:, 0:8 * NPADM].rearrange("p (j c) -> p j c", c=NPADM)[:, :, 0:2],
                                    in1=mvf[:, c0_:c0_ + 16].rearrange("p (j c) -> p j c", c=2), op=ALU.add),
                                    reads=(ps_s_, sm), writes=(sm,))
                        if last_:
                            epilogue()
                    yield cast_chunk(stg, w_mod[l], kc * 128, rng * 2048, 2048, stores, pre=(pre if first else None))

        def mcol(l, wi, kc, ci):
            return MODV[:, l, wi, kc, ci:ci + 1]

        with ExitStack() as ph:
            cnd = ph.enter_context(SBT(nc, "cnd", [128, KC, 2], F32))
            stgP = Ring(nc, ph, "stgP", [2048], BF16, 3)
            stg2P = Ring(nc, ph, "stg2P", [2048], BF16, 3)
            s_cnd = Slot()
            dma("sp", cnd[:], condT[:, :, :], writes=(s_cnd,))
            S.op("dve", lambda e: e.memset(scnb[:], 0.0), writes=(s_scn,))
            S.op("act", lambda e: e.activation(out=scnb[:, :, 0:2], in_=cnd[:], func=AF.Silu), reads=(s_cnd, s_scn), writes=(s_scn,))
            bg.append(run_chunks(chunks_mod(0, stgP), 2))
            bg.append(run_chunks(chunks_even(0, stgP, stg2P), 2))
            bg_drain()
            S.barrier()

        def rstd_chain(stat_ps, stat_s, n, npart, inv_n, out_t, out_s):
            S.op("dve", lambda e: e.tensor_scalar(out=out_t[0:npart, 0:n], in0=stat_ps[0:npart, 0:n], scalar1=inv_n, scalar2=EPS,
                                                  op0=ALU.mult, op1=ALU.add), reads=(stat_s,), writes=(out_s,))
            S.op("act", lambda e: e.activation(out=out_t[0:npart, 0:n], in_=out_t[0:npart, 0:n], func=AF.Sqrt), reads=(out_s,), writes=(out_s,))
            S.op("dve", lambda e: e.reciprocal(out=out_t[0:npart, 0:n], in_=out_t[0:npart, 0:n]), reads=(out_s,), writes=(out_s,))

        def norm_block(xsrc, xname, b, l, which, emit_h):
            c0, n, ci = blocks[b]
            wi_a, wi_b = (1, 0) if which == 1 else (4, 3)
            st_ps, st_s = PS()
            for kc in range(KC):
                dma("sp", xs[:, kc, 0:n], xsrc[kc * 128:(kc + 1) * 128, c0:c0 + n], reads=(DS(xname, b, kc),), writes=(xs_slots[kc],))
                sq, sq_s = f32t.get()
                S.op("act", lambda e, kc=kc, sq=sq: e.activation(out=sq[:, 0:n], in_=xs[:, kc, 0:n], func=AF.Square), reads=(xs_slots[kc],), writes=(sq_s,))
                S.op("pe", lambda e, kc=kc, sq=sq: e.matmul(st_ps[:, 0:n], lhsT=ones_f[:], rhs=sq[:, 0:n], start=(kc == 0), stop=(kc == KC - 1)),
                     reads=(sq_s, s_const), writes=(st_s,))
            rs, rs_s = rsr.get()
            rstd_chain(st_ps, st_s, n, 128, 1.0 / D, rs, rs_s)
            for kc in range(KC):
                tmp, tmp_s = f32t.get()
                S.op("dve", lambda e, kc=kc, tmp=tmp: e.scalar_tensor_tensor(out=tmp[:, 0:n], in0=xs[:, kc, 0:n], scalar=mcol(l, wi_a, kc, ci),
                                                                           in1=rs[:, 0:n], op0=ALU.mult, op1=ALU.mult),
                     reads=(xs_slots[kc], rs_s, s_modv_l[l]), writes=(tmp_s,))
                emit_h(kc, tmp, tmp_s, mcol(l, wi_b, kc, ci))

        def norm_to_hT(xsrc, xname, b, l, which):
            c0, n, ci = blocks[b]
            hT, hT_s = hTr.get()

            def emit_h(kc, tmp, tmp_s, bcol):
                S.op("act", lambda e: e.activation(out=hT[:, kc, 0:n], in_=tmp[:, 0:n], func=AF.Identity, bias=bcol, scale=1.0),
                     reads=(tmp_s, s_modv_l[l]), writes=(hT_s,))
            norm_block(xsrc, xname, b, l, which, emit_h)
            return hT, hT_s

        def residual_store(ps, ps_s, dc, gcolap, xsrc, xname, xdst, dname, b, l):
            c0, n, ci = blocks[b]
            xr, xr_s = f32t.get()
            dma("sp", xr[:, 0:n], xsrc[dc * 128:(dc + 1) * 128, c0:c0 + n], reads=(DS(xname, b, dc),), writes=(xr_s,))
            yo, yo_s = f32t.get()
            S.op("dve", lambda e: e.scalar_tensor_tensor(out=yo[:, 0:n], in0=ps[:, 0:n], scalar=gcolap, in1=xr[:, 0:n], op0=ALU.mult, op1=ALU.add),
                 reads=(ps_s, xr_s, s_modv_l[l]), writes=(yo_s,))
            dma("pool", xdst[dc * 128:(dc + 1) * 128, c0:c0 + n], yo[:, 0:n], reads=(yo_s,), writes=(DS(dname, b, dc),))

        def ffn_phase(l, xsrc, xname, xdst, dname):
            with ExitStack() as ph:
                actT = ph.enter_context(SBT(nc, "actT", [128, FC, 512], BF16))
                act_s = [Slot() for _ in range(FC)]
                if l + 1 < DEPTH:
                    stgF = Ring(nc, ph, "stgF", [2048], BF16, 3)
                    bg.append(run_chunks(chunks_mod(l + 1, stgF), 2))
                    if (l + 1) % 2 == 0:
                        stg2F = Ring(nc, ph, "stg2F", [2048], BF16, 3)
                        bg.append(run_chunks(chunks_even((l + 1) // 2, stgF, stg2F), 2))
                    else:
                        bg.append(run_chunks(chunks_pool((l + 1) // 2, stgF), 2))
                        bg.append(run_chunks(chunks_ffn(l + 1, stgF), 2))
                hT_next = norm_to_hT(xsrc, xname, 0, l, 2)
                for b in range(NB):
                    c0, n, ci = blocks[b]
                    hT, hT_s = hT_next
                    for fg in range(22):
                        wt, wts = wA.get()
                        wv = wt[:, 0:8192].rearrange("p (g k c) -> p g k c", g=2, k=KC)
                        dma("sp", wv, Wgu[l, fg], reads=(DS("Wgu", l),), writes=(wts,))
                        for j in range(2):
                            fc = fg * 2 + j
                            pg, pg_s = PS()
                            pu, pu_s = PS()

                            def mm(e, wv=wv, j=j, pg=pg, pu=pu, hT=hT):
                                for kc in range(KC):
                                    e.matmul(pg[:, 0:n], lhsT=wv[:, 0, kc, j * 128:(j + 1) * 128], rhs=hT[:, kc, 0:n], start=(kc == 0), stop=(kc == KC - 1))
                                last = None
                                for kc in range(KC):
                                    last = e.matmul(pu[:, 0:n], lhsT=wv[:, 1, kc, j * 128:(j + 1) * 128], rhs=hT[:, kc, 0:n], start=(kc == 0), stop=(kc == KC - 1))
                                return last
                            S.op("pe", mm, reads=(wts, hT_s), writes=(pg_s, pu_s))
                            sg, sg_s = f32t.get()
                            S.op("act", lambda e, pg=pg, sg=sg: e.activation(out=sg[:, 0:n], in_=pg[:, 0:n], func=AF.Silu), reads=(pg_s,), writes=(sg_s,))
                            S.op("dve", lambda e, pu=pu, sg=sg, fc=fc: e.tensor_tensor(out=actT[:, fc, 0:n], in0=pu[:, 0:n], in1=sg[:, 0:n], op=ALU.mult),
                                 reads=(pu_s, sg_s), writes=(act_s[fc],))
                        bg_step()
                    if b + 1 < NB:
                        hT_next = norm_to_hT(xsrc, xname, b + 1, l, 2)
                    for dc in range(KC):
                        wt, wts = wA.get()
                        wv = wt[:, 0:FC * 128].rearrange("p (k c) -> p k c", k=FC)
                        dma("sp", wv, Wdn[l, dc], reads=(DS("Wdn", l),), writes=(wts,))
                        po, po_s = PS()

                        def mm2(e, wv=wv, po=po):
                            last = None
                            for fc in range(FC):
                                last = e.matmul(po[:, 0:n], lhsT=wv[:, fc, :], rhs=actT[:, fc, 0:n], start=(fc == 0), stop=(fc == FC - 1))
                            return last
                        S.op("pe", mm2, reads=[wts] + act_s, writes=(po_s,))
                        residual_store(po, po_s, dc, mcol(l, 5, dc, ci), xsrc, xname, xdst, dname, b, l)
                        bg_step()
                bg_drain()
                S.barrier()

        def pool_phase(l, xsrc, xname, xdst, dname):
            o_ = l // 2
            for b in range(NB):
                c0, n, ci = blocks[b]

                def emit_h(kc, tmp, tmp_s, bcol, b=b, c0=c0, n=n):
                    ho, ho_s = f32t.get()
                    S.op("act", lambda e: e.activation(out=ho[:, 0:n], in_=tmp[:, 0:n], func=AF.Identity, bias=bcol, scale=1.0),
                         reads=(tmp_s, s_modv_l[l]), writes=(ho_s,))
                    if b < NSB:
                        dma("pool", hpad[kc * 128:(kc + 1) * 128, pcol(c0):pcol(c0) + n], ho[:, 0:n], reads=(ho_s,), writes=(DS("hpad", b, kc),))
                    else:
                        for hf in range(2):
                            cc = c0 + hf * 256
                            dma("pool", hpad[kc * 128:(kc + 1) * 128, pcol(cc):pcol(cc) + 256], ho[:, hf * 256:(hf + 1) * 256], reads=(ho_s,),
                                writes=(DS("hpad", b, kc, hf),))
                norm_block(xsrc, xname, b, l, 1, emit_h)
            with ExitStack() as ph:
                PW = ph.enter_context(SBT(nc, "PW", [128, 16, 512], BF16))
                pw_s = Slot()
                dma("sp", PW[:], Wpl[o_].rearrange("(k p) c -> p k c", p=128), reads=(DS("Wpl", o_),), writes=(pw_s,))
                hp = Ring(nc, ph, "hp", [512 + 2 * PAD], F32, 3)
                pa = Ring(nc, ph, "pa", [512 + 2 * PAD], F32, 4)
                icr = Ring(nc, ph, "icr", [512], F32, 2)
                subs = [(b, b * 512, 512) for b in range(NSB)] + [(NSB, SS, 256), (NSB, SS + 256, 256)]
                for (b, c0, n) in subs:
                    ci = blocks[b][2]
                    mixT, mix_s = hTr.get()
                    for g in range(4):
                        w = POOL_W[g]
                        ic, ic_s = icr.get()
                        dma("sp", ic[:, 0:n], invcntB[:, g, c0:c0 + n], writes=(ic_s,))
                        for cc in range(4):
                            kc = g * 4 + cc
                            a, a_s = hp.get()
                            nb_lo = max(b - 1, 0)
                            rd = [DS("hpad", bb, kc) for bb in range(max(b - 1, 0), min(b + 1, NSB - 1) + 1)] if b < NSB else \
                                [DS("hpad", b, kc, 0), DS("hpad", b, kc, 1)]
                            p0 = pcol(c0) - PAD
                            L = n + 2 * PAD
                            dma("sp", a[:, 0:L], hpad[kc * 128:(kc + 1) * 128, p0:p0 + L], reads=rd, writes=(a_s,))
                            cur, cur_s, width = a, a_s, 1
                            eng_i = 0
                            while width < w:
                                nx, nx_s = pa.get()
                                ln = L - 2 * width + 1
                                S.op("dve", lambda e, cur=cur, nx=nx, width=width, ln=ln: e.tensor_tensor(
                                    out=nx[:, 0:ln], in0=cur[:, 0:ln], in1=cur[:, width:width + ln], op=ALU.add), reads=(cur_s,), writes=(nx_s,))
                                cur, cur_s = nx, nx_s
                                width *= 2
                            off = PAD - w // 2
                            if b < NSB and 0 < b < NSB - 1:
                                S.op("dve", lambda e, cur=cur, a=a, kc=kc, mixT=mixT, off=off, n=n, w=w: e.scalar_tensor_tensor(
                                    out=mixT[:, kc, 0:n], in0=cur[:, off:off + n], scalar=1.0 / w, in1=a[:, PAD:PAD + n], op0=ALU.mult, op1=ALU.subtract),
                                    reads=(cur_s, a_s), writes=(mix_s,))
                                continue
                            t1, t1_s = pa.get()
                            S.op("pool", lambda e, cur=cur, t1=t1, ic=ic, off=off, n=n: e.tensor_tensor(
                                out=t1[:, 0:n], in0=cur[:, off:off + n], in1=ic[:, 0:n], op=ALU.mult), reads=(cur_s, ic_s), writes=(t1_s,))
                            S.op("dve", lambda e, t1=t1, a=a, kc=kc, mixT=mixT, n=n: e.tensor_tensor(
                                out=mixT[:, kc, 0:n], in0=t1[:, 0:n], in1=a[:, PAD:PAD + n], op=ALU.subtract), reads=(t1_s, a_s), writes=(mix_s,))
                    for g in range(4):
                        for ncb in range(4):
                            dc = g * 4 + ncb
                            po, po_s = PS()

                            def mm(e, g=g, ncb=ncb, po=po, mixT=mixT, n=n):
                                last = None
                                for cc in range(4):
                                    last = e.matmul(po[:, 0:n], lhsT=PW[:, g * 4 + cc, ncb * 128:(ncb + 1) * 128], rhs=mixT[:, g * 4 + cc, 0:n],
                                                    start=(cc == 0), stop=(cc == 3))
                                return last
                            S.op("pe", mm, reads=(pw_s, mix_s), writes=(po_s,))
                            xr, xr_s = f32t.get()
                            dma("sp", xr[:, 0:n], xsrc[dc * 128:(dc + 1) * 128, c0:c0 + n], reads=(DS(xname, b, dc),), writes=(xr_s,))
                            yo, yo_s = f32t.get()
                            S.op("dve", lambda e, po=po, xr=xr, yo=yo, dc=dc, n=n, ci=ci: e.scalar_tensor_tensor(
                                out=yo[:, 0:n], in0=po[:, 0:n], scalar=mcol(l, 2, dc, ci), in1=xr[:, 0:n], op0=ALU.mult, op1=ALU.add),
                                reads=(po_s, xr_s, s_modv_l[l]), writes=(yo_s,))
                            dma("pool", xdst[dc * 128:(dc + 1) * 128, c0:c0 + n], yo[:, 0:n], reads=(yo_s,), writes=(DS(dname, b, dc),))
                S.barrier()

        def even_phase(l, xsrc, xname, xdst, dname):
            e_ = l // 2
            with ExitStack() as ph:
                qa = ph.enter_context(SBT(nc, "qa", [128, 6, 512], F32))
                qan = ph.enter_context(SBT(nc, "qan", [128, 6, 512], BF16))
                ckv = ph.enter_context(SBT(nc, "ckv", [128, 4, 512], F32))
                ckvn = ph.enter_context(SBT(nc, "ckvn", [128, 4, 512], BF16))
                kpesq = ph.enter_context(SBT(nc, "kpesq", [64, 512], F32))
                krb = ph.enter_context(SBT(nc, "krb", [64, 512], F32))
                rC = ph.enter_context(SBT(nc, "rC", [64, 512], F32))
                rSn = ph.enter_context(SBT(nc, "rSn", [64, 512], F32))
                Wk = ph.enter_context(SBT(nc, "Wk", [128, 4, 1024], BF16))
                Wv = ph.enter_context(SBT(nc, "Wv", [128, 4, 1024], BF16))
                vec = ph.enter_context(SBT(nc, "vec", [128, 32], F32))
                s_qa, s_qan, s_ckv, s_ckvn, s_kpesq, s_krb, s_rope, s_wkv, s_vec = [Slot() for _ in range(9)]
                dma("sp", vec[:, 0:6], qanw[e_], writes=(s_vec,))
                dma("sp", vec[:, 6:10], kvanw[e_], writes=(s_vec,))
                dma("sp", vec[:, 10:14], qnw[e_], writes=(s_vec,))
                dma("sp", vec[:, 14:18], knw[e_], writes=(s_vec,))
                wkv_v = Wkv[e_].rearrange("(k p) (h t c) -> p k h t c", p=128, h=NH, t=2)
                for kc in range(4):
                    dma("sp", Wk[:, kc, :].rearrange("p (h c) -> p h c", h=NH), wkv_v[:, kc, :, 0, :], reads=(DS("Wkv", e_),), writes=(s_wkv,))
                    dma("sp", Wv[:, kc, :].rearrange("p (h c) -> p h c", h=NH), wkv_v[:, kc, :, 1, :], reads=(DS("Wkv", e_),), writes=(s_wkv,))

                def kv_path(n, key0, kname, rope, c0):
                    for h in range(NH):
                        pk, pk_s = PS()

                        def mm(e, h=h, pk=pk):
                            last = None
                            for rc in range(4):
                                last = e.matmul(pk[:, 0:n], lhsT=Wk[:, rc, h * 128:(h + 1) * 128], rhs=ckvn[:, rc, 0:n], start=(rc == 0), stop=(rc == 3))
                            return last
                        S.op("pe", mm, reads=(s_wkv, s_ckvn), writes=(pk_s,))
                        sq, sq_s = f32t.get()
                        S.op("act", lambda e, pk=pk, sq=sq: e.activation(out=sq[:, 0:n], in_=pk[:, 0:n], func=AF.Square), reads=(pk_s,), writes=(sq_s,))
                        stp, stp_s = PS()

                        def mm2(e, sq=sq, stp=stp):
                            e.matmul(stp[:, 0:n], lhsT=ones_f[:], rhs=sq[:, 0:n], start=True, stop=False)
                            return e.matmul(stp[:, 0:n], lhsT=ones_f[0:64, :], rhs=kpesq[:, 0:n], start=False, stop=True)
                        S.op("pe", mm2, reads=(sq_s, s_kpesq, s_const), writes=(stp_s,))
                        rs, rs_s = rsr.get()
                        rstd_chain(stp, stp_s, n, 128, 1.0 / 192, rs, rs_s)
                        ko, ko_s = b16t.get()
                        S.op("dve", lambda e, pk=pk, rs=rs, ko=ko: e.scalar_tensor_tensor(out=ko[:, 0:n], in0=pk[:, 0:n], scalar=vec[:, 14:15], in1=rs[:, 0:n],
                                                                                   op0=ALU.mult, op1=ALU.mult), reads=(pk_s, rs_s, s_vec), writes=(ko_s,))
                        dma("pool", KnT[h, :, key0:key0 + n], ko[:, 0:n], reads=(ko_s,), writes=(DS("Kn", kname, h),))
                        kr, kr_s = b16t.get()
                        S.op("pool", lambda e, rs=rs, kr=kr: e.tensor_tensor(out=kr[0:64, 0:n], in0=krb[:, 0:n], in1=rs[0:64, 0:n], op=ALU.mult),
                             reads=(rs_s, s_krb), writes=(kr_s,))
                        dma("pool", KrT[h, :, key0:key0 + n], kr[0:64, 0:n], reads=(kr_s,), writes=(DS("Kr", kname, h),))
                    for i in range(n // 128):
                        for hf in range(2):
                            pv, pv_s = PS()

                            def mm3(e, i=i, hf=hf, pv=pv):
                                last = None
                                for rc in range(4):
                                    last = e.matmul(pv[:, 0:512], lhsT=ckvn[:, rc, i * 128:(i + 1) * 128], rhs=Wv[:, rc, hf * 512:(hf + 1) * 512],
                                                    start=(rc == 0), stop=(rc == 3))
                                return last
                            S.op("pe", mm3, reads=(s_wkv, s_ckvn), writes=(pv_s,))
                            vo, vo_s = b16t.get()
                            S.op("act", lambda e, pv=pv, vo=vo: e.activation(out=vo[:, 0:512], in_=pv[:, 0:512], func=AF.Identity), reads=(pv_s,), writes=(vo_s,))
                            dma("pool", Vs[key0 + i * 128:key0 + (i + 1) * 128, hf * 512:(hf + 1) * 512], vo[:, 0:512], reads=(vo_s,),
                                writes=(DS("V", kname, i, hf),))

                def rope_apply(src, src_s, srcsw, srcsw_s, wcol, wswcol, n, out, out_s, npart=64):
                    t1, t1_s = f32t.get()
                    S.op("dve", lambda e: e.scalar_tensor_tensor(out=t1[0:64, 0:n], in0=src[0:64, 0:n], scalar=wcol, in1=rC[:, 0:n], op0=ALU.mult, op1=ALU.mult),
                         reads=(src_s, s_rope, s_vec), writes=(t1_s,))
                    t2, t2_s = f32t.get()
                    S.op("dve", lambda e: e.scalar_tensor_tensor(out=t2[0:64, 0:n], in0=srcsw[0:64, 0:n], scalar=wswcol, in1=rSn[:, 0:n], op0=ALU.mult, op1=ALU.mult),
                         reads=(srcsw_s, s_rope, s_vec), writes=(t2_s,))
                    S.op("pool", lambda e: e.tensor_tensor(out=out[0:64, 0:n], in0=t1[0:64, 0:n], in1=t2[0:64, 0:n], op=ALU.add), reads=(t1_s, t2_s), writes=(out_s,))

                cst = ph.enter_context(SBT(nc, "cst", [128, 4, PAST], F32))
                ckp = ph.enter_context(SBT(nc, "ckp", [64, PAST], F32))
                s_cst, s_ckp = Slot(), Slot()
                dma("sp", cst[:], cckvT[e_].rearrange("(k p) t -> p k t", p=128), writes=(s_cst,))
                dma("sp", ckp[:], ckpeT[e_], writes=(s_ckp,))
                S.op("act", lambda e: e.activation(out=ckvn[:, :, 0:PAST], in_=cst[:], func=AF.Identity), reads=(s_cst,), writes=(s_ckvn,))
                S.op("act", lambda e: e.activation(out=kpesq[:, 0:PAST], in_=ckp[:], func=AF.Square), reads=(s_ckp,), writes=(s_kpesq,))
                S.op("dve", lambda e: e.tensor_scalar(out=krb[:, 0:PAST], in0=ckp[:], scalar1=vec[0:64, 15:16], scalar2=None, op0=ALU.mult),
                     reads=(s_ckp, s_vec), writes=(s_krb,))
                E1S = 9
                kv_path(PAST, 0, "ctx", False, 0)

                hT_next = norm_to_hT(xsrc, xname, 0, l, 1)
                for b in range(NB):
                    c0, n, ci = blocks[b]
                    is_s = (ci == 0)
                    hT, hT_s = hT_next
                    if is_s:
                        dma("sp", rC[:, 0:n], ropeC[:, c0:c0 + n], writes=(s_rope,))
                        dma("sp", rSn[:, 0:n], ropeS[:, c0:c0 + n], writes=(s_rope,))
                    qst, qst_s = ACC(0)
                    cst_ps, cst_ps_s = ACC(1)
                    for mg in range(14):
                        wt, wts = wA.get()
                        ncol = 256
                        wv = wt[:, 0:KC * 256].rearrange("p (k c) -> p k c", k=KC)
                        dma("sp", wv[:, :, 0:ncol], Win[e_, mg, :, :, 0:ncol], reads=(DS("Win", e_),), writes=(wts,))

                        def chain(ps, lo, m, wv=wv, hT=hT):
                            def f(e):
                                last = None
                                for kc in range(KC):
                                    last = e.matmul(ps[0:m, 0:n], lhsT=wv[:, kc, lo:lo + m], rhs=hT[:, kc, 0:n], start=(kc == 0), stop=(kc == KC - 1))
                                return last
                            return f
                        if mg < 8:
                            pa_, pa_s = PS()
                            pg_, pg_s = PS()
                            S.op("pe", chain(pa_, 0, 128), reads=(wts, hT_s), writes=(pa_s,))
                            S.op("pe", chain(pg_, 128, 128), reads=(wts, hT_s), writes=(pg_s,))
                            sg, sg_s = f32t.get()
                            S.op("act", lambda e, pg_=pg_, sg=sg: e.activation(out=sg[:, 0:n], in_=pg_[:, 0:n], func=AF.Sigmoid), reads=(pg_s,), writes=(sg_s,))
                            go, go_s = b16t.get()
                            S.op("dve", lambda e, pa_=pa_, sg=sg, go=go: e.tensor_tensor(out=go[:, 0:n], in0=pa_[:, 0:n], in1=sg[:, 0:n], op=ALU.mult),
                                 reads=(pa_s, sg_s), writes=(go_s,))
                            if is_s:
                                dma("pool", glu[mg * 128:(mg + 1) * 128, pcol(c0):pcol(c0) + n], go[:, 0:n], reads=(go_s,), writes=(DS("glu", b, mg),))
                            else:
                                for hf in range(2):
                                    cc = c0 + hf * 256
                                    dma("pool", glu[mg * 128:(mg + 1) * 128, pcol(cc):pcol(cc) + 256], go[:, hf * 256:(hf + 1) * 256], reads=(go_s,),
                                        writes=(DS("glu", b, mg, hf),))
                        elif mg < 13:
                            for j in range(2):
                                mc = (mg - 8) * 2 + j
                                pp, pp_s = PS()
                                S.op("pe", chain(pp, j * 128, 128), reads=(wts, hT_s), writes=(pp_s,))
                                if mc < 6:
                                    dst, dst_s, stp, stp_s, k, last = qa[:, mc, 0:n], s_qa, qst, qst_s, mc, 5
                                else:
                                    dst, dst_s, stp, stp_s, k, last = ckv[:, mc - 6, 0:n], s_ckv, cst_ps, cst_ps_s, mc - 6, 3
                                S.op("act", lambda e, pp=pp, dst=dst: e.activation(out=dst, in_=pp[:, 0:n], func=AF.Identity), reads=(pp_s,), writes=(dst_s,))
                                sq, sq_s = f32t.get()
                                S.op("act", lambda e, pp=pp, sq=sq: e.activation(out=sq[:, 0:n], in_=pp[:, 0:n], func=AF.Square), reads=(pp_s,), writes=(sq_s,))
                                S.op("pe", lambda e, sq=sq, stp=stp, k=k, last=last: e.matmul(stp[:, 0:n], lhsT=ones_f[:], rhs=sq[:, 0:n], start=(k == 0), stop=(k == last)),
                                     reads=(sq_s, s_const), writes=(stp_s,))
                        else:
                            pk1, pk1_s = PS()
                            pk2, pk2_s = PS()
                            S.op("pe", chain(pk1, 0, 128), reads=(wts, hT_s), writes=(pk1_s,))
                            S.op("pe", chain(pk2, 128, 128), reads=(wts, hT_s), writes=(pk2_s,))
                            ko, ko_s = f32t.get()
                            S.op("act", lambda e, pk1=pk1, ko=ko: e.activation(out=ko[:, 0:n], in_=pk1[:, 0:n], func=AF.Identity), reads=(pk1_s,), writes=(ko_s,))
                            ko2, ko2_s = f32t.get()
                            S.op("act", lambda e, pk2=pk2, ko2=ko2: e.activation(out=ko2[:, 0:n], in_=pk2[:, 0:n], func=AF.Identity), reads=(pk2_s,), writes=(ko2_s,))
                            S.op("act", lambda e, ko=ko: e.activation(out=kpesq[:, 0:n], in_=ko[0:64, 0:n], func=AF.Square), reads=(ko_s,), writes=(s_kpesq,))
                            if not is_s:
                                dma("pool", kpe_out[e_, :, :], ko[0:64, 0:n], reads=(ko_s,))
                                S.op("dve", lambda e, ko=ko: e.tensor_scalar(out=krb[:, 0:n], in0=ko[0:64, 0:n], scalar1=vec[0:64, 15:16], scalar2=None, op0=ALU.mult),
                                     reads=(ko_s, s_vec), writes=(s_krb,))
                            else:
                                rope_apply(ko, ko_s, ko2, ko2_s, vec[0:64, 15:16], vec[0:64, 16:17], n, krb, s_krb)
                    if E1S < 5:
                        continue
                    rs, rs_s = rsr.get()
                    rstd_chain(qst, qst_s, n, 128, 1.0 / 768, rs, rs_s)
                    for kc in range(6):
                        S.op("dve", lambda e, kc=kc, rs=rs: e.scalar_tensor_tensor(out=qan[:, kc, 0:n], in0=qa[:, kc, 0:n], scalar=vec[:, kc:kc + 1], in1=rs[:, 0:n],
                                                                                 op0=ALU.mult, op1=ALU.mult), reads=(s_qa, rs_s, s_vec), writes=(s_qan,))
                    rs2, rs2_s = rsr.get()
                    rstd_chain(cst_ps, cst_ps_s, n, 128, 1.0 / 512, rs2, rs2_s)
                    for kc in range(4):
                        cf, cf_s = f32t.get()
                        S.op("dve", lambda e, kc=kc, rs2=rs2, cf=cf: e.scalar_tensor_tensor(out=cf[:, 0:n], in0=ckv[:, kc, 0:n], scalar=vec[:, 6 + kc:7 + kc], in1=rs2[:, 0:n],
                                                                                         op0=ALU.mult, op1=ALU.mult), reads=(s_ckv, rs2_s, s_vec), writes=(cf_s,))
                        S.op("act", lambda e, kc=kc, cf=cf: e.activation(out=ckvn[:, kc, 0:n], in_=cf[:, 0:n], func=AF.Identity), reads=(cf_s,), writes=(s_ckvn,))
                        if not is_s:
                            dma("pool", ckv_out[e_, kc * 128:(kc + 1) * 128, :], cf[:, 0:n], reads=(cf_s,))
                    if b + 1 < NB:
                        hT_next = norm_to_hT(xsrc, xname, b + 1, l, 1)
                    for h in range(NH if E1S >= 6 else 0):
                        wt, wts = wA.get()
                        wq = wt[:, 0:6 * 384].rearrange("p (k c) -> p k c", k=6)
                        dma("sp", wq, Wqb[e_, h], reads=(DS("Wqb", e_),), writes=(wts,))

                        def qchain(ps, lo, m, wq=wq):
                            def f(e):
                                last = None
                                for kc in range(6):
                                    last = e.matmul(ps[0:m, 0:n], lhsT=wq[:, kc, lo:lo + m], rhs=qan[:, kc, 0:n], start=(kc == 0), stop=(kc == 5))
                                return last
                            return f
                        pn, pn_s = PS()
                        pr, pr_s = PS()
                        S.op("pe", qchain(pn, 0, 128), reads=(wts, s_qan), writes=(pn_s,))
                        S.op("pe", qchain(pr, 128, 128), reads=(wts, s_qan), writes=(pr_s,))
                        if is_s:
                            prs, prs_s = PS()
                            S.op("pe", qchain(prs, 256, 128), reads=(wts, s_qan), writes=(prs_s,))
                        sq1, sq1_s = f32t.get()
                        S.op("act", lambda e, pn=pn, sq1=sq1: e.activation(out=sq1[:, 0:n], in_=pn[:, 0:n], func=AF.Square), reads=(pn_s,), writes=(sq1_s,))
                        prc, prc_s = f32t.get()
                        S.op("act", lambda e, pr=pr, prc=prc: e.activation(out=prc[:, 0:n], in_=pr[:, 0:n], func=AF.Identity), reads=(pr_s,), writes=(prc_s,))
                        pr, pr_s = prc, prc_s
                        if is_s:
                            prsc, prsc_s = f32t.get()
                            S.op("act", lambda e, prs=prs, prsc=prsc: e.activation(out=prsc[:, 0:n], in_=prs[:, 0:n], func=AF.Identity), reads=(prs_s,), writes=(prsc_s,))
                            prs, prs_s = prsc, prsc_s
                        sq2, sq2_s = f32t.get()
                        S.op("act", lambda e, pr=pr, sq2=sq2: e.activation(out=sq2[0:64, 0:n], in_=pr[0:64, 0:n], func=AF.Square), reads=(pr_s,), writes=(sq2_s,))
                        stp, stp_s = PS()

                        def mmq(e, sq1=sq1, sq2=sq2, stp=stp):
                            e.matmul(stp[:, 0:n], lhsT=ones_f[:], rhs=sq1[:, 0:n], start=True, stop=False)
                            return e.matmul(stp[:, 0:n], lhsT=ones_f[0:64, :], rhs=sq2[0:64, 0:n], start=False, stop=True)
                        S.op("pe", mmq, reads=(sq1_s, sq2_s, s_const), writes=(stp_s,))
                        rq, rq_s = rsr.get()
                        rstd_chain(stp, stp_s, n, 128, 1.0 / 192, rq, rq_s)
                        qo, qo_s = b16t.get()
                        S.op("dve", lambda e, pn=pn, rq=rq, qo=qo: e.scalar_tensor_tensor(out=qo[:, 0:n], in0=pn[:, 0:n], scalar=vec[:, 10:11], in1=rq[:, 0:n],
                                                                                       op0=ALU.mult, op1=ALU.mult), reads=(pn_s, rq_s, s_vec), writes=(qo_s,))
                        dma("pool", QnT[h, :, c0:c0 + n], qo[:, 0:n], reads=(qo_s,), writes=(DS("Qn", b, h),))
                        qr, qr_s = b16t.get()
                        if is_s:
                            rt, rt_s = f32t.get()
                            rope_apply(pr, pr_s, prs, prs_s, vec[0:64, 11:12], vec[0:64, 12:13], n, rt, rt_s)
                            S.op("dve", lambda e, rt=rt, rq=rq, qr=qr: e.tensor_tensor(out=qr[0:64, 0:n], in0=rt[0:64, 0:n], in1=rq[0:64, 0:n], op=ALU.mult),
                                 reads=(rt_s, rq_s), writes=(qr_s,))
                        else:
                            S.op("dve", lambda e, pr=pr, rq=rq, qr=qr: e.scalar_tensor_tensor(out=qr[0:64, 0:n], in0=pr[0:64, 0:n], scalar=vec[0:64, 11:12], in1=rq[0:64, 0:n],
                                                                                           op0=ALU.mult, op1=ALU.mult), reads=(pr_s, rq_s, s_vec), writes=(qr_s,))
                        dma("pool", QrT[h, :, c0:c0 + n], qr[0:64, 0:n], reads=(qr_s,), writes=(DS("Qr", b, h),))
                    if E1S >= 7:
                        kv_path(n, PAST + c0, b, is_s, c0)
                S.barrier()

            if stop == 4:
                return
            with ExitStack() as ph:
                Dg = ph.enter_context(SBT(nc, "Dg", [128, 8, CONV_K, 128], BF16))
                cw = ph.enter_context(SBT(nc, "cw", [128, 8, CONV_K], F32))
                cv = ph.enter_context(SBT(nc, "cv", [128, 3, 8], F32))
                s_dg, s_dg2, s_cw = Slot(), Slot(), Slot()
                dma("sp", cw[:], cdw[e_], writes=(s_cw,))
                dma("sp", cv[:, 0, :], cdb[e_], writes=(s_cw,))
                dma("sp", cv[:, 1, :], clw[e_], writes=(s_cw,))
                dma("sp", cv[:, 2, :], clb[e_], writes=(s_cw,))
                for c in range(8):
                    for k in range(CONV_K):
                        S.op("dve" if (k % 2 == 0) else "pool", lambda e, c=c, k=k: e.tensor_scalar(out=Dg[:, c, k, :], in0=ident[:], scalar1=cw[:, c, k:k + 1], scalar2=None,
                                                                                                op0=ALU.mult), reads=(s_cw, s_const), writes=((s_dg,) if (k % 2 == 0) else (s_dg2,)))
                gp = Ring(nc, ph, "gp", [512 + 2 * PAD], BF16, 3)
                cr = xs
                cr_s = xs_slots
                subs = [(b, b * 512, 512) for b in range(NSB)] + [(NSB, SS, 256), (NSB, SS + 256, 256)]
                for (b, c0, n) in subs:
                    s1, s1_s = ACC(0)
                    s2, s2_s = ACC(1)
                    for c in range(8):
                        g_, g_s = gp.get()
                        rd = [DS("glu", bb, c) for bb in range(max(b - 1, 0), min(b + 1, NSB - 1) + 1)] if b < NSB else \
                            [DS("glu", b, c, 0), DS("glu", b, c, 1)]
                        p0 = pcol(c0) - PAD
                        L = n + 2 * PAD
                        dma("sp", g_[:, 0:L], glu[c * 128:(c + 1) * 128, p0:p0 + L], reads=rd, writes=(g_s,))
                        pc, pc_s = PS()

                        def mm(e, c=c, g_=g_, pc=pc, n=n):
                            last = None
                            for k in range(CONV_K):
                                o = PAD - 15 + k
                                last = e.matmul(pc[:, 0:n], lhsT=Dg[:, c, k, :], rhs=g_[:, o:o + n], start=(k == 0), stop=(k == CONV_K - 1))
                            return last
                        S.op("pe", mm, reads=(g_s, s_dg, s_dg2), writes=(pc_s,))
                        S.op("act", lambda e, c=c, pc=pc, n=n: e.activation(out=cr[:, c, 0:n], in_=pc[:, 0:n], func=AF.Identity, bias=cv[:, 0, c:c + 1], scale=1.0),
                             reads=(pc_s, s_cw), writes=(cr_s[c],))
                        sq, sq_s = f32t.get()
                        S.op("act", lambda e, c=c, sq=sq, n=n: e.activation(out=sq[:, 0:n], in_=cr[:, c, 0:n], func=AF.Square), reads=(cr_s[c],), writes=(sq_s,))

                        def mms(e, c=c, sq=sq, n=n, s1=s1, s2=s2):
                            e.matmul(s1[:, 0:n], lhsT=ones_f[:], rhs=cr[:, c, 0:n], start=(c == 0), stop=(c == 7))
                            return e.matmul(s2[:, 0:n], lhsT=ones_f[:], rhs=sq[:, 0:n], start=(c == 0), stop=(c == 7))
                        S.op("pe", mms, reads=(cr_s[c], sq_s, s_const), writes=(s1_s, s2_s))
                    mu, mu_s = rsr.get()
                    S.op("dve", lambda e, mu=mu, s1=s1, n=n: e.tensor_scalar(out=mu[:, 0:n], in0=s1[:, 0:n], scalar1=1.0 / 1024, scalar2=None, op0=ALU.mult),
                         reads=(s1_s,), writes=(mu_s,))
                    m2, m2_s = f32t.get()
                    S.op("dve", lambda e, mu=mu, m2=m2, n=n: e.tensor_tensor(out=m2[:, 0:n], in0=mu[:, 0:n], in1=mu[:, 0:n], op=ALU.mult), reads=(mu_s,), writes=(m2_s,))
                    var, var_s = rsr.get()
                    S.op("dve", lambda e, var=var, s2=s2, m2=m2, n=n: e.scalar_tensor_tensor(out=var[:, 0:n], in0=s2[:, 0:n], scalar=1.0 / 1024, in1=m2[:, 0:n],
                                                                                           op0=ALU.mult, op1=ALU.subtract), reads=(s2_s, m2_s), writes=(var_s,))
                    S.op("dve", lambda e, var=var, n=n: e.tensor_scalar(out=var[:, 0:n], in0=var[:, 0:n], scalar1=EPS, scalar2=None, op0=ALU.add), reads=(var_s,), writes=(var_s,))
                    S.op("act", lambda e, var=var, n=n: e.activation(out=var[:, 0:n], in_=var[:, 0:n], func=AF.Sqrt), reads=(var_s,), writes=(var_s,))
                    S.op("dve", lambda e, var=var, n=n: e.reciprocal(out=var[:, 0:n], in_=var[:, 0:n]), reads=(var_s,), writes=(var_s,))
                    for c in range(8):
                        t1, t1_s = f32t.get()
                        S.op("pool", lambda e, c=c, t1=t1, mu=mu, n=n: e.tensor_tensor(out=t1[:, 0:n], in0=cr[:, c, 0:n], in1=mu[:, 0:n], op=ALU.subtract),
                             reads=(cr_s[c], mu_s), writes=(t1_s,))
                        S.op("dve", lambda e, t1=t1, var=var, n=n: e.tensor_tensor(out=t1[:, 0:n], in0=t1[:, 0:n], in1=var[:, 0:n], op=ALU.mult),
                             reads=(t1_s, var_s), writes=(t1_s,))
                        co, co_s = b16t.get()
                        S.op("act", lambda e, c=c, t1=t1, co=co, n=n: e.activation(out=co[:, 0:n], in_=t1[:, 0:n], func=AF.Silu, bias=cv[:, 2, c:c + 1], scale=cv[:, 1, c:c + 1]),
                             reads=(t1_s, s_cw), writes=(co_s,))
                        dma("pool", catT[c * 128:(c + 1) * 128, c0:c0 + n], co[:, 0:n], reads=(co_s,), writes=(DS("cat", c0, c),))
                S.barrier()

            if stop == 5:
                return
            with ExitStack() as ph:
                nkmax = PAST + SS
                Knr = Ring(nc, ph, "Knr", [nkmax], BF16, 2)
                Krr = Ring(nc, ph, "Krr", [nkmax], BF16, 2)
                Vr = Ring(nc, ph, "Vr", [nkmax // 128, 128], BF16, 2)
                qnr = Ring(nc, ph, "qnr", [512], BF16, 2)
                qrr = Ring(nc, ph, "qrr", [512], BF16, 2)
                ptr = Ring(nc, ph, "ptr", [512], BF16, 4)
                stgA = Ring(nc, ph, "stgA", [2048], BF16, 2)
                bg.append(run_batched(chunks_ffn(l, stgA), 2))
                sc = 1.0 / np.sqrt(192.0)
                groups = [(0, PAST + SS, [(b * 512, 512, b) for b in range(NSB)]),
                          (PAST + SS, 256, [(SS, 256, NSB)]),
                          (PAST + SS + 256, 256, [(SS + 256, 256, NSB)])]
                for (key0, nk, qbs) in groups:
                    nkc = nk // 128
                    if key0 == 0:
                        krd = lambda nm, h: [DS(nm, "ctx", h)] + [DS(nm, b, h) for b in range(NSB)]
                        vrd = [DS("V", "ctx", i, hf) for i in range(2) for hf in range(2)] + [DS("V", b, i, hf) for b in range(NSB) for i in range(4) for hf in range(2)]
                    else:
                        krd = lambda nm, h: [DS(nm, NSB, h)]
                        vrd = [DS("V", NSB, i, hf) for i in range(4) for hf in range(2)]
                    for h in range(NH):
                        Kn, Kn_s = Knr.get()
                        Kr, Kr_s = Krr.get()
                        Vh, Vh_s = Vr.get()
                        dma("sp", Kn[:, 0:nk], KnT[h, :, key0:key0 + nk], reads=krd("Kn", h), writes=(Kn_s,))
                        dma("sp", Kr[0:64, 0:nk], KrT[h, :, key0:key0 + nk], reads=krd("Kr", h), writes=(Kr_s,))
                        for v0 in range(0, nkc, 8):
                            v1 = min(v0 + 8, nkc)
                            dma("sp", Vh[:, v0:v1, :], Vs[key0 + v0 * 128:key0 + v1 * 128, h * 128:(h + 1) * 128].rearrange("(c p) d -> p c d", p=128),
                                reads=vrd, writes=(Vh_s,))
                        for (c0, n, b) in qbs:
                            qn, qn_s = qnr.get()
                            qr, qr_s = qrr.get()
                            dma("sp", qn[:, 0:n], QnT[h, :, c0:c0 + n], reads=(DS("Qn", b, h),), writes=(qn_s,))
                            dma("sp", qr[0:64, 0:n], QrT[h, :, c0:c0 + n], reads=(DS("Qr", b, h),), writes=(qr_s,))
                            po, po_s = ACC(0)
                            pz, pz_s = ACC(1)
                            pend = []
                            for kc in range(nkc + 2):
                                if kc < nkc:
                                    pss, pss_s = PS()

                                    def mmS(e, kc=kc, pss=pss, Kn=Kn, Kr=Kr, qn=qn, qr=qr, n=n):
                                        e.matmul(pss[:, 0:n], lhsT=Kn[:, kc * 128:(kc + 1) * 128], rhs=qn[:, 0:n], start=True, stop=False)
                                        return e.matmul(pss[:, 0:n], lhsT=Kr[0:64, kc * 128:(kc + 1) * 128], rhs=qr[0:64, 0:n], start=False, stop=True)
                                    S.op("pe", mmS, reads=(Kn_s, Kr_s, qn_s, qr_s), writes=(pss_s,))
                                    pt, pt_s = ptr.get()
                                    S.op("act", lambda e, pss=pss, pt=pt, n=n: e.activation(out=pt[:, 0:n], in_=pss[:, 0:n], func=AF.Exp, scale=float(sc)),
                                         reads=(pss_s,), writes=(pt_s,))
                                    pend.append((kc, pt, pt_s))
                                if kc >= 2:
                                    k2, pt2, pt2_s = pend.pop(0)

                                    def mmP(e, k2=k2, pt2=pt2, Vh=Vh, po=po, pz=pz, n=n, nkc=nkc):
                                        e.matmul(po[:, 0:n], lhsT=Vh[:, k2, :], rhs=pt2[:, 0:n], start=(k2 == 0), stop=(k2 == nkc - 1))
                                        return e.matmul(pz[:, 0:n], lhsT=ones_b[:], rhs=pt2[:, 0:n], start=(k2 == 0), stop=(k2 == nkc - 1))
                                    S.op("pe", mmP, reads=(Vh_s, pt2_s, s_const), writes=(po_s, pz_s))
                            rz, rz_s = f32t.get()
                            S.op("dve", lambda e, rz=rz, pz=pz, n=n: e.reciprocal(out=rz[:, 0:n], in_=pz[:, 0:n]), reads=(pz_s,), writes=(rz_s,))
                            ao, ao_s = b16t.get()
                            S.op("dve", lambda e, ao=ao, po=po, rz=rz, n=n: e.tensor_tensor(out=ao[:, 0:n], in0=po[:, 0:n], in1=rz[:, 0:n], op=ALU.mult),
                                 reads=(po_s, rz_s), writes=(ao_s,))
                            dma("pool", catT[1024 + h * 128:1024 + (h + 1) * 128, c0:c0 + n], ao[:, 0:n], reads=(ao_s,), writes=(DS("cat", c0, 8 + h),))
                            bg_step()
                bg_drain()
                S.barrier()

            if stop == 6:
                return
            for b in range(NB):
                c0, n, ci = blocks[b]
                cT, cT_s = hTr.get()
                if b < NSB:
                    rd = [DS("cat", c0, r) for r in range(16)]
                else:
                    rd = [DS("cat", c0 + hf * 256, r) for r in range(16) for hf in range(2)]
                dma("sp", cT[:, :, 0:n], catT[:, c0:c0 + n].rearrange("(k p) t -> p k t", p=128), reads=rd, writes=(cT_s,))
                for dg in range(8):
                    wt, wts = wA.get()
                    wv = wt[:, 0:KC * 256].rearrange("p (k c) -> p k c", k=KC)
                    dma("sp", wv, Wo[e_, dg], reads=(DS("Wo", e_),), writes=(wts,))
                    for j in range(2):
                        dc = dg * 2 + j
                        po, po_s = PS()

                        def mm(e, wv=wv, j=j, po=po, cT=cT, n=n):
                            last = None
                            for kc in range(KC):
                                last = e.matmul(po[:, 0:n], lhsT=wv[:, kc, j * 128:(j + 1) * 128], rhs=cT[:, kc, 0:n], start=(kc == 0), stop=(kc == KC - 1))
                            return last
                        S.op("pe", mm, reads=(wts, cT_s), writes=(po_s,))
                        residual_store(po, po_s, dc, mcol(l, 2, dc, ci), xsrc, xname, xdst, dname, b, l)
            S.barrier()

        cur, cname = xT, "x_in"
        for l in range(DEPTH if stop >= 4 else 0):
            if l % 2 == 0:
                even_phase(l, cur, cname, xA, "xA%d" % l)
            else:
                pool_phase(l, cur, cname, xA, "xA%d" % l)
            last = (l == DEPTH - 1)
            dst, dn = (yT, "y") if last else (xB, "xB%d" % l)
            if stop >= 8:
                ffn_phase(l, xA, "xA%d" % l, dst, dn)
            cur, cname = dst, dn
        S.emit()
    return nc


def _colvec(v, nchunk):
    return np.ascontiguousarray(np.asarray(v, np.float32).reshape(nchunk, 128).T)


def _rope_tables(rows):
    row = np.broadcast_to(np.arange(rows, dtype=np.float32)[:, None], (rows, GRID_W)).reshape(-1)
    col = np.broadcast_to(np.arange(GRID_W, dtype=np.float32)[None, :], (rows, GRID_W)).reshape(-1)
    n_freq = 16
    inv_freq = (1.0 / (np.float32(10000.0) ** (np.arange(n_freq, dtype=np.float32) / np.float32(n_freq)))).astype(np.float32)
    ang = np.concatenate([row[:, None] * inv_freq, col[:, None] * inv_freq], axis=-1).astype(np.float32)
    c, s = np.cos(ang).astype(np.float32), np.sin(ang).astype(np.float32)
    C = np.repeat(c.T, 2, axis=0)
    Sg = np.repeat(s.T, 2, axis=0).copy()
    Sg[0::2] *= -1.0
    return np.ascontiguousarray(C), np.ascontiguousarray(Sg)


def _invcnt(s):
    t = np.arange(s)
    out = np.zeros((4, s), np.float32)
    for gi, w in enumerate(POOL_W):
        lo = np.clip(t - w // 2, 0, s)
        hi = np.clip(t + w // 2, 0, s)
        out[gi] = 1.0 / (hi - lo).astype(np.float32)
    return out


def _pairswap(v):
    v = np.asarray(v)
    o = v.copy()
    o[0::2] = v[1::2]
    o[1::2] = v[0::2]
    return o


def make_in_maps(inp, NSB, DEPTH, ncores):
    SS = NSB * 512
    NE = (DEPTH + 1) // 2
    NO = DEPTH // 2
    f = lambda a: np.ascontiguousarray(np.asarray(a, np.float32))
    shared = {}
    shared["n1w"] = np.stack([_colvec(inp["norm1_w"][l], KC) for l in range(DEPTH)])
    shared["n2w"] = np.stack([_colvec(inp["norm2_w"][l], KC) for l in range(DEPTH)])
    bm = np.stack([_colvec(inp["b_mod"][l], 96) for l in range(DEPTH)])
    shared["bmodX"] = np.ascontiguousarray(np.repeat(bm[:, :, :, None], 2, axis=3))
    shared["w_mod"] = f(inp["w_mod"][:DEPTH])
    shared["w_in"] = f(inp["w_in"][:NE])
    cdw = np.asarray(inp["conv_dw_w"], np.float32)[:NE]
    shared["cdw"] = np.ascontiguousarray(cdw.reshape(NE, CONV_K, 8, 128).transpose(0, 3, 2, 1))
    for nm, key in (("cdb", "conv_dw_b"), ("clw", "conv_ln_w"), ("clb", "conv_ln_b")):
        shared[nm] = np.stack([_colvec(inp[key][e], 8) for e in range(NE)])
    shared["qanw"] = np.stack([_colvec(inp["q_a_norm_w"][e], 6) for e in range(NE)])
    shared["kvanw"] = np.stack([_colvec(inp["kv_a_norm_w"][e], 4) for e in range(NE)])
    shared["w_qb"] = f(inp["w_q_b"][:NE])
    shared["w_kvb"] = f(inp["w_kv_b"][:NE])

    def headvec(w):
        o = np.zeros((128, 4), np.float32)
        w = np.asarray(w, np.float32)
        o[:, 0] = w[:128]
        o[:64, 1] = w[128:192]
        o[:64, 2] = _pairswap(w[128:192])
        return o
    shared["qnw"] = np.stack([headvec(inp["q_norm_w"][e]) for e in range(NE)])
    shared["knw"] = np.stack([headvec(inp["k_norm_w"][e]) for e in range(NE)])
    shared["w_out"] = f(inp["w_out"][:NE])
    if NO > 0:
        shared["pool_w"] = f(np.asarray(inp["pool_w"])[:NO].reshape(NO, 4 * 512, 512))
        shared["pscale"] = np.stack([_colvec(inp["pool_scale"][o], KC) for o in range(NO)])
    else:
        shared["pool_w"] = np.zeros((1, 2048, 512), np.float32)
        shared["pscale"] = np.zeros((1, 128, KC), np.float32)
    shared["w_gate"] = f(inp["ffn_w_gate"][:DEPTH])
    shared["w_up"] = f(inp["ffn_w_up"][:DEPTH])
    shared["w_down"] = f(inp["ffn_w_down"][:DEPTH])
    C, Sg = _rope_tables(SS // GRID_W)
    shared["ropeC"], shared["ropeS"] = C, Sg
    ic_ = np.concatenate([_invcnt(SS), _invcnt(256), _invcnt(256)], axis=1)
    shared["invcntB"] = np.ascontiguousarray(np.broadcast_to(ic_[None], (128,) + ic_.shape))
    shared["ident_in"] = np.eye(128, dtype=np.float32)
    xp = np.asarray(inp["x_prompt"], np.float32)
    xs_ = np.asarray(inp["x_sample"], np.float32)
    cc = np.asarray(inp["c"], np.float32)
    cctx = np.asarray(inp["c_ctx"], np.float32)
    cckv = np.asarray(inp["cache_ckv"], np.float32)
    ckpe = np.asarray(inp["cache_kpe"], np.float32)
    maps = []
    for i in range(ncores):
        m = dict(shared)
        xt = np.concatenate([xs_[i].T, xp[2 * i].T, xp[2 * i + 1].T], axis=1)
        m["xT"] = np.ascontiguousarray(xt)
        cnd = np.stack([cc[i], cctx], axis=1)
        m["condT"] = np.ascontiguousarray(cnd.reshape(KC, 128, 2).transpose(1, 0, 2))
        m["cckvT"] = np.ascontiguousarray(cckv[i, :NE].transpose(0, 2, 1))
        m["ckpeT"] = np.ascontiguousarray(ckpe[i, :NE].transpose(0, 2, 1))
        maps.append(m)
    return maps


def assemble(results, NSB, DEPTH, ncores):
    SS = NSB * 512
    NE = (DEPTH + 1) // 2
    y_s = np.zeros((ncores, SS, D), np.float32)
    y_p = np.zeros((2 * ncores, 256, D), np.float32)
    n_ckv = np.zeros((2 * ncores, NE, 256, 512), np.float32)
    n_kpe = np.zeros((2 * ncores, NE, 256, 64), np.float32)
    for i, r in enumerate(results):
        yT = np.asarray(r["yT"])
        y_s[i] = yT[:, :SS].T
        y_p[2 * i] = yT[:, SS:SS + 256].T
        y_p[2 * i + 1] = yT[:, SS + 256:SS + 512].T
        ck = np.asarray(r["ckv_out"])
        kp = np.asarray(r["kpe_out"])
        for j in range(2):
            n_ckv[2 * i + j] = ck[:, :, j * 256:(j + 1) * 256].transpose(0, 2, 1)
            n_kpe[2 * i + j] = kp[:, :, j * 256:(j + 1) * 256].transpose(0, 2, 1)
    return y_p, y_s, n_ckv, n_kpe


def run(inp, NSB=8, DEPTH=4, ncores=8, dbg=False):
    nc = build(NSB, DEPTH, dbg)
    maps = make_in_maps(inp, NSB, DEPTH, ncores)
    res = run_bass_kernel_spmd(nc, maps, core_ids=list(range(ncores)))
    return res


def kernel(**inputs):
    res = run(inputs, 8, 4, 8)
    return assemble(res.results, 8, 4, 8)
```

```python
import numpy as np
from contextlib import ExitStack
import concourse.bass as bass
import concourse.mybir as mybir
from concourse.bass_utils import run_bass_kernel_spmd

F32 = mybir.dt.float32
BF16 = mybir.dt.bfloat16
AF = mybir.ActivationFunctionType
ALU = mybir.AluOpType

D = 2048
KC = 16
DFF = 5632
FC = 44
CONV_K = 31
NH = 8
EPS = 1e-6
POOL_W = (2, 4, 8, 16)
GRID_W = 64
PAST = 256
PAD = 16


class Tok:
    __slots__ = ("sem", "val", "eng")

    def __init__(self, sem, val, eng):
        self.sem, self.val, self.eng = sem, val, eng


class Slot:
    __slots__ = ("w", "r")

    def __init__(self):
        self.w = None
        self.r = {}


class Sched:
    COMPUTE = ("pe", "act", "dve", "pool")
    NDMA = 12

    def __init__(self, nc, stack):
        self.nc = nc
        self.streams = {e: [] for e in ("pe", "act", "dve", "pool", "sp")}
        self.sem = {}
        self.cnt = {}
        for e in self.COMPUTE:
            self.sem[e] = stack.enter_context(nc.semaphore("s_" + e))
            self.cnt[e] = 0
        self.dsem = {}
        self.dcnt = {}
        self.dptr = {}
        for q in ("sp", "pool"):
            self.dsem[q] = [stack.enter_context(nc.semaphore("d_%s%d" % (q, i))) for i in range(self.NDMA)]
            self.dcnt[q] = [0] * self.NDMA
            self.dptr[q] = 0
        self.known = {e: {} for e in self.streams}
        self.nops = 0

    def _wait(self, eng, tok):
        k = self.known[eng]
        key = id(tok.sem)
        if k.get(key, 0) >= tok.val:
            return
        k[key] = tok.val
        sem, val = tok.sem, tok.val
        self.streams[eng].append(lambda e: e.wait_ge(sem, val))

    def op(self, eng, fn, reads=(), writes=(), dma=False):
        deps = []
        for s in reads:
            if s.w is not None:
                deps.append((s.w, True))
        for s in writes:
            if s.w is not None:
                deps.append((s.w, False))
            for t in s.r.values():
                deps.append((t, False))
        for t, raw in deps:
            if (not dma) and t.eng == eng:
                if eng == "pe" or not raw:
                    continue
            self._wait(eng, t)
        if dma:
            q = eng
            i = self.dptr[q]
            self.dptr[q] = (i + 1) % self.NDMA
            sem = self.dsem[q][i]
            if self.dcnt[q][i] > 0:
                self._wait(eng, Tok(sem, self.dcnt[q][i], "dma_" + q))
            self.dcnt[q][i] += 16
            tok = Tok(sem, self.dcnt[q][i], "dma_" + q)
            self.streams[eng].append(lambda e: fn(e).then_inc(sem, 16))
        else:
            self.cnt[eng] += 1
            sem = self.sem[eng]
            tok = Tok(sem, self.cnt[eng], eng)
            self.streams[eng].append(lambda e: fn(e).then_inc(sem, 1))
        for s in writes:
            s.w = tok
            s.r = {}
        for s in reads:
            s.r[id(tok.sem)] = tok
        self.nops += 1
        return tok

    def barrier(self):
        toks = []
        for e in self.COMPUTE:
            if self.cnt[e] > 0:
                toks.append(Tok(self.sem[e], self.cnt[e], e))
        for q in ("sp", "pool"):
            for i in range(self.NDMA):
                if self.dcnt[q][i] > 0:
                    toks.append(Tok(self.dsem[q][i], self.dcnt[q][i], "dma_" + q))
        for e in self.streams:
            for t in toks:
                self._wait(e, t)

    def emit(self):
        nc = self.nc
        self.barrier()
        with nc.Block() as block:
            @block.sync
            def _(eng):
                for f in self.streams["sp"]:
                    f(eng)

            @block.tensor
            def _(eng):
                for f in self.streams["pe"]:
                    f(eng)

            @block.scalar
            def _(eng):
                for f in self.streams["act"]:
                    f(eng)

            @block.vector
            def _(eng):
                for f in self.streams["dve"]:
                    f(eng)

            @block.gpsimd
            def _(eng):
                for f in self.streams["pool"]:
                    f(eng)


_UID = [0]


def SBT(nc, name, shape, dtype):
    _UID[0] += 1
    return nc.sbuf_tensor("%s_u%d" % (name, _UID[0]), shape, dtype)


class Ring:
    def __init__(self, nc, st, name, shape, dtype, n):
        self.t = st.enter_context(SBT(nc, name, [128, n] + list(shape), dtype))
        self.slots = [Slot() for _ in range(n)]
        self.i = 0
        self.n = n

    def get(self):
        i = self.i
        self.i = (i + 1) % self.n
        return self.t[:, i], self.slots[i]


def build(NSB=8, DEPTH=4, dbg=False, stop=99):
    SS = NSB * 512
    T = SS + 512
    TK = PAST + T
    TP = T + 6 * PAD
    NE = (DEPTH + 1) // 2
    NO = DEPTH // 2
    nc = bass.Bass("TRN2", target_bir_lowering=False)

    def din(name, shape, dt=F32):
        return nc.dram_tensor(name, list(shape), dt, kind="ExternalInput").ap()

    def dout(name, shape, dt=F32):
        return nc.dram_tensor(name, list(shape), dt, kind="ExternalOutput").ap()

    def dscr(name, shape, dt):
        return nc.dram_tensor(name, list(shape), dt, kind="ExternalOutput" if dbg else "Internal").ap()

    xT = din("xT", [D, T])
    condT = din("condT", [128, KC, 2])
    cckvT = din("cckvT", [NE, 512, PAST])
    ckpeT = din("ckpeT", [NE, 64, PAST])
    n1w = din("n1w", [DEPTH, 128, KC])
    n2w = din("n2w", [DEPTH, 128, KC])
    bmodX = din("bmodX", [DEPTH, 128, 96, 2])
    w_mod = din("w_mod", [DEPTH, D, 6 * D])
    w_in = din("w_in", [NE, D, 3392])
    cdw = din("cdw", [NE, 128, 8, CONV_K])
    cdb = din("cdb", [NE, 128, 8])
    clw = din("clw", [NE, 128, 8])
    clb = din("clb", [NE, 128, 8])
    qanw = din("qanw", [NE, 128, 6])
    w_qb = din("w_qb", [NE, 768, 1536])
    kvanw = din("kvanw", [NE, 128, 4])
    w_kvb = din("w_kvb", [NE, 512, 2048])
    qnw = din("qnw", [NE, 128, 4])
    knw = din("knw", [NE, 128, 4])
    w_out = din("w_out", [NE, D, D])
    pool_w = din("pool_w", [max(NO, 1), 4 * 512, 512])
    pscale = din("pscale", [max(NO, 1), 128, KC])
    w_gate = din("w_gate", [DEPTH, D, DFF])
    w_up = din("w_up", [DEPTH, D, DFF])
    w_down = din("w_down", [DEPTH, DFF, D])
    ropeC = din("ropeC", [64, SS])
    ropeS = din("ropeS", [64, SS])
    invcntB = din("invcntB", [128, 4, T])
    ident_in = din("ident_in", [128, 128])
    yT = dout("yT", [D, T])
    ckv_out = dout("ckv_out", [NE, 512, 512])
    kpe_out = dout("kpe_out", [NE, 64, 512])
    xA = dscr("xA", [D, T], F32)
    xB = dscr("xB", [D, T], F32)
    hpad = dscr("hpad", [D, TP], F32)
    glu = dscr("glu", [1024, TP], BF16)
    catT = dscr("catT", [D, T], BF16)
    QnT = dscr("QnT", [NH, 128, T], BF16)
    QrT = dscr("QrT", [NH, 64, T], BF16)
    KnT = dscr("KnT", [NH, 128, TK], BF16)
    KrT = dscr("KrT", [NH, 64, TK], BF16)
    Vs = dscr("Vs", [TK, 1024], BF16)
    Wgu = dscr("Wgu", [DEPTH, 22, 128, 2, KC, 256], BF16)
    Wdn = dscr("Wdn", [DEPTH, 16, 128, FC, 128], BF16)
    Win = dscr("Win", [NE, 14, 128, KC, 256], BF16)
    Wqb = dscr("Wqb", [NE, NH, 128, 6, 384], BF16)
    Wkv = dscr("Wkv", [NE, 512, 2048], BF16)
    Wo = dscr("Wo", [NE, 8, 128, KC, 256], BF16)
    Wpl = dscr("Wpl", [max(NO, 1), 2048, 512], BF16)

    seqs = [(0, SS, PAD), (SS, 256, 3 * PAD + SS), (SS + 256, 256, 5 * PAD + SS + 256)]
    blocks = []
    for b in range(NSB):
        blocks.append((b * 512, 512, 0))
    blocks.append((SS, 512, 1))
    NB = len(blocks)

    def pcol(c):
        if c < SS:
            return c + PAD
        if c < SS + 256:
            return c + 3 * PAD
        return c + 5 * PAD

    dsl = {}

    def DS(*key):
        s = dsl.get(key)
        if s is None:
            s = Slot()
            dsl[key] = s
        return s

    with ExitStack() as st:
        S = Sched(nc, st)

        def dma(q, out, in_, reads=(), writes=()):
            return S.op(q, lambda e: e.dma_start(out=out, in_=in_), reads, writes, dma=True)

        ident = st.enter_context(SBT(nc, "ident", [128, 128], F32))
        identb = st.enter_context(SBT(nc, "identb", [128, 128], BF16))
        ones_f = st.enter_context(SBT(nc, "ones_f", [128, 128], F32))
        ones_b = st.enter_context(SBT(nc, "ones_b", [128, 128], BF16))
        zeros_b = st.enter_context(SBT(nc, "zeros_b", [128, 2 * PAD], BF16))
        zeros_f = st.enter_context(SBT(nc, "zeros_f", [128, 2 * PAD], F32))
        MODV = st.enter_context(SBT(nc, "MODV", [128, DEPTH, 6, KC, 2], F32))
        s_const = Slot()
        s_modv_l = [Slot() for _ in range(DEPTH)]
        NPADM = 64
        scnb = st.enter_context(SBT(nc, "scnb", [128, KC, NPADM], BF16))
        bm = st.enter_context(SBT(nc, "bm", [128, 96, 2], F32))
        nw = st.enter_context(SBT(nc, "nw", [128, 2, KC], F32))
        psc = st.enter_context(SBT(nc, "psc", [128, KC], F32))
        s_scn, s_bm, s_nw = Slot(), Slot(), Slot()
        PSB = [st.enter_context(nc.psum_tensor("psb%d" % i, [128, 512], F32)) for i in range(8)]
        ps_slots = [Slot() for _ in range(8)]
        ps_i = [0]

        def PS():
            i = ps_i[0]
            ps_i[0] = (i + 1) % 6
            return PSB[i], ps_slots[i]

        def ACC(i):
            return PSB[6 + i], ps_slots[6 + i]

        rsr = Ring(nc, st, "rsr", [512], F32, 4)

        f32t = Ring(nc, st, "f32t", [512], F32, 8)
        b16t = Ring(nc, st, "b16t", [512], BF16, 8)
        wA = Ring(nc, st, "wA", [8192], BF16, 2)
        xs = st.enter_context(SBT(nc, "xs", [128, KC, 512], F32))
        xs_slots = [Slot() for _ in range(KC)]
        hTr = Ring(nc, st, "hT", [KC, 512], BF16, 2)

        dma("sp", ident[:], ident_in[:, :], writes=(s_const,))
        S.op("dve", lambda e: e.memset(ones_f[:], 1.0), writes=(s_const,))
        S.op("dve", lambda e: e.memset(ones_b[:], 1.0), writes=(s_const,))
        S.op("dve", lambda e: e.memset(zeros_b[:], 0.0), writes=(s_const,))
        S.op("dve", lambda e: e.memset(zeros_f[:], 0.0), writes=(s_const,))
        S.op("dve", lambda e: e.tensor_copy(out=identb[:], in_=ident[:]), reads=(s_const,), writes=(s_const,))
        S.barrier()
        for (c0, n, p0) in (seqs if stop >= 1 else []):
            for pc in (p0 - PAD, p0 + n):
                for r in range(8):
                    dma("pool", glu[r * 128:(r + 1) * 128, pc:pc + PAD], zeros_b[:, 0:PAD], reads=(s_const,))
                for r in range(16):
                    dma("pool", hpad[r * 128:(r + 1) * 128, pc:pc + PAD], zeros_f[:, 0:PAD], reads=(s_const,))

        bg = []

        def bg_step(k=1):
            for _ in range(k):
                while bg:
                    try:
                        next(bg[0])
                        break
                    except StopIteration:
                        bg.pop(0)

        def bg_drain():
            while bg:
                bg_step()

        def run_chunks(chunks, lag):
            pend = []
            for (load, mid, fin) in chunks:
                for p in pend:
                    p[2] += 1
                while pend and pend[0][2] >= lag:
                    p = pend.pop(0)
                    if p[0] is not None:
                        p[0]()
                    p[1]()
                for p in pend:
                    if p[0] is not None and p[2] >= 1:
                        p[0]()
                        p[0] = None
                load()
                pend.append([mid, fin, 0])
                yield
            for p in pend:
                if p[0] is not None:
                    p[0]()
                p[1]()
            yield

        def run_batched(chunks, batch):
            pend = []
            it = iter(chunks)
            while True:
                for (mid, fin) in pend:
                    if mid is not None:
                        mid()
                    fin()
                pend = []
                for c in it:
                    c[0]()
                    pend.append((c[1], c[2]))
                    if len(pend) == batch:
                        break
                if not pend:
                    break
                yield

        def cast_chunk(stg, src, r0, c0, ncol, stores, mid=None, pre=None):
            box = {}

            def load():
                if pre is not None:
                    pre()
                t, s_ = stg.get()
                S.op("pool", lambda e: e.dma_start(out=t[:, 0:ncol], in_=src[r0:r0 + 128, c0:c0 + ncol]), writes=(s_,), dma=True)
                box["t"], box["s"] = t, s_

            def fin():
                stores(box["t"], box["s"], box)
            m = None
            if mid is not None:
                def m():
                    mid(box["t"], box["s"], box)
            return (load, m, fin)

        def chunks_ffn(l, stg):
            for gi, wsrc in enumerate((w_gate, w_up)):
                for kc in range(KC):
                    for (g0, ng) in ((0, 8), (8, 8), (16, 6)):
                        def stores(t, s_, box, gi=gi, kc=kc, g0=g0, ng=ng):
                            dma("sp", Wgu[l, g0:g0 + ng, :, gi, kc, :].rearrange("g p c -> p g c"),
                                t[:, 0:ng * 256].rearrange("p (g c) -> p g c", c=256), reads=(s_,))
                        yield cast_chunk(stg, wsrc[l], kc * 128, g0 * 256, ng * 256, stores)
            for fc in range(FC):
                def stores(t, s_, box, fc=fc):
                    dma("sp", Wdn[l, :, :, fc, :].rearrange("g p c -> p g c"),
                        t[:, 0:2048].rearrange("p (g c) -> p g c", c=128), reads=(s_,))
                yield cast_chunk(stg, w_down[l], fc * 128, 0, 2048, stores)

        def chunks_pool(o_, stg):
            for kc in range(16):
                def stores(t, s_, box, kc=kc):
                    dma("sp", Wpl[o_, kc * 128:(kc + 1) * 128, :], t[:, 0:512], reads=(s_,))
                yield cast_chunk(stg, pool_w[o_], kc * 128, 0, 512, stores)

        def chunks_even(e_, stg, stg2):
            for kc in range(KC):
                def stores(t, s_, box, kc=kc):
                    dma("sp", Win[e_, 0:8, :, kc, 0:128].rearrange("g p c -> p g c"), t[:, 0:1024].rearrange("p (g c) -> p g c", c=128), reads=(s_,))
                    dma("sp", Win[e_, 0:8, :, kc, 128:256].rearrange("g p c -> p g c"), t[:, 1024:2048].rearrange("p (g c) -> p g c", c=128), reads=(s_,))
                yield cast_chunk(stg, w_in[e_], kc * 128, 0, 2048, stores)

                def mid(t, s_, box):
                    t2, s2 = stg2.get()
                    box["t2"], box["s2"] = t2, s2
                    for (o0, swp) in ((0, False), (64, True), (128, True), (192, False)):
                        if not swp:
                            S.op("dve", lambda e, t=t, t2=t2, o0=o0: e.tensor_copy(out=t2[:, o0:o0 + 64], in_=t[:, 1280:1344]), reads=(s_,), writes=(s2,))
                        else:
                            S.op("dve", lambda e, t=t, t2=t2, o0=o0: e.tensor_copy(out=t2[:, o0:o0 + 64:2], in_=t[:, 1281:1344:2]), reads=(s_,), writes=(s2,))
                            S.op("dve", lambda e, t=t, t2=t2, o0=o0: e.tensor_copy(out=t2[:, o0 + 1:o0 + 64:2], in_=t[:, 1280:1344:2]), reads=(s_,), writes=(s2,))

                def stores(t, s_, box, kc=kc):
                    dma("sp", Win[e_, 8:13, :, kc, :].rearrange("g p c -> p g c"), t[:, 0:1280].rearrange("p (g c) -> p g c", c=256), reads=(s_,))
                    dma("sp", Win[e_, 13, :, kc, 0:256], box["t2"][:, 0:256], reads=(box["s2"],))
                yield cast_chunk(stg, w_in[e_], kc * 128, 2048, 1344, stores, mid=mid)
            for kc in range(6):
                def mid(t, s_, box):
                    t2, s2 = stg2.get()
                    box["t2"], box["s2"] = t2, s2
                    tv = t[:, 0:1536].rearrange("p (h c) -> p h c", c=192)
                    t2v = t2[:, 0:2048].rearrange("p (h c) -> p h c", c=256)
                    for (o0, swp) in ((0, False), (64, True), (128, True), (192, False)):
                        if not swp:
                            S.op("dve", lambda e, tv=tv, t2v=t2v, o0=o0: e.tensor_copy(out=t2v[:, :, o0:o0 + 64], in_=tv[:, :, 128:192]), reads=(s_,), writes=(s2,))
                        else:
                            S.op("dve", lambda e, tv=tv, t2v=t2v, o0=o0: e.tensor_copy(out=t2v[:, :, o0:o0 + 64:2], in_=tv[:, :, 129:192:2]), reads=(s_,), writes=(s2,))
                            S.op("dve", lambda e, tv=tv, t2v=t2v, o0=o0: e.tensor_copy(out=t2v[:, :, o0 + 1:o0 + 64:2], in_=tv[:, :, 128:192:2]), reads=(s_,), writes=(s2,))

                def stores(t, s_, box, kc=kc):
                    tv = t[:, 0:1536].rearrange("p (h c) -> p h c", c=192)
                    t2v = box["t2"][:, 0:2048].rearrange("p (h c) -> p h c", c=256)
                    dma("sp", Wqb[e_, :, :, kc, 0:128].rearrange("h p c -> p h c"), tv[:, :, 0:128], reads=(s_,))
                    dma("sp", Wqb[e_, :, :, kc, 128:384].rearrange("h p c -> p h c"), t2v, reads=(box["s2"],))
                yield cast_chunk(stg, w_qb[e_], kc * 128, 0, 1536, stores, mid=mid)
            for kc in range(4):
                def stores(t, s_, box, kc=kc):
                    dma("sp", Wkv[e_, kc * 128:(kc + 1) * 128, :], t[:, 0:2048], reads=(s_,))
                yield cast_chunk(stg, w_kvb[e_], kc * 128, 0, 2048, stores)
            for kc in range(KC):
                def stores(t, s_, box, kc=kc):
                    dma("sp", Wo[e_, :, :, kc, :].rearrange("g p c -> p g c"), t[:, 0:2048].rearrange("p (g c) -> p g c", c=256), reads=(s_,))
                yield cast_chunk(stg, w_out[e_], kc * 128, 0, 2048, stores)

        def chunks_mod(l, stg):
            accs = [ACC(0), ACC(1)]
            sm = s_modv_l[l]
            mvf = MODV[:, l].rearrange("p w k c -> p (w k c)")
            bmf = bm[:].rearrange("p j c -> p (j c)")

            def pre():
                dma("sp", bm[:], bmodX[l], writes=(s_bm,))
                dma("sp", nw[:, 0, :], n1w[l], writes=(s_nw,))
                dma("sp", nw[:, 1, :], n2w[l], writes=(s_nw,))
                if l % 2 == 1:
                    dma("sp", psc[:], pscale[l // 2], writes=(s_nw,))
                S.op("dve", lambda e: e.tensor_copy(out=mvf, in_=bmf), reads=(s_bm,), writes=(sm,))

            def epilogue():
                for wi, ni in ((1, 0), (4, 1)):
                    for ci in range(2):
                        S.op("dve", lambda e, wi=wi, ni=ni, ci=ci: e.scalar_tensor_tensor(
                            out=MODV[:, l, wi, :, ci], in0=MODV[:, l, wi, :, ci], scalar=1.0, in1=nw[:, ni, :], op0=ALU.add, op1=ALU.mult),
                            reads=(sm, s_nw), writes=(sm,))
                if l % 2 == 1:
                    for ci in range(2):
                        S.op("dve", lambda e, ci=ci: e.tensor_tensor(out=MODV[:, l, 2, :, ci], in0=MODV[:, l, 2, :, ci], in1=psc[:], op=ALU.mult),
                             reads=(sm, s_nw), writes=(sm,))

            for rng in range(6):
                for kc in range(KC):
                    first = (rng == 0 and kc == 0)
                    last_ = (rng == 5 and kc == KC - 1)

                    def stores(t, s_, box, rng=rng, kc=kc, last_=last_):
                        def mm(e):
                            lastm = None
                            for j in range(16):
                                ps_ = accs[j // 8][0]
                                jj = j % 8
                                lastm = e.matmul(ps_[:, jj * NPADM:(jj + 1) * NPADM], lhsT=t[:, j * 128:(j + 1) * 128], rhs=scnb[:, kc, :],
                                                 start=True, stop=True)
                            return lastm
                        S.op("pe", mm, reads=(s_, s_scn), writes=(accs[0][1], accs[1][1]))
                        if True:
                            for hb in range(2):
                                ps_, ps_s_ = accs[hb]
                                c0_ = rng * 32 + hb * 16
                                S.op("dve", lambda e, ps_=ps_, c0_=c0_: e.tensor_tensor(
                                    out=mvf[:, c0_:c0_ + 16].rearrange("p (j c) -> p j c", c=2),
                                    in0=ps_[:, 0:8 * NPADM].rearrange("p (j c) -> p j c", c=NPADM)[:, :, 0:2],
                                    in1=mvf[:, c0_:c0_ + 16].rearrange("p (j c) -> p j c", c=2), op=ALU.add),
                                    reads=(ps_s_, sm), writes=(sm,))
                        if last_:
                            epilogue()
                    yield cast_chunk(stg, w_mod[l], kc * 128, rng * 2048, 2048, stores, pre=(pre if first else None))

        def mcol(l, wi, kc, ci):
            return MODV[:, l, wi, kc, ci:ci + 1]

        with ExitStack() as ph:
            cnd = ph.enter_context(SBT(nc, "cnd", [128, KC, 2], F32))
            stgP = Ring(nc, ph, "stgP", [2048], BF16, 3)
            stg2P = Ring(nc, ph, "stg2P", [2048], BF16, 3)
            s_cnd = Slot()
            dma("sp", cnd[:], condT[:, :, :], writes=(s_cnd,))
            S.op("dve", lambda e: e.memset(scnb[:], 0.0), writes=(s_scn,))
            S.op("act", lambda e: e.activation(out=scnb[:, :, 0:2], in_=cnd[:], func=AF.Silu), reads=(s_cnd, s_scn), writes=(s_scn,))
            bg.append(run_chunks(chunks_mod(0, stgP), 2))
            bg.append(run_chunks(chunks_even(0, stgP, stg2P), 2))
            bg_drain()
            S.barrier()

        def rstd_chain(stat_ps, stat_s, n, npart, inv_n, out_t, out_s):
            S.op("dve", lambda e: e.tensor_scalar(out=out_t[0:npart, 0:n], in0=stat_ps[0:npart, 0:n], scalar1=inv_n, scalar2=EPS,
                                                  op0=ALU.mult, op1=ALU.add), reads=(stat_s,), writes=(out_s,))
            S.op("act", lambda e: e.activation(out=out_t[0:npart, 0:n], in_=out_t[0:npart, 0:n], func=AF.Sqrt), reads=(out_s,), writes=(out_s,))
            S.op("dve", lambda e: e.reciprocal(out=out_t[0:npart, 0:n], in_=out_t[0:npart, 0:n]), reads=(out_s,), writes=(out_s,))

        def norm_block(xsrc, xname, b, l, which, emit_h):
            c0, n, ci = blocks[b]
            wi_a, wi_b = (1, 0) if which == 1 else (4, 3)
            st_ps, st_s = PS()
            for kc in range(KC):
                dma("sp", xs[:, kc, 0:n], xsrc[kc * 128:(kc + 1) * 128, c0:c0 + n], reads=(DS(xname, b, kc),), writes=(xs_slots[kc],))
                sq, sq_s = f32t.get()
                S.op("act", lambda e, kc=kc, sq=sq: e.activation(out=sq[:, 0:n], in_=xs[:, kc, 0:n], func=AF.Square), reads=(xs_slots[kc],), writes=(sq_s,))
                S.op("pe", lambda e, kc=kc, sq=sq: e.matmul(st_ps[:, 0:n], lhsT=ones_f[:], rhs=sq[:, 0:n], start=(kc == 0), stop=(kc == KC - 1)),
                     reads=(sq_s, s_const), writes=(st_s,))
            rs, rs_s = rsr.get()
            rstd_chain(st_ps, st_s, n, 128, 1.0 / D, rs, rs_s)
            for kc in range(KC):
                tmp, tmp_s = f32t.get()
                S.op("dve", lambda e, kc=kc, tmp=tmp: e.scalar_tensor_tensor(out=tmp[:, 0:n], in0=xs[:, kc, 0:n], scalar=mcol(l, wi_a, kc, ci),
                                                                           in1=rs[:, 0:n], op0=ALU.mult, op1=ALU.mult),
                     reads=(xs_slots[kc], rs_s, s_modv_l[l]), writes=(tmp_s,))
                emit_h(kc, tmp, tmp_s, mcol(l, wi_b, kc, ci))

        def norm_to_hT(xsrc, xname, b, l, which):
            c0, n, ci = blocks[b]
            hT, hT_s = hTr.get()

            def emit_h(kc, tmp, tmp_s, bcol):
                S.op("act", lambda e: e.activation(out=hT[:, kc, 0:n], in_=tmp[:, 0:n], func=AF.Identity, bias=bcol, scale=1.0),
                     reads=(tmp_s, s_modv_l[l]), writes=(hT_s,))
            norm_block(xsrc, xname, b, l, which, emit_h)
            return hT, hT_s

        def residual_store(ps, ps_s, dc, gcolap, xsrc, xname, xdst, dname, b, l):
            c0, n, ci = blocks[b]
            xr, xr_s = f32t.get()
            dma("sp", xr[:, 0:n], xsrc[dc * 128:(dc + 1) * 128, c0:c0 + n], reads=(DS(xname, b, dc),), writes=(xr_s,))
            yo, yo_s = f32t.get()
            S.op("dve", lambda e: e.scalar_tensor_tensor(out=yo[:, 0:n], in0=ps[:, 0:n], scalar=gcolap, in1=xr[:, 0:n], op0=ALU.mult, op1=ALU.add),
                 reads=(ps_s, xr_s, s_modv_l[l]), writes=(yo_s,))
            dma("pool", xdst[dc * 128:(dc + 1) * 128, c0:c0 + n], yo[:, 0:n], reads=(yo_s,), writes=(DS(dname, b, dc),))

        def ffn_phase(l, xsrc, xname, xdst, dname):
            with ExitStack() as ph:
                actT = ph.enter_context(SBT(nc, "actT", [128, FC, 512], BF16))
                act_s = [Slot() for _ in range(FC)]
                if l + 1 < DEPTH:
                    stgF = Ring(nc, ph, "stgF", [2048], BF16, 3)
                    bg.append(run_chunks(chunks_mod(l + 1, stgF), 2))
                    if (l + 1) % 2 == 0:
                        stg2F = Ring(nc, ph, "stg2F", [2048], BF16, 3)
                        bg.append(run_chunks(chunks_even((l + 1) // 2, stgF, stg2F), 2))
                    else:
                        bg.append(run_chunks(chunks_pool((l + 1) // 2, stgF), 2))
                        bg.append(run_chunks(chunks_ffn(l + 1, stgF), 2))
                hT_next = norm_to_hT(xsrc, xname, 0, l, 2)
                for b in range(NB):
                    c0, n, ci = blocks[b]
                    hT, hT_s = hT_next
                    for fg in range(22):
                        wt, wts = wA.get()
                        wv = wt[:, 0:8192].rearrange("p (g k c) -> p g k c", g=2, k=KC)
                        dma("sp", wv, Wgu[l, fg], reads=(DS("Wgu", l),), writes=(wts,))
                        for j in range(2):
                            fc = fg * 2 + j
                            pg, pg_s = PS()
                            pu, pu_s = PS()

                            def mm(e, wv=wv, j=j, pg=pg, pu=pu, hT=hT):
                                for kc in range(KC):
                                    e.matmul(pg[:, 0:n], lhsT=wv[:, 0, kc, j * 128:(j + 1) * 128], rhs=hT[:, kc, 0:n], start=(kc == 0), stop=(kc == KC - 1))
                                last = None
                                for kc in range(KC):
                                    last = e.matmul(pu[:, 0:n], lhsT=wv[:, 1, kc, j * 128:(j + 1) * 128], rhs=hT[:, kc, 0:n], start=(kc == 0), stop=(kc == KC - 1))
                                return last
                            S.op("pe", mm, reads=(wts, hT_s), writes=(pg_s, pu_s))
                            sg, sg_s = f32t.get()
                            S.op("act", lambda e, pg=pg, sg=sg: e.activation(out=sg[:, 0:n], in_=pg[:, 0:n], func=AF.Silu), reads=(pg_s,), writes=(sg_s,))
                            S.op("dve", lambda e, pu=pu, sg=sg, fc=fc: e.tensor_tensor(out=actT[:, fc, 0:n], in0=pu[:, 0:n], in1=sg[:, 0:n], op=ALU.mult),
                                 reads=(pu_s, sg_s), writes=(act_s[fc],))
                        bg_step()
                    if b + 1 < NB:
                        hT_next = norm_to_hT(xsrc, xname, b + 1, l, 2)
                    for dc in range(KC):
                        wt, wts = wA.get()
                        wv = wt[:, 0:FC * 128].rearrange("p (k c) -> p k c", k=FC)
                        dma("sp", wv, Wdn[l, dc], reads=(DS("Wdn", l),), writes=(wts,))
                        po, po_s = PS()

                        def mm2(e, wv=wv, po=po):
                            last = None
                            for fc in range(FC):
                                last = e.matmul(po[:, 0:n], lhsT=wv[:, fc, :], rhs=actT[:, fc, 0:n], start=(fc == 0), stop=(fc == FC - 1))
                            return last
                        S.op("pe", mm2, reads=[wts] + act_s, writes=(po_s,))
                        residual_store(po, po_s, dc, mcol(l, 5, dc, ci), xsrc, xname, xdst, dname, b, l)
                        bg_step()
                bg_drain()
                S.barrier()

        def pool_phase(l, xsrc, xname, xdst, dname):
            o_ = l // 2
            for b in range(NB):
                c0, n, ci = blocks[b]

                def emit_h(kc, tmp, tmp_s, bcol, b=b, c0=c0, n=n):
                    ho, ho_s = f32t.get()
                    S.op("act", lambda e: e.activation(out=ho[:, 0:n], in_=tmp[:, 0:n], func=AF.Identity, bias=bcol, scale=1.0),
                         reads=(tmp_s, s_modv_l[l]), writes=(ho_s,))
                    if b < NSB:
                        dma("pool", hpad[kc * 128:(kc + 1) * 128, pcol(c0):pcol(c0) + n], ho[:, 0:n], reads=(ho_s,), writes=(DS("hpad", b, kc),))
                    else:
                        for hf in range(2):
                            cc = c0 + hf * 256
                            dma("pool", hpad[kc * 128:(kc + 1) * 128, pcol(cc):pcol(cc) + 256], ho[:, hf * 256:(hf + 1) * 256], reads=(ho_s,),
                                writes=(DS("hpad", b, kc, hf),))
                norm_block(xsrc, xname, b, l, 1, emit_h)
            with ExitStack() as ph:
                PW = ph.enter_context(SBT(nc, "PW", [128, 16, 512], BF16))
                pw_s = Slot()
                dma("sp", PW[:], Wpl[o_].rearrange("(k p) c -> p k c", p=128), reads=(DS("Wpl", o_),), writes=(pw_s,))
                hp = Ring(nc, ph, "hp", [512 + 2 * PAD], F32, 3)
                pa = Ring(nc, ph, "pa", [512 + 2 * PAD], F32, 4)
                icr = Ring(nc, ph, "icr", [512], F32, 2)
                subs = [(b, b * 512, 512) for b in range(NSB)] + [(NSB, SS, 256), (NSB, SS + 256, 256)]
                ic_cur = {}

                def mix_step(sub, mixT, mix_s, kc):
                    b, c0, n = sub
                    g, cc = kc // 4, kc % 4
                    w = POOL_W[g]
                    interior = (b < NSB and 0 < b < NSB - 1)
                    if cc == 0 and not interior:
                        ic, ic_s = icr.get()
                        dma("sp", ic[:, 0:n], invcntB[:, g, c0:c0 + n], writes=(ic_s,))
                        ic_cur[(b, c0)] = (ic, ic_s)
                    a, a_s = hp.get()
                    rd = [DS("hpad", bb, kc) for bb in range(max(b - 1, 0), min(b + 1, NSB - 1) + 1)] if b < NSB else \
                        [DS("hpad", b, kc, 0), DS("hpad", b, kc, 1)]
                    p0 = pcol(c0) - PAD
                    L = n + 2 * PAD
                    dma("sp", a[:, 0:L], hpad[kc * 128:(kc + 1) * 128, p0:p0 + L], reads=rd, writes=(a_s,))
                    cur, cur_s, width = a, a_s, 1
                    while width < w:
                        nx, nx_s = pa.get()
                        ln = L - 2 * width + 1
                        S.op("dve", lambda e, cur=cur, nx=nx, width=width, ln=ln: e.tensor_tensor(
                            out=nx[:, 0:ln], in0=cur[:, 0:ln], in1=cur[:, width:width + ln], op=ALU.add), reads=(cur_s,), writes=(nx_s,))
                        cur, cur_s = nx, nx_s
                        width *= 2
                    off = PAD - w // 2
                    if interior:
                        S.op("dve", lambda e, cur=cur, a=a, kc=kc, mixT=mixT, off=off, n=n, w=w: e.scalar_tensor_tensor(
                            out=mixT[:, kc, 0:n], in0=cur[:, off:off + n], scalar=1.0 / w, in1=a[:, PAD:PAD + n], op0=ALU.mult, op1=ALU.subtract),
                            reads=(cur_s, a_s), writes=(mix_s,))
                        return
                    ic, ic_s = ic_cur[(b, c0)]
                    t1, t1_s = pa.get()
                    S.op("pool", lambda e, cur=cur, t1=t1, ic=ic, off=off, n=n: e.tensor_tensor(
                        out=t1[:, 0:n], in0=cur[:, off:off + n], in1=ic[:, 0:n], op=ALU.mult), reads=(cur_s, ic_s), writes=(t1_s,))
                    S.op("dve", lambda e, t1=t1, a=a, kc=kc, mixT=mixT, n=n: e.tensor_tensor(
                        out=mixT[:, kc, 0:n], in0=t1[:, 0:n], in1=a[:, PAD:PAD + n], op=ALU.subtract), reads=(t1_s, a_s), writes=(mix_s,))

                def resid_step(sub, mixT, mix_s, dc):
                    b, c0, n = sub
                    ci = blocks[b][2]
                    g, ncb = dc // 4, dc % 4
                    po, po_s = PS()

                    def mm(e):
                        last = None
                        for cc in range(4):
                            last = e.matmul(po[:, 0:n], lhsT=PW[:, g * 4 + cc, ncb * 128:(ncb + 1) * 128], rhs=mixT[:, g * 4 + cc, 0:n],
                                            start=(cc == 0), stop=(cc == 3))
                        return last
                    S.op("pe", mm, reads=(pw_s, mix_s), writes=(po_s,))
                    xr, xr_s = f32t.get()
                    dma("sp", xr[:, 0:n], xsrc[dc * 128:(dc + 1) * 128, c0:c0 + n], reads=(DS(xname, b, dc),), writes=(xr_s,))
                    yo, yo_s = f32t.get()
                    S.op("dve", lambda e: e.scalar_tensor_tensor(
                        out=yo[:, 0:n], in0=po[:, 0:n], scalar=mcol(l, 2, dc, ci), in1=xr[:, 0:n], op0=ALU.mult, op1=ALU.add),
                        reads=(po_s, xr_s, s_modv_l[l]), writes=(yo_s,))
                    dma("pool", xdst[dc * 128:(dc + 1) * 128, c0:c0 + n], yo[:, 0:n], reads=(yo_s,), writes=(DS(dname, b, dc),))

                mts = [None] * len(subs)
                mts[0] = hTr.get()
                for kc in range(16):
                    mix_step(subs[0], mts[0][0], mts[0][1], kc)
                for si in range(len(subs)):
                    if si + 1 < len(subs):
                        mts[si + 1] = hTr.get()
                    for i in range(16):
                        if si + 1 < len(subs):
                            mix_step(subs[si + 1], mts[si + 1][0], mts[si + 1][1], i)
                        resid_step(subs[si], mts[si][0], mts[si][1], i)
                S.barrier()

        def even_phase(l, xsrc, xname, xdst, dname):
            e_ = l // 2
            with ExitStack() as ph:
                qa = ph.enter_context(SBT(nc, "qa", [128, 6, 512], F32))
                qan = ph.enter_context(SBT(nc, "qan", [128, 6, 512], BF16))
                ckv = ph.enter_context(SBT(nc, "ckv", [128, 4, 512], F32))
                ckvn = ph.enter_context(SBT(nc, "ckvn", [128, 4, 512], BF16))
                kpesq = ph.enter_context(SBT(nc, "kpesq", [64, 512], F32))
                krb = ph.enter_context(SBT(nc, "krb", [64, 512], F32))
                rC = ph.enter_context(SBT(nc, "rC", [64, 512], F32))
                rSn = ph.enter_context(SBT(nc, "rSn", [64, 512], F32))
                Wk = ph.enter_context(SBT(nc, "Wk", [128, 4, 1024], BF16))
                Wv = ph.enter_context(SBT(nc, "Wv", [128, 4, 1024], BF16))
                vec = ph.enter_context(SBT(nc, "vec", [128, 32], F32))
                s_qa, s_qan, s_ckv, s_ckvn, s_kpesq, s_krb, s_rope, s_wkv, s_vec = [Slot() for _ in range(9)]
                dma("sp", vec[:, 0:6], qanw[e_], writes=(s_vec,))
                dma("sp", vec[:, 6:10], kvanw[e_], writes=(s_vec,))
                dma("sp", vec[:, 10:14], qnw[e_], writes=(s_vec,))
                dma("sp", vec[:, 14:18], knw[e_], writes=(s_vec,))
                wkv_v = Wkv[e_].rearrange("(k p) (h t c) -> p k h t c", p=128, h=NH, t=2)
                for kc in range(4):
                    dma("sp", Wk[:, kc, :].rearrange("p (h c) -> p h c", h=NH), wkv_v[:, kc, :, 0, :], reads=(DS("Wkv", e_),), writes=(s_wkv,))
                    dma("sp", Wv[:, kc, :].rearrange("p (h c) -> p h c", h=NH), wkv_v[:, kc, :, 1, :], reads=(DS("Wkv", e_),), writes=(s_wkv,))

                def kv_path(n, key0, kname, rope, c0):
                    for h in range(NH):
                        pk, pk_s = PS()

                        def mm(e, h=h, pk=pk):
                            last = None
                            for rc in range(4):
                                last = e.matmul(pk[:, 0:n], lhsT=Wk[:, rc, h * 128:(h + 1) * 128], rhs=ckvn[:, rc, 0:n], start=(rc == 0), stop=(rc == 3))
                            return last
                        S.op("pe", mm, reads=(s_wkv, s_ckvn), writes=(pk_s,))
                        sq, sq_s = f32t.get()
                        S.op("act", lambda e, pk=pk, sq=sq: e.activation(out=sq[:, 0:n], in_=pk[:, 0:n], func=AF.Square), reads=(pk_s,), writes=(sq_s,))
                        stp, stp_s = PS()

                        def mm2(e, sq=sq, stp=stp):
                            e.matmul(stp[:, 0:n], lhsT=ones_f[:], rhs=sq[:, 0:n], start=True, stop=False)
                            return e.matmul(stp[:, 0:n], lhsT=ones_f[0:64, :], rhs=kpesq[:, 0:n], start=False, stop=True)
                        S.op("pe", mm2, reads=(sq_s, s_kpesq, s_const), writes=(stp_s,))
                        rs, rs_s = rsr.get()
                        rstd_chain(stp, stp_s, n, 128, 1.0 / 192, rs, rs_s)
                        ko, ko_s = b16t.get()
                        S.op("dve", lambda e, pk=pk, rs=rs, ko=ko: e.scalar_tensor_tensor(out=ko[:, 0:n], in0=pk[:, 0:n], scalar=vec[:, 14:15], in1=rs[:, 0:n],
                                                                                   op0=ALU.mult, op1=ALU.mult), reads=(pk_s, rs_s, s_vec), writes=(ko_s,))
                        dma("pool", KnT[h, :, key0:key0 + n], ko[:, 0:n], reads=(ko_s,), writes=(DS("Kn", kname, h),))
                        kr, kr_s = b16t.get()
                        S.op("pool", lambda e, rs=rs, kr=kr: e.tensor_tensor(out=kr[0:64, 0:n], in0=krb[:, 0:n], in1=rs[0:64, 0:n], op=ALU.mult),
                             reads=(rs_s, s_krb), writes=(kr_s,))
                        dma("pool", KrT[h, :, key0:key0 + n], kr[0:64, 0:n], reads=(kr_s,), writes=(DS("Kr", kname, h),))
                    for i in range(n // 128):
                        for hf in range(2):
                            pv, pv_s = PS()

                            def mm3(e, i=i, hf=hf, pv=pv):
                                last = None
                                for rc in range(4):
                                    last = e.matmul(pv[:, 0:512], lhsT=ckvn[:, rc, i * 128:(i + 1) * 128], rhs=Wv[:, rc, hf * 512:(hf + 1) * 512],
                                                    start=(rc == 0), stop=(rc == 3))
                                return last
                            S.op("pe", mm3, reads=(s_wkv, s_ckvn), writes=(pv_s,))
                            vo, vo_s = b16t.get()
                            S.op("act", lambda e, pv=pv, vo=vo: e.activation(out=vo[:, 0:512], in_=pv[:, 0:512], func=AF.Identity), reads=(pv_s,), writes=(vo_s,))
                            dma("pool", Vs[key0 + i * 128:key0 + (i + 1) * 128, hf * 512:(hf + 1) * 512], vo[:, 0:512], reads=(vo_s,),
                                writes=(DS("V", kname, i, hf),))

                def rope_apply(src, src_s, srcsw, srcsw_s, wcol, wswcol, n, out, out_s, npart=64):
                    t1, t1_s = f32t.get()
                    S.op("dve", lambda e: e.scalar_tensor_tensor(out=t1[0:64, 0:n], in0=src[0:64, 0:n], scalar=wcol, in1=rC[:, 0:n], op0=ALU.mult, op1=ALU.mult),
                         reads=(src_s, s_rope, s_vec), writes=(t1_s,))
                    t2, t2_s = f32t.get()
                    S.op("dve", lambda e: e.scalar_tensor_tensor(out=t2[0:64, 0:n], in0=srcsw[0:64, 0:n], scalar=wswcol, in1=rSn[:, 0:n], op0=ALU.mult, op1=ALU.mult),
                         reads=(srcsw_s, s_rope, s_vec), writes=(t2_s,))
                    S.op("pool", lambda e: e.tensor_tensor(out=out[0:64, 0:n], in0=t1[0:64, 0:n], in1=t2[0:64, 0:n], op=ALU.add), reads=(t1_s, t2_s), writes=(out_s,))

                cst = ph.enter_context(SBT(nc, "cst", [128, 4, PAST], F32))
                ckp = ph.enter_context(SBT(nc, "ckp", [64, PAST], F32))
                s_cst, s_ckp = Slot(), Slot()
                dma("sp", cst[:], cckvT[e_].rearrange("(k p) t -> p k t", p=128), writes=(s_cst,))
                dma("sp", ckp[:], ckpeT[e_], writes=(s_ckp,))
                S.op("act", lambda e: e.activation(out=ckvn[:, :, 0:PAST], in_=cst[:], func=AF.Identity), reads=(s_cst,), writes=(s_ckvn,))
                S.op("act", lambda e: e.activation(out=kpesq[:, 0:PAST], in_=ckp[:], func=AF.Square), reads=(s_ckp,), writes=(s_kpesq,))
                S.op("dve", lambda e: e.tensor_scalar(out=krb[:, 0:PAST], in0=ckp[:], scalar1=vec[0:64, 15:16], scalar2=None, op0=ALU.mult),
                     reads=(s_ckp, s_vec), writes=(s_krb,))
                E1S = 9
                kv_path(PAST, 0, "ctx", False, 0)

                hT_next = norm_to_hT(xsrc, xname, 0, l, 1)
                for b in range(NB):
                    c0, n, ci = blocks[b]
                    is_s = (ci == 0)
                    hT, hT_s = hT_next
                    if is_s:
                        dma("sp", rC[:, 0:n], ropeC[:, c0:c0 + n], writes=(s_rope,))
                        dma("sp", rSn[:, 0:n], ropeS[:, c0:c0 + n], writes=(s_rope,))
                    qst, qst_s = ACC(0)
                    cst_ps, cst_ps_s = ACC(1)
                    for mg in range(14):
                        wt, wts = wA.get()
                        ncol = 256
                        wv = wt[:, 0:KC * 256].rearrange("p (k c) -> p k c", k=KC)
                        dma("sp", wv[:, :, 0:ncol], Win[e_, mg, :, :, 0:ncol], reads=(DS("Win", e_),), writes=(wts,))

                        def chain(ps, lo, m, wv=wv, hT=hT):
                            def f(e):
                                last = None
                                for kc in range(KC):
                                    last = e.matmul(ps[0:m, 0:n], lhsT=wv[:, kc, lo:lo + m], rhs=hT[:, kc, 0:n], start=(kc == 0), stop=(kc == KC - 1))
                                return last
                            return f
                        if mg < 8:
                            pa_, pa_s = PS()
                            pg_, pg_s = PS()
                            S.op("pe", chain(pa_, 0, 128), reads=(wts, hT_s), writes=(pa_s,))
                            S.op("pe", chain(pg_, 128, 128), reads=(wts, hT_s), writes=(pg_s,))
                            sg, sg_s = f32t.get()
                            S.op("act", lambda e, pg_=pg_, sg=sg: e.activation(out=sg[:, 0:n], in_=pg_[:, 0:n], func=AF.Sigmoid), reads=(pg_s,), writes=(sg_s,))
                            go, go_s = b16t.get()
                            S.op("dve", lambda e, pa_=pa_, sg=sg, go=go: e.tensor_tensor(out=go[:, 0:n], in0=pa_[:, 0:n], in1=sg[:, 0:n], op=ALU.mult),
                                 reads=(pa_s, sg_s), writes=(go_s,))
                            if is_s:
                                dma("pool", glu[mg * 128:(mg + 1) * 128, pcol(c0):pcol(c0) + n], go[:, 0:n], reads=(go_s,), writes=(DS("glu", b, mg),))
                            else:
                                for hf in range(2):
                                    cc = c0 + hf * 256
                                    dma("pool", glu[mg * 128:(mg + 1) * 128, pcol(cc):pcol(cc) + 256], go[:, hf * 256:(hf + 1) * 256], reads=(go_s,),
                                        writes=(DS("glu", b, mg, hf),))
                        elif mg < 13:
                            for j in range(2):
                                mc = (mg - 8) * 2 + j
                                pp, pp_s = PS()
                                S.op("pe", chain(pp, j * 128, 128), reads=(wts, hT_s), writes=(pp_s,))
                                if mc < 6:
                                    dst, dst_s, stp, stp_s, k, last = qa[:, mc, 0:n], s_qa, qst, qst_s, mc, 5
                                else:
                                    dst, dst_s, stp, stp_s, k, last = ckv[:, mc - 6, 0:n], s_ckv, cst_ps, cst_ps_s, mc - 6, 3
                                S.op("act", lambda e, pp=pp, dst=dst: e.activation(out=dst, in_=pp[:, 0:n], func=AF.Identity), reads=(pp_s,), writes=(dst_s,))
                                sq, sq_s = f32t.get()
                                S.op("act", lambda e, pp=pp, sq=sq: e.activation(out=sq[:, 0:n], in_=pp[:, 0:n], func=AF.Square), reads=(pp_s,), writes=(sq_s,))
                                S.op("pe", lambda e, sq=sq, stp=stp, k=k, last=last: e.matmul(stp[:, 0:n], lhsT=ones_f[:], rhs=sq[:, 0:n], start=(k == 0), stop=(k == last)),
                                     reads=(sq_s, s_const), writes=(stp_s,))
                        else:
                            pk1, pk1_s = PS()
                            pk2, pk2_s = PS()
                            S.op("pe", chain(pk1, 0, 128), reads=(wts, hT_s), writes=(pk1_s,))
                            S.op("pe", chain(pk2, 128, 128), reads=(wts, hT_s), writes=(pk2_s,))
                            ko, ko_s = f32t.get()
                            S.op("act", lambda e, pk1=pk1, ko=ko: e.activation(out=ko[:, 0:n], in_=pk1[:, 0:n], func=AF.Identity), reads=(pk1_s,), writes=(ko_s,))
                            ko2, ko2_s = f32t.get()
                            S.op("act", lambda e, pk2=pk2, ko2=ko2: e.activation(out=ko2[:, 0:n], in_=pk2[:, 0:n], func=AF.Identity), reads=(pk2_s,), writes=(ko2_s,))
                            S.op("act", lambda e, ko=ko: e.activation(out=kpesq[:, 0:n], in_=ko[0:64, 0:n], func=AF.Square), reads=(ko_s,), writes=(s_kpesq,))
                            if not is_s:
                                dma("pool", kpe_out[e_, :, :], ko[0:64, 0:n], reads=(ko_s,))
                                S.op("dve", lambda e, ko=ko: e.tensor_scalar(out=krb[:, 0:n], in0=ko[0:64, 0:n], scalar1=vec[0:64, 15:16], scalar2=None, op0=ALU.mult),
                                     reads=(ko_s, s_vec), writes=(s_krb,))
                            else:
                                rope_apply(ko, ko_s, ko2, ko2_s, vec[0:64, 15:16], vec[0:64, 16:17], n, krb, s_krb)
                    if E1S < 5:
                        continue
                    rs, rs_s = rsr.get()
                    rstd_chain(qst, qst_s, n, 128, 1.0 / 768, rs, rs_s)
                    for kc in range(6):
                        S.op("dve", lambda e, kc=kc, rs=rs: e.scalar_tensor_tensor(out=qan[:, kc, 0:n], in0=qa[:, kc, 0:n], scalar=vec[:, kc:kc + 1], in1=rs[:, 0:n],
                                                                                 op0=ALU.mult, op1=ALU.mult), reads=(s_qa, rs_s, s_vec), writes=(s_qan,))
                    rs2, rs2_s = rsr.get()
                    rstd_chain(cst_ps, cst_ps_s, n, 128, 1.0 / 512, rs2, rs2_s)
                    for kc in range(4):
                        cf, cf_s = f32t.get()
                        S.op("dve", lambda e, kc=kc, rs2=rs2, cf=cf: e.scalar_tensor_tensor(out=cf[:, 0:n], in0=ckv[:, kc, 0:n], scalar=vec[:, 6 + kc:7 + kc], in1=rs2[:, 0:n],
                                                                                         op0=ALU.mult, op1=ALU.mult), reads=(s_ckv, rs2_s, s_vec), writes=(cf_s,))
                        S.op("act", lambda e, kc=kc, cf=cf: e.activation(out=ckvn[:, kc, 0:n], in_=cf[:, 0:n], func=AF.Identity), reads=(cf_s,), writes=(s_ckvn,))
                        if not is_s:
                            dma("pool", ckv_out[e_, kc * 128:(kc + 1) * 128, :], cf[:, 0:n], reads=(cf_s,))
                    if b + 1 < NB:
                        hT_next = norm_to_hT(xsrc, xname, b + 1, l, 1)
                    for h in range(NH if E1S >= 6 else 0):
                        wt, wts = wA.get()
                        wq = wt[:, 0:6 * 384].rearrange("p (k c) -> p k c", k=6)
                        dma("sp", wq, Wqb[e_, h], reads=(DS("Wqb", e_),), writes=(wts,))

                        def qchain(ps, lo, m, wq=wq):
                            def f(e):
                                last = None
                                for kc in range(6):
                                    last = e.matmul(ps[0:m, 0:n], lhsT=wq[:, kc, lo:lo + m], rhs=qan[:, kc, 0:n], start=(kc == 0), stop=(kc == 5))
                                return last
                            return f
                        pn, pn_s = PS()
                        pr, pr_s = PS()
                        S.op("pe", qchain(pn, 0, 128), reads=(wts, s_qan), writes=(pn_s,))
                        S.op("pe", qchain(pr, 128, 128), reads=(wts, s_qan), writes=(pr_s,))
                        if is_s:
                            prs, prs_s = PS()
                            S.op("pe", qchain(prs, 256, 128), reads=(wts, s_qan), writes=(prs_s,))
                        sq1, sq1_s = f32t.get()
                        S.op("act", lambda e, pn=pn, sq1=sq1: e.activation(out=sq1[:, 0:n], in_=pn[:, 0:n], func=AF.Square), reads=(pn_s,), writes=(sq1_s,))
                        prc, prc_s = f32t.get()
                        S.op("act", lambda e, pr=pr, prc=prc: e.activation(out=prc[:, 0:n], in_=pr[:, 0:n], func=AF.Identity), reads=(pr_s,), writes=(prc_s,))
                        pr, pr_s = prc, prc_s
                        if is_s:
                            prsc, prsc_s = f32t.get()
                            S.op("act", lambda e, prs=prs, prsc=prsc: e.activation(out=prsc[:, 0:n], in_=prs[:, 0:n], func=AF.Identity), reads=(prs_s,), writes=(prsc_s,))
                            prs, prs_s = prsc, prsc_s
                        sq2, sq2_s = f32t.get()
                        S.op("act", lambda e, pr=pr, sq2=sq2: e.activation(out=sq2[0:64, 0:n], in_=pr[0:64, 0:n], func=AF.Square), reads=(pr_s,), writes=(sq2_s,))
                        stp, stp_s = PS()

                        def mmq(e, sq1=sq1, sq2=sq2, stp=stp):
                            e.matmul(stp[:, 0:n], lhsT=ones_f[:], rhs=sq1[:, 0:n], start=True, stop=False)
                            return e.matmul(stp[:, 0:n], lhsT=ones_f[0:64, :], rhs=sq2[0:64, 0:n], start=False, stop=True)
                        S.op("pe", mmq, reads=(sq1_s, sq2_s, s_const), writes=(stp_s,))
                        rq, rq_s = rsr.get()
                        rstd_chain(stp, stp_s, n, 128, 1.0 / 192, rq, rq_s)
                        qo, qo_s = b16t.get()
                        S.op("dve", lambda e, pn=pn, rq=rq, qo=qo: e.scalar_tensor_tensor(out=qo[:, 0:n], in0=pn[:, 0:n], scalar=vec[:, 10:11], in1=rq[:, 0:n],
                                                                                       op0=ALU.mult, op1=ALU.mult), reads=(pn_s, rq_s, s_vec), writes=(qo_s,))
                        dma("pool", QnT[h, :, c0:c0 + n], qo[:, 0:n], reads=(qo_s,), writes=(DS("Qn", b, h),))
                        qr, qr_s = b16t.get()
                        if is_s:
                            rt, rt_s = f32t.get()
                            rope_apply(pr, pr_s, prs, prs_s, vec[0:64, 11:12], vec[0:64, 12:13], n, rt, rt_s)
                            S.op("dve", lambda e, rt=rt, rq=rq, qr=qr: e.tensor_tensor(out=qr[0:64, 0:n], in0=rt[0:64, 0:n], in1=rq[0:64, 0:n], op=ALU.mult),
                                 reads=(rt_s, rq_s), writes=(qr_s,))
                        else:
                            S.op("dve", lambda e, pr=pr, rq=rq, qr=qr: e.scalar_tensor_tensor(out=qr[0:64, 0:n], in0=pr[0:64, 0:n], scalar=vec[0:64, 11:12], in1=rq[0:64, 0:n],
                                                                                           op0=ALU.mult, op1=ALU.mult), reads=(pr_s, rq_s, s_vec), writes=(qr_s,))
                        dma("pool", QrT[h, :, c0:c0 + n], qr[0:64, 0:n], reads=(qr_s,), writes=(DS("Qr", b, h),))
                    if E1S >= 7:
                        kv_path(n, PAST + c0, b, is_s, c0)
                S.barrier()

            if stop == 4:
                return
            with ExitStack() as ph:
                Dg = ph.enter_context(SBT(nc, "Dg", [128, 8, CONV_K, 128], BF16))
                cw = ph.enter_context(SBT(nc, "cw", [128, 8, CONV_K], F32))
                cv = ph.enter_context(SBT(nc, "cv", [128, 3, 8], F32))
                s_dg, s_dg2, s_cw = Slot(), Slot(), Slot()
                dma("sp", cw[:], cdw[e_], writes=(s_cw,))
                dma("sp", cv[:, 0, :], cdb[e_], writes=(s_cw,))
                dma("sp", cv[:, 1, :], clw[e_], writes=(s_cw,))
                dma("sp", cv[:, 2, :], clb[e_], writes=(s_cw,))
                for c in range(8):
                    for k in range(CONV_K):
                        S.op("dve" if (k % 2 == 0) else "pool", lambda e, c=c, k=k: e.tensor_scalar(out=Dg[:, c, k, :], in0=ident[:], scalar1=cw[:, c, k:k + 1], scalar2=None,
                                                                                                op0=ALU.mult), reads=(s_cw, s_const), writes=((s_dg,) if (k % 2 == 0) else (s_dg2,)))
                gp = Ring(nc, ph, "gp", [512 + 2 * PAD], BF16, 3)
                cr = xs
                cr_s = xs_slots
                subs = [(b, b * 512, 512) for b in range(NSB)] + [(NSB, SS, 256), (NSB, SS + 256, 256)]
                for (b, c0, n) in subs:
                    s1, s1_s = ACC(0)
                    s2, s2_s = ACC(1)
                    for c in range(8):
                        g_, g_s = gp.get()
                        rd = [DS("glu", bb, c) for bb in range(max(b - 1, 0), min(b + 1, NSB - 1) + 1)] if b < NSB else \
                            [DS("glu", b, c, 0), DS("glu", b, c, 1)]
                        p0 = pcol(c0) - PAD
                        L = n + 2 * PAD
                        dma("sp", g_[:, 0:L], glu[c * 128:(c + 1) * 128, p0:p0 + L], reads=rd, writes=(g_s,))
                        pc, pc_s = PS()

                        def mm(e, c=c, g_=g_, pc=pc, n=n):
                            last = None
                            for k in range(CONV_K):
                                o = PAD - 15 + k
                                last = e.matmul(pc[:, 0:n], lhsT=Dg[:, c, k, :], rhs=g_[:, o:o + n], start=(k == 0), stop=(k == CONV_K - 1))
                            return last
                        S.op("pe", mm, reads=(g_s, s_dg, s_dg2), writes=(pc_s,))
                        S.op("act", lambda e, c=c, pc=pc, n=n: e.activation(out=cr[:, c, 0:n], in_=pc[:, 0:n], func=AF.Identity, bias=cv[:, 0, c:c + 1], scale=1.0),
                             reads=(pc_s, s_cw), writes=(cr_s[c],))
                        sq, sq_s = f32t.get()
                        S.op("act", lambda e, c=c, sq=sq, n=n: e.activation(out=sq[:, 0:n], in_=cr[:, c, 0:n], func=AF.Square), reads=(cr_s[c],), writes=(sq_s,))

                        def mms(e, c=c, sq=sq, n=n, s1=s1, s2=s2):
                            e.matmul(s1[:, 0:n], lhsT=ones_f[:], rhs=cr[:, c, 0:n], start=(c == 0), stop=(c == 7))
                            return e.matmul(s2[:, 0:n], lhsT=ones_f[:], rhs=sq[:, 0:n], start=(c == 0), stop=(c == 7))
                        S.op("pe", mms, reads=(cr_s[c], sq_s, s_const), writes=(s1_s, s2_s))
                    mu, mu_s = rsr.get()
                    S.op("dve", lambda e, mu=mu, s1=s1, n=n: e.tensor_scalar(out=mu[:, 0:n], in0=s1[:, 0:n], scalar1=1.0 / 1024, scalar2=None, op0=ALU.mult),
                         reads=(s1_s,), writes=(mu_s,))
                    m2, m2_s = f32t.get()
                    S.op("dve", lambda e, mu=mu, m2=m2, n=n: e.tensor_tensor(out=m2[:, 0:n], in0=mu[:, 0:n], in1=mu[:, 0:n], op=ALU.mult), reads=(mu_s,), writes=(m2_s,))
                    var, var_s = rsr.get()
                    S.op("dve", lambda e, var=var, s2=s2, m2=m2, n=n: e.scalar_tensor_tensor(out=var[:, 0:n], in0=s2[:, 0:n], scalar=1.0 / 1024, in1=m2[:, 0:n],
                                                                                           op0=ALU.mult, op1=ALU.subtract), reads=(s2_s, m2_s), writes=(var_s,))
                    S.op("dve", lambda e, var=var, n=n: e.tensor_scalar(out=var[:, 0:n], in0=var[:, 0:n], scalar1=EPS, scalar2=None, op0=ALU.add), reads=(var_s,), writes=(var_s,))
                    S.op("act", lambda e, var=var, n=n: e.activation(out=var[:, 0:n], in_=var[:, 0:n], func=AF.Sqrt), reads=(var_s,), writes=(var_s,))
                    S.op("dve", lambda e, var=var, n=n: e.reciprocal(out=var[:, 0:n], in_=var[:, 0:n]), reads=(var_s,), writes=(var_s,))
                    for c in range(8):
                        t1, t1_s = f32t.get()
                        S.op("pool", lambda e, c=c, t1=t1, mu=mu, n=n: e.tensor_tensor(out=t1[:, 0:n], in0=cr[:, c, 0:n], in1=mu[:, 0:n], op=ALU.subtract),
                             reads=(cr_s[c], mu_s), writes=(t1_s,))
                        S.op("dve", lambda e, t1=t1, var=var, n=n: e.tensor_tensor(out=t1[:, 0:n], in0=t1[:, 0:n], in1=var[:, 0:n], op=ALU.mult),
                             reads=(t1_s, var_s), writes=(t1_s,))
                        co, co_s = b16t.get()
                        S.op("act", lambda e, c=c, t1=t1, co=co, n=n: e.activation(out=co[:, 0:n], in_=t1[:, 0:n], func=AF.Silu, bias=cv[:, 2, c:c + 1], scale=cv[:, 1, c:c + 1]),
                             reads=(t1_s, s_cw), writes=(co_s,))
                        dma("pool", catT[c * 128:(c + 1) * 128, c0:c0 + n], co[:, 0:n], reads=(co_s,), writes=(DS("cat", c0, c),))
                S.barrier()

            if stop == 5:
                return
            with ExitStack() as ph:
                nkmax = PAST + SS
                Knr = Ring(nc, ph, "Knr", [nkmax], BF16, 2)
                Krr = Ring(nc, ph, "Krr", [nkmax], BF16, 2)
                Vr = Ring(nc, ph, "Vr", [nkmax // 128, 128], BF16, 2)
                qnr = Ring(nc, ph, "qnr", [512], BF16, 2)
                qrr = Ring(nc, ph, "qrr", [512], BF16, 2)
                ptr = Ring(nc, ph, "ptr", [512], BF16, 4)
                stgA = Ring(nc, ph, "stgA", [2048], BF16, 2)
                bg.append(run_batched(chunks_ffn(l, stgA), 2))
                sc = 1.0 / np.sqrt(192.0)
                groups = [(0, PAST + SS, [(b * 512, 512, b) for b in range(NSB)]),
                          (PAST + SS, 256, [(SS, 256, NSB)]),
                          (PAST + SS + 256, 256, [(SS + 256, 256, NSB)])]
                for (key0, nk, qbs) in groups:
                    nkc = nk // 128
                    if key0 == 0:
                        krd = lambda nm, h: [DS(nm, "ctx", h)] + [DS(nm, b, h) for b in range(NSB)]
                        vrd = [DS("V", "ctx", i, hf) for i in range(2) for hf in range(2)] + [DS("V", b, i, hf) for b in range(NSB) for i in range(4) for hf in range(2)]
                    else:
                        krd = lambda nm, h: [DS(nm, NSB, h)]
                        vrd = [DS("V", NSB, i, hf) for i in range(4) for hf in range(2)]
                    for h in range(NH):
                        Kn, Kn_s = Knr.get()
                        Kr, Kr_s = Krr.get()
                        Vh, Vh_s = Vr.get()
                        dma("sp", Kn[:, 0:nk], KnT[h, :, key0:key0 + nk], reads=krd("Kn", h), writes=(Kn_s,))
                        dma("sp", Kr[0:64, 0:nk], KrT[h, :, key0:key0 + nk], reads=krd("Kr", h), writes=(Kr_s,))
                        for v0 in range(0, nkc, 8):
                            v1 = min(v0 + 8, nkc)
                            dma("sp", Vh[:, v0:v1, :], Vs[key0 + v0 * 128:key0 + v1 * 128, h * 128:(h + 1) * 128].rearrange("(c p) d -> p c d", p=128),
                                reads=vrd, writes=(Vh_s,))
                        for (c0, n, b) in qbs:
                            qn, qn_s = qnr.get()
                            qr, qr_s = qrr.get()
                            dma("sp", qn[:, 0:n], QnT[h, :, c0:c0 + n], reads=(DS("Qn", b, h),), writes=(qn_s,))
                            dma("sp", qr[0:64, 0:n], QrT[h, :, c0:c0 + n], reads=(DS("Qr", b, h),), writes=(qr_s,))
                            po, po_s = ACC(0)
                            pz, pz_s = ACC(1)
                            pend = []
                            for kc in range(nkc + 2):
                                if kc < nkc:
                                    pss, pss_s = PS()

                                    def mmS(e, kc=kc, pss=pss, Kn=Kn, Kr=Kr, qn=qn, qr=qr, n=n):
                                        e.matmul(pss[:, 0:n], lhsT=Kn[:, kc * 128:(kc + 1) * 128], rhs=qn[:, 0:n], start=True, stop=False)
                                        return e.matmul(pss[:, 0:n], lhsT=Kr[0:64, kc * 128:(kc + 1) * 128], rhs=qr[0:64, 0:n], start=False, stop=True)
                                    S.op("pe", mmS, reads=(Kn_s, Kr_s, qn_s, qr_s), writes=(pss_s,))
                                    pt, pt_s = ptr.get()
                                    S.op("act", lambda e, pss=pss, pt=pt, n=n: e.activation(out=pt[:, 0:n], in_=pss[:, 0:n], func=AF.Exp, scale=float(sc)),
                                         reads=(pss_s,), writes=(pt_s,))
                                    pend.append((kc, pt, pt_s))
                                if kc >= 2:
                                    k2, pt2, pt2_s = pend.pop(0)

                                    def mmP(e, k2=k2, pt2=pt2, Vh=Vh, po=po, pz=pz, n=n, nkc=nkc):
                                        e.matmul(po[:, 0:n], lhsT=Vh[:, k2, :], rhs=pt2[:, 0:n], start=(k2 == 0), stop=(k2 == nkc - 1))
                                        return e.matmul(pz[:, 0:n], lhsT=ones_b[:], rhs=pt2[:, 0:n], start=(k2 == 0), stop=(k2 == nkc - 1))
                                    S.op("pe", mmP, reads=(Vh_s, pt2_s, s_const), writes=(po_s, pz_s))
                            rz, rz_s = f32t.get()
                            S.op("dve", lambda e, rz=rz, pz=pz, n=n: e.reciprocal(out=rz[:, 0:n], in_=pz[:, 0:n]), reads=(pz_s,), writes=(rz_s,))
                            ao, ao_s = b16t.get()
                            S.op("dve", lambda e, ao=ao, po=po, rz=rz, n=n: e.tensor_tensor(out=ao[:, 0:n], in0=po[:, 0:n], in1=rz[:, 0:n], op=ALU.mult),
                                 reads=(po_s, rz_s), writes=(ao_s,))
                            dma("pool", catT[1024 + h * 128:1024 + (h + 1) * 128, c0:c0 + n], ao[:, 0:n], reads=(ao_s,), writes=(DS("cat", c0, 8 + h),))
                            bg_step()
                bg_drain()
                S.barrier()

            if stop == 6:
                return
            for b in range(NB):
                c0, n, ci = blocks[b]
                cT, cT_s = hTr.get()
                if b < NSB:
                    rd = [DS("cat", c0, r) for r in range(16)]
                else:
                    rd = [DS("cat", c0 + hf * 256, r) for r in range(16) for hf in range(2)]
                dma("sp", cT[:, :, 0:n], catT[:, c0:c0 + n].rearrange("(k p) t -> p k t", p=128), reads=rd, writes=(cT_s,))
                for dg in range(8):
                    wt, wts = wA.get()
                    wv = wt[:, 0:KC * 256].rearrange("p (k c) -> p k c", k=KC)
                    dma("sp", wv, Wo[e_, dg], reads=(DS("Wo", e_),), writes=(wts,))
                    for j in range(2):
                        dc = dg * 2 + j
                        po, po_s = PS()

                        def mm(e, wv=wv, j=j, po=po, cT=cT, n=n):
                            last = None
                            for kc in range(KC):
                                last = e.matmul(po[:, 0:n], lhsT=wv[:, kc, j * 128:(j + 1) * 128], rhs=cT[:, kc, 0:n], start=(kc == 0), stop=(kc == KC - 1))
                            return last
                        S.op("pe", mm, reads=(wts, cT_s), writes=(po_s,))
                        residual_store(po, po_s, dc, mcol(l, 2, dc, ci), xsrc, xname, xdst, dname, b, l)
            S.barrier()

        cur, cname = xT, "x_in"
        for l in range(DEPTH if stop >= 4 else 0):
            if l % 2 == 0:
                even_phase(l, cur, cname, xA, "xA%d" % l)
            else:
                pool_phase(l, cur, cname, xA, "xA%d" % l)
            last = (l == DEPTH - 1)
            dst, dn = (yT, "y") if last else (xB, "xB%d" % l)
            if stop >= 8:
                ffn_phase(l, xA, "xA%d" % l, dst, dn)
            cur, cname = dst, dn
        S.emit()
    return nc


def _colvec(v, nchunk):
    return np.ascontiguousarray(np.asarray(v, np.float32).reshape(nchunk, 128).T)


def _rope_tables(rows):
    row = np.broadcast_to(np.arange(rows, dtype=np.float32)[:, None], (rows, GRID_W)).reshape(-1)
    col = np.broadcast_to(np.arange(GRID_W, dtype=np.float32)[None, :], (rows, GRID_W)).reshape(-1)
    n_freq = 16
    inv_freq = (1.0 / (np.float32(10000.0) ** (np.arange(n_freq, dtype=np.float32) / np.float32(n_freq)))).astype(np.float32)
    ang = np.concatenate([row[:, None] * inv_freq, col[:, None] * inv_freq], axis=-1).astype(np.float32)
    c, s = np.cos(ang).astype(np.float32), np.sin(ang).astype(np.float32)
    C = np.repeat(c.T, 2, axis=0)
    Sg = np.repeat(s.T, 2, axis=0).copy()
    Sg[0::2] *= -1.0
    return np.ascontiguousarray(C), np.ascontiguousarray(Sg)


def _invcnt(s):
    t = np.arange(s)
    out = np.zeros((4, s), np.float32)
    for gi, w in enumerate(POOL_W):
        lo = np.clip(t - w // 2, 0, s)
        hi = np.clip(t + w // 2, 0, s)
        out[gi] = 1.0 / (hi - lo).astype(np.float32)
    return out


def _pairswap(v):
    v = np.asarray(v)
    o = v.copy()
    o[0::2] = v[1::2]
    o[1::2] = v[0::2]
    return o


def make_in_maps(inp, NSB, DEPTH, ncores):
    SS = NSB * 512
    NE = (DEPTH + 1) // 2
    NO = DEPTH // 2
    f = lambda a: np.ascontiguousarray(np.asarray(a, np.float32))
    shared = {}
    shared["n1w"] = np.stack([_colvec(inp["norm1_w"][l], KC) for l in range(DEPTH)])
    shared["n2w"] = np.stack([_colvec(inp["norm2_w"][l], KC) for l in range(DEPTH)])
    bm = np.stack([_colvec(inp["b_mod"][l], 96) for l in range(DEPTH)])
    shared["bmodX"] = np.ascontiguousarray(np.repeat(bm[:, :, :, None], 2, axis=3))
    shared["w_mod"] = f(inp["w_mod"][:DEPTH])
    shared["w_in"] = f(inp["w_in"][:NE])
    cdw = np.asarray(inp["conv_dw_w"], np.float32)[:NE]
    shared["cdw"] = np.ascontiguousarray(cdw.reshape(NE, CONV_K, 8, 128).transpose(0, 3, 2, 1))
    for nm, key in (("cdb", "conv_dw_b"), ("clw", "conv_ln_w"), ("clb", "conv_ln_b")):
        shared[nm] = np.stack([_colvec(inp[key][e], 8) for e in range(NE)])
    shared["qanw"] = np.stack([_colvec(inp["q_a_norm_w"][e], 6) for e in range(NE)])
    shared["kvanw"] = np.stack([_colvec(inp["kv_a_norm_w"][e], 4) for e in range(NE)])
    shared["w_qb"] = f(inp["w_q_b"][:NE])
    shared["w_kvb"] = f(inp["w_kv_b"][:NE])

    def headvec(w):
        o = np.zeros((128, 4), np.float32)
        w = np.asarray(w, np.float32)
        o[:, 0] = w[:128]
        o[:64, 1] = w[128:192]
        o[:64, 2] = _pairswap(w[128:192])
        return o
    shared["qnw"] = np.stack([headvec(inp["q_norm_w"][e]) for e in range(NE)])
    shared["knw"] = np.stack([headvec(inp["k_norm_w"][e]) for e in range(NE)])
    shared["w_out"] = f(inp["w_out"][:NE])
    if NO > 0:
        shared["pool_w"] = f(np.asarray(inp["pool_w"])[:NO].reshape(NO, 4 * 512, 512))
        shared["pscale"] = np.stack([_colvec(inp["pool_scale"][o], KC) for o in range(NO)])
    else:
        shared["pool_w"] = np.zeros((1, 2048, 512), np.float32)
        shared["pscale"] = np.zeros((1, 128, KC), np.float32)
    shared["w_gate"] = f(inp["ffn_w_gate"][:DEPTH])
    shared["w_up"] = f(inp["ffn_w_up"][:DEPTH])
    shared["w_down"] = f(inp["ffn_w_down"][:DEPTH])
    C, Sg = _rope_tables(SS // GRID_W)
    shared["ropeC"], shared["ropeS"] = C, Sg
    ic_ = np.concatenate([_invcnt(SS), _invcnt(256), _invcnt(256)], axis=1)
    shared["invcntB"] = np.ascontiguousarray(np.broadcast_to(ic_[None], (128,) + ic_.shape))
    shared["ident_in"] = np.eye(128, dtype=np.float32)
    xp = np.asarray(inp["x_prompt"], np.float32)
    xs_ = np.asarray(inp["x_sample"], np.float32)
    cc = np.asarray(inp["c"], np.float32)
    cctx = np.asarray(inp["c_ctx"], np.float32)
    cckv = np.asarray(inp["cache_ckv"], np.float32)
    ckpe = np.asarray(inp["cache_kpe"], np.float32)
    maps = []
    for i in range(ncores):
        m = dict(shared)
        xt = np.concatenate([xs_[i].T, xp[2 * i].T, xp[2 * i + 1].T], axis=1)
        m["xT"] = np.ascontiguousarray(xt)
        cnd = np.stack([cc[i], cctx], axis=1)
        m["condT"] = np.ascontiguousarray(cnd.reshape(KC, 128, 2).transpose(1, 0, 2))
        m["cckvT"] = np.ascontiguousarray(cckv[i, :NE].transpose(0, 2, 1))
        m["ckpeT"] = np.ascontiguousarray(ckpe[i, :NE].transpose(0, 2, 1))
        maps.append(m)
    return maps


def assemble(results, NSB, DEPTH, ncores):
    SS = NSB * 512
    NE = (DEPTH + 1) // 2
    y_s = np.zeros((ncores, SS, D), np.float32)
    y_p = np.zeros((2 * ncores, 256, D), np.float32)
    n_ckv = np.zeros((2 * ncores, NE, 256, 512), np.float32)
    n_kpe = np.zeros((2 * ncores, NE, 256, 64), np.float32)
    for i, r in enumerate(results):
        yT = np.asarray(r["yT"])
        y_s[i] = yT[:, :SS].T
        y_p[2 * i] = yT[:, SS:SS + 256].T
        y_p[2 * i + 1] = yT[:, SS + 256:SS + 512].T
        ck = np.asarray(r["ckv_out"])
        kp = np.asarray(r["kpe_out"])
        for j in range(2):
            n_ckv[2 * i + j] = ck[:, :, j * 256:(j + 1) * 256].transpose(0, 2, 1)
            n_kpe[2 * i + j] = kp[:, :, j * 256:(j + 1) * 256].transpose(0, 2, 1)
    return y_p, y_s, n_ckv, n_kpe


def run(inp, NSB=8, DEPTH=4, ncores=8, dbg=False):
    nc = build(NSB, DEPTH, dbg)
    maps = make_in_maps(inp, NSB, DEPTH, ncores)
    res = run_bass_kernel_spmd(nc, maps, core_ids=list(range(ncores)))
    return res


def kernel(**inputs):
    res = run(inputs, 8, 4, 8)
    return assemble(res.results, 8, 4, 8)
```
